# Optimizing a Trainium2 kernel written in Bass

```python
import jax, jax.numpy as jnp
from jax import lax
import numpy as np

D_MODEL = 2048
BATCH = 4
SEQ = 4096
DEPTH = 2

N_HEADS_MLA = 16
QK_NOPE_DIM = 128
QK_ROPE_DIM = 64
V_HEAD_DIM = 128
Q_LORA_RANK = 512
KV_LORA_RANK = 512
ROPE_THETA = 10000.0
Q_BLOCK = 128

LRU_WIDTH = D_MODEL
LRU_HEADS = 16
LRU_BLOCK = LRU_WIDTH // LRU_HEADS
CONV_WIDTH = 4
LRU_C = 8.0

D_FF = 4 * D_MODEL
PLE_DIM = 256
EPS = 1e-6

IN_SPLITS = (Q_LORA_RANK, KV_LORA_RANK + QK_ROPE_DIM, LRU_WIDTH, LRU_WIDTH, D_MODEL, D_MODEL)
D_IN = Q_LORA_RANK + KV_LORA_RANK + QK_ROPE_DIM + 2 * LRU_WIDTH + 2 * D_MODEL

kernel_name = 'hybrid_mla_rglru_gated_parallel'


def rms_norm(x, g):
    xf = x.astype(jnp.float32)
    y = xf * lax.rsqrt(jnp.mean(xf * xf, axis=-1, keepdims=True) + EPS)
    return (y * g.astype(jnp.float32)).astype(x.dtype)


def rope_tables(positions):
    half = QK_ROPE_DIM // 2
    inv_freq = jnp.power(jnp.float32(ROPE_THETA), -jnp.arange(half, dtype=jnp.float32) * (2.0 / QK_ROPE_DIM))
    ang = positions.astype(jnp.float32)[..., None] * inv_freq
    return jnp.cos(ang), jnp.sin(ang)


def apply_rope(t, cos, sin):
    half = QK_ROPE_DIM // 2
    t1 = t[..., :half].astype(jnp.float32)
    t2 = t[..., half:].astype(jnp.float32)
    out = jnp.concatenate([t1 * cos - t2 * sin, t2 * cos + t1 * sin], axis=-1)
    return out.astype(t.dtype)


def split_columns(z):
    parts = []
    start = 0
    for width in IN_SPLITS:
        parts.append(z[..., start:start + width])
        start += width
    return parts


def mla_branch(cq_raw, ckv_raw, g_q, w_qb, g_kv, w_kvb, cos, sin):
    b, s, _ = cq_raw.shape
    cq = rms_norm(cq_raw, g_q)
    q = (cq @ w_qb).reshape(b, s, N_HEADS_MLA, QK_NOPE_DIM + QK_ROPE_DIM)
    q_nope = q[..., :QK_NOPE_DIM]
    q_pe = apply_rope(q[..., QK_NOPE_DIM:], cos[:, :, None, :], sin[:, :, None, :])
    c_kv = rms_norm(ckv_raw[..., :KV_LORA_RANK], g_kv)
    k_pe = apply_rope(ckv_raw[..., KV_LORA_RANK:], cos, sin)
    kv = (c_kv @ w_kvb).reshape(b, s, N_HEADS_MLA, QK_NOPE_DIM + V_HEAD_DIM)
    k_nope = kv[..., :QK_NOPE_DIM]
    v = kv[..., QK_NOPE_DIM:]
    scale = (QK_NOPE_DIM + QK_ROPE_DIM) ** -0.5
    outs = []
    for blk in range(s // Q_BLOCK):
        s0 = blk * Q_BLOCK
        s1 = s0 + Q_BLOCK
        sc = (jnp.einsum('bqhd,bkhd->bhqk', q_nope[:, s0:s1], k_nope[:, :s1])
              + jnp.einsum('bqhr,bkr->bhqk', q_pe[:, s0:s1], k_pe[:, :s1])).astype(jnp.float32) * scale
        causal = jnp.arange(s1)[None, :] <= jnp.arange(s0, s1)[:, None]
        sc = jnp.where(causal, sc, -jnp.inf)
        pr = jax.nn.softmax(sc, axis=-1).astype(v.dtype)
        outs.append(jnp.einsum('bhqk,bkhd->bqhd', pr, v[:, :s1]))
    o = jnp.concatenate(outs, axis=1)
    return o.reshape(b, s, N_HEADS_MLA * V_HEAD_DIM)


def rglru_branch(xb, yb, conv_w, conv_b, w_a, b_a, w_x, b_x, lru_lambda):
    b, s, _ = xb.shape
    xc = lax.conv_general_dilated(
        xb, conv_w[:, None, :], window_strides=(1,), padding=[(CONV_WIDTH - 1, 0)],
        dimension_numbers=('NWC', 'WIO', 'NWC'), feature_group_count=LRU_WIDTH) + conv_b
    xh = xc.reshape(b, s, LRU_HEADS, LRU_BLOCK)
    r = jax.nn.sigmoid(jnp.einsum('bshi,hij->bshj', xh, w_a).reshape(b, s, LRU_WIDTH) + b_a)
    i = jax.nn.sigmoid(jnp.einsum('bshi,hij->bshj', xh, w_x).reshape(b, s, LRU_WIDTH) + b_x)
    log_a = -LRU_C * r.astype(jnp.float32) * jax.nn.softplus(-lru_lambda.astype(jnp.float32))
    a = jnp.exp(log_a)
    gated_x = jnp.sqrt(-jnp.expm1(2.0 * log_a)) * (i * xc).astype(jnp.float32)

    def combine(left, right):
        a1, h1 = left
        a2, h2 = right
        return a1 * a2, a2 * h1 + h2

    _, h = lax.associative_scan(combine, (a, gated_x), axis=1)
    return jax.nn.gelu(yb, approximate=True) * h.astype(xb.dtype)


def setup_inputs(seed: int = 0) -> dict:
    key = jax.random.key(seed)
    ks = jax.random.split(key, 26)
    f32 = jnp.float32

    def nrm(k, shape, fan_in):
        return jax.random.normal(k, shape, f32) * (fan_in ** -0.5)

    def gain(k, shape):
        return 1.0 + 0.02 * jax.random.normal(k, shape, f32)

    u = jax.random.uniform(ks[13], (DEPTH, LRU_WIDTH), f32, minval=0.9, maxval=0.999)
    a0 = u ** (1.0 / LRU_C)
    lru_lambda = jnp.log(a0) - jnp.log1p(-a0)
    return {
        'x': jax.random.normal(ks[0], (BATCH, SEQ, D_MODEL), f32),
        'p': jax.random.normal(ks[1], (DEPTH, BATCH, SEQ, PLE_DIM), f32),
        'positions': jnp.broadcast_to(jnp.arange(SEQ, dtype=jnp.int32)[None, :], (BATCH, SEQ)),
        'g_mix': gain(ks[2], (DEPTH, D_MODEL)),
        'w_in': nrm(ks[3], (DEPTH, D_MODEL, D_IN), D_MODEL),
        'g_q': gain(ks[4], (DEPTH, Q_LORA_RANK)),
        'w_qb': nrm(ks[5], (DEPTH, Q_LORA_RANK, N_HEADS_MLA * (QK_NOPE_DIM + QK_ROPE_DIM)), Q_LORA_RANK),
        'g_kv': gain(ks[6], (DEPTH, KV_LORA_RANK)),
        'w_kvb': nrm(ks[7], (DEPTH, KV_LORA_RANK, N_HEADS_MLA * (QK_NOPE_DIM + V_HEAD_DIM)), KV_LORA_RANK),
        'conv_w': nrm(ks[8], (DEPTH, CONV_WIDTH, LRU_WIDTH), CONV_WIDTH),
        'conv_b': 0.01 * jax.random.normal(ks[9], (DEPTH, LRU_WIDTH), f32),
        'w_a': nrm(ks[10], (DEPTH, LRU_HEADS, LRU_BLOCK, LRU_BLOCK), LRU_BLOCK),
        'b_a': 0.01 * jax.random.normal(ks[11], (DEPTH, LRU_WIDTH), f32),
        'w_x': nrm(ks[12], (DEPTH, LRU_HEADS, LRU_BLOCK, LRU_BLOCK), LRU_BLOCK),
        'b_x': 0.01 * jax.random.normal(ks[14], (DEPTH, LRU_WIDTH), f32),
        'lru_lambda': lru_lambda,
        'w_out': nrm(ks[15], (DEPTH, D_MODEL, D_MODEL), D_MODEL),
        'g_mlp': gain(ks[16], (DEPTH, D_MODEL)),
        'w_up': nrm(ks[17], (DEPTH, D_MODEL, D_FF), D_MODEL),
        'w_down': nrm(ks[18], (DEPTH, D_FF, D_MODEL), D_FF),
        'g_ple': gain(ks[19], (DEPTH, D_MODEL)),
        'w_ple_gate': nrm(ks[20], (DEPTH, D_MODEL, D_MODEL), D_MODEL),
        'w_ple_proj': nrm(ks[21], (DEPTH, PLE_DIM, D_MODEL), PLE_DIM),
        'g_final': gain(ks[22], (D_MODEL,)),
    }


def reference(x, p, positions, g_mix, w_in, g_q, w_qb, g_kv, w_kvb, conv_w, conv_b, w_a, b_a,
              w_x, b_x, lru_lambda, w_out, g_mlp, w_up, w_down, g_ple, w_ple_gate, w_ple_proj, g_final):
    cos, sin = rope_tables(positions)
    for l in range(DEPTH):
        u = rms_norm(x, g_mix[l])
        cq_raw, ckv_raw, xb, yb, gate_a, gate_r = split_columns(u @ w_in[l])
        attn = mla_branch(cq_raw, ckv_raw, g_q[l], w_qb[l], g_kv[l], w_kvb[l], cos, sin)
        rec = rglru_branch(xb, yb, conv_w[l], conv_b[l], w_a[l], b_a[l], w_x[l], b_x[l], lru_lambda[l])
        merged = jax.nn.sigmoid(gate_a) * attn + jax.nn.sigmoid(gate_r) * rec
        x = x + merged @ w_out[l]
        hdn = jnp.square(jax.nn.relu(rms_norm(x, g_mlp[l]) @ w_up[l]))
        x = x + hdn @ w_down[l]
        ple_gate = jax.nn.sigmoid(rms_norm(x, g_ple[l]) @ w_ple_gate[l])
        x = x + ple_gate * (p[l] @ w_ple_proj[l])
    return rms_norm(x, g_final)
```

```python
import numpy as np
from contextlib import ExitStack
import concourse.bass as bass
import concourse.mybir as mybir
from concourse.bass_utils import run_bass_kernel_spmd
import ml_dtypes

F32, BF16, I32 = mybir.dt.float32, mybir.dt.bfloat16, mybir.dt.int32
AF = mybir.ActivationFunctionType
ALU = mybir.AluOpType

D = 2048
KD = D // 128
NH = 16
DQ = 512
DKV = 512
DR = 64
DIN = 9280
DFF = 8192
DPLE = 256
DEPTH = 2
EPS = 1e-6
C_CQ, C_CKV, C_KPE, C_XB, C_YB, C_GA, C_GR = 0, 512, 1024, 1088, 3136, 5184, 7232
C_KROT = DIN
NEG = -30000.0
SCALE = float((128 + 64) ** -0.5)
TWO_PI_HI = 6.28125
TWO_PI_LO = float(2.0 * np.pi - 6.28125)

V_GMIX, V_GMLP, V_GPLE, V_GQ, V_GKV, V_CONVW, V_CONVB, V_BA, V_BX, V_LAM = 0, 16, 32, 48, 52, 56, 120, 136, 152, 168
V_PER_LAYER = 184
V_GFINAL = 2 * V_PER_LAYER
V_INVF = V_GFINAL + 16
V_FLAG = V_INVF + 1
V_B0 = V_FLAG + 1
V_B1 = V_B0 + 1
NV = V_B1 + 1


class Sem:
    def __init__(self, h, name):
        self.h = h
        self.name = name
        self.cnt = 0


class Buf:
    def __init__(self, name, dsem=None):
        self.name = name
        self.w = {}
        self.r = {}
        self.dsem = dsem


class Eng:
    def __init__(self, name, h, sem):
        self.name = name
        self.h = h
        self.sem = sem
        self.seen = {}


class Trk:
    def __init__(self, nc, stack):
        self.nc = nc
        self.stack = stack
        self.allsems = []
        self.scoped = False
        self.free = []
        self.inuse = []
        self.pe = Eng("pe", nc.tensor, self.sem("e_pe"))
        self.act = Eng("act", nc.scalar, self.sem("e_act"))
        self.dve = Eng("dve", nc.vector, self.sem("e_dve"))
        self.pool = Eng("pool", nc.gpsimd, self.sem("e_pool"))
        self.sp = Eng("sp", nc.sync, None)
        self.engs = [self.pe, self.act, self.dve, self.pool, self.sp]
        self.ccsem = self.sem("e_cc")
        self.n_wait = 0

    def sem(self, name):
        if self.scoped and self.free:
            s = self.free.pop()
            self.inuse.append(s)
            return s
        s = Sem(self.stack.enter_context(self.nc.semaphore(name)), name)
        self.allsems.append(s)
        if self.scoped:
            self.inuse.append(s)
        return s

    def buf(self, name, dma=False):
        return Buf(name, self.sem("d_" + name) if dma else None)

    def _wait1(self, eng, sem, val):
        if sem is eng.sem and eng is self.pe:
            return
        if eng.seen.get(sem, 0) >= val:
            return
        eng.h.wait_ge(sem.h, val)
        eng.seen[sem] = val
        self.n_wait += 1

    def _waits(self, eng, reads, writes, pwrites):
        for b in reads:
            for s, v in b.w.items():
                self._wait1(eng, s, v)
        for b in writes:
            for s, v in b.w.items():
                self._wait1(eng, s, v)
            for s, v in b.r.items():
                self._wait1(eng, s, v)
        for b in pwrites:
            for s, v in b.r.items():
                self._wait1(eng, s, v)

    def _record(self, sem, val, reads, writes, pwrites):
        for b in reads:
            b.r[sem] = val
        for b in writes:
            b.w[sem] = val
        for b in pwrites:
            b.w[sem] = val

    def op(self, eng, emit, reads=(), writes=(), pwrites=()):
        self._waits(eng, reads, writes, pwrites)
        ins = emit()
        eng.sem.cnt += 1
        ins.then_inc(eng.sem.h, 1)
        self._record(eng.sem, eng.sem.cnt, reads, writes, pwrites)

    def dma(self, q, dsem, out, in_, reads=(), writes=(), pwrites=(), **kw):
        self._waits(q, reads, writes, pwrites)
        ins = q.h.dma_start(out=out, in_=in_, **kw)
        dsem.cnt += 16
        ins.then_inc(dsem.h, 16)
        self._record(dsem, dsem.cnt, reads, writes, pwrites)

    def cc(self, groups, in_t, out_t, reads=(), writes=(), pwrites=()):
        q = self.pool
        self._waits(q, reads, writes, pwrites)
        ins = self.nc.gpsimd.collective_compute("AllGather", ALU.bypass, replica_groups=groups,
                                                ins=[in_t.ap().opt()], outs=[out_t.ap().opt()])
        self.ccsem.cnt += 1
        ins.then_inc(self.ccsem.h)
        self._record(self.ccsem, self.ccsem.cnt, reads, writes, pwrites)

    def barrier(self):
        for e in self.engs:
            for s in self.allsems:
                if s.cnt > 0 and s is not self.ccsem:
                    self._wait1(e, s, s.cnt)
        self.free.extend(self.inuse)
        self.inuse = []

    def final_wait(self, eng):
        for s in self.allsems:
            if s.cnt > 0 and s is not eng.sem:
                if eng.seen.get(s, 0) < s.cnt:
                    eng.h.wait_ge(s.h, s.cnt)
                    eng.seen[s] = s.cnt


class Slot:
    def __init__(self, t, buf):
        self.t = t
        self.buf = buf

    @property
    def dsem(self):
        return self.buf.dsem


class Ring:
    def __init__(self, slots):
        self.slots = slots
        self.i = 0

    def next(self):
        s = self.slots[self.i % len(self.slots)]
        self.i += 1
        return s


def build_program(T, dbg=(), depth=DEPTH):
    NQ = T // 512
    NT = T // 128
    T2 = 2 * T
    nc = bass.Bass("TRN2", target_bir_lowering=False)
    x_in = nc.dram_tensor("x", [T, D], F32, kind="ExternalInput").ap()
    p_in = nc.dram_tensor("p", [DEPTH, T, DPLE], F32, kind="ExternalInput").ap()
    pos_in = nc.dram_tensor("pos", [1, T], I32, kind="ExternalInput").ap()
    vecs_in = nc.dram_tensor("vecs", [128, NV], F32, kind="ExternalInput").ap()
    masks_in = nc.dram_tensor("masks", [2, 4, 128, 512], BF16, kind="ExternalInput").ap()
    cst_in = nc.dram_tensor("cst", [128, 256], F32, kind="ExternalInput").ap()
    w_in = nc.dram_tensor("w_in", [DEPTH, D, DIN], F32, kind="ExternalInput").ap()
    w_qb = nc.dram_tensor("w_qb", [DEPTH, DQ, NH * 192], F32, kind="ExternalInput").ap()
    w_kvb = nc.dram_tensor("w_kvb", [DEPTH, DKV, NH * 256], F32, kind="ExternalInput").ap()
    w_a = nc.dram_tensor("w_a", [DEPTH, NH, 128, 128], F32, kind="ExternalInput").ap()
    w_x = nc.dram_tensor("w_x", [DEPTH, NH, 128, 128], F32, kind="ExternalInput").ap()
    w_out = nc.dram_tensor("w_out", [DEPTH, D, D], F32, kind="ExternalInput").ap()
    w_up = nc.dram_tensor("w_up", [DEPTH, D, DFF], F32, kind="ExternalInput").ap()
    w_down = nc.dram_tensor("w_down", [DEPTH, DFF, D], F32, kind="ExternalInput").ap()
    w_pg = nc.dram_tensor("w_ple_gate", [DEPTH, D, D], F32, kind="ExternalInput").ap()
    w_pe = nc.dram_tensor("w_ple_proj", [DEPTH, DPLE, D], F32, kind="ExternalInput").ap()
    y_out = nc.dram_tensor("out", [T, D], F32, kind="ExternalOutput").ap()
    xT_t = nc.dram_tensor("s_xT", [D, T], F32)
    zT_t = nc.dram_tensor("s_zT", [DIN + 64, T], F32)
    qn_t = nc.dram_tensor("s_qn", [NH * 128, T], BF16)
    qr_t = nc.dram_tensor("s_qr", [NH * 64, T], BF16)
    CW = min(T, 1024)
    NP = T // CW
    ckv_own_ts = [nc.dram_tensor("s_ckv_own%d" % i, [576, CW], BF16) for i in range(NP)]
    ckv_all_ts = [nc.dram_tensor("s_ckv_all%d" % i, [2 * 576, CW], BF16) for i in range(NP)]
    xtail_own_t = nc.dram_tensor("s_xtail_own", [D, 4], F32)
    xtail_all_t = nc.dram_tensor("s_xtail_all", [2 * D, 4], F32)
    bb_t = nc.dram_tensor("s_bb", [D, T], F32)
    hfin_own_t = nc.dram_tensor("s_hfin_own", [D, 1], F32)
    hfin_all_t = nc.dram_tensor("s_hfin_all", [2 * D, 1], F32)
    ma_t = nc.dram_tensor("s_ma", [D, T], F32)
    hh_t = nc.dram_tensor("s_hh", [DFF, T], BF16)
    pT_t = nc.dram_tensor("s_pT", [DEPTH * DPLE, T], BF16)
    trig_t = nc.dram_tensor("s_trig", [4 * 64, T], F32)
    xT, zT, qn, qr = xT_t.ap(), zT_t.ap(), qn_t.ap(), qr_t.ap()
    ckv_own = [t_.ap() for t_ in ckv_own_ts]
    ckv_all = [t_.ap() for t_ in ckv_all_ts]
    xtail_own, xtail_all, bb = xtail_own_t.ap(), xtail_all_t.ap(), bb_t.ap()
    hfin_own, hfin_all, ma, hh, pT, trig = (hfin_own_t.ap(), hfin_all_t.ap(), ma_t.ap(),
                                            hh_t.ap(), pT_t.ap(), trig_t.ap())
    kn = hh[0:4096, :].rearrange("(r two) n -> r (two n)", two=2)
    vv = hh[4096:8192, :].rearrange("(r f) n -> r (f n)", f=2048 // T)
    aa = zT[C_XB:C_XB + D, :]
    mg = qn
    dbg_outs = {}
    dbg_src = {"xT": (xT_t, [D, T], F32), "zT": (zT_t, [DIN + 64, T], F32), "qn": (qn_t, [NH * 128, T], BF16),
               "qr": (qr_t, [NH * 64, T], BF16),
               "bb": (bb_t, [D, T], F32), "ma": (ma_t, [D, T], F32),
               "hh": (hh_t, [DFF, T], BF16), "trig": (trig_t, [256, T], F32),
               "hfin_all": (hfin_all_t, [2 * D, 1], F32), "xtail_all": (xtail_all_t, [2 * D, 4], F32),
               "pT": (pT_t, [DEPTH * DPLE, T], BF16)}
    for nme in dbg:
        dbg_outs[nme] = nc.dram_tensor("dbg_" + nme, dbg_src[nme][1], dbg_src[nme][2], kind="ExternalOutput").ap()

    PAIRS = [[0, 1], [2, 3], [4, 5], [6, 7]]

    with ExitStack() as G:
        K = Trk(nc, G)
        PE, ACT, DVE, POOL, SP = K.pe, K.act, K.dve, K.pool, K.sp

        uid = [0]

        def sb(stack, name, shape, dt):
            uid[0] += 1
            return stack.enter_context(nc.sbuf_tensor("%s_u%d" % (name, uid[0]), shape, dt))

        B = {n: Buf(n) for n in ["xT", "zT", "qn", "qr", "ckv_own", "ckv_all", "kn", "vv", "xtail_own", "xtail_all",
                                 "aa", "bb", "hfin_own", "hfin_all", "ma", "mg", "hh", "pT", "trig", "out"]}

        vecs = sb(G, "vecs", [128, NV], F32)
        cst = sb(G, "cst", [128, 256], F32)
        ident_f = cst[:, 0:128]
        ones_f = cst[:, 128:256]
        cbf = sb(G, "cbf", [128, 256], BF16)
        ident_b = cbf[:, 0:128]
        ones_b = cbf[:, 128:256]
        gl = K.buf("gl", dma=True)
        K.dma(SP, gl.dsem, vecs[:], vecs_in[:, :], writes=[gl])
        K.dma(SP, gl.dsem, cst[:], cst_in[:, :], writes=[gl])
        K.op(DVE, lambda: nc.vector.tensor_copy(out=cbf[:], in_=cst[:]), reads=[gl], writes=[gl])

        def vcol(c):
            return vecs[:, c:c + 1]

        PS = Ring([Slot(G.enter_context(nc.psum_tensor("ps%d" % i, [128, 512], F32)), Buf("ps%d" % i))
                   for i in range(8)])
        WS = Ring([Slot(sb(G, "ws%d" % i, [128, 8192], BF16), K.buf("ws%d" % i, dma=True)) for i in range(3)])
        SF = Ring([Slot(sb(G, "sf%d" % i, [128, 512], F32), K.buf("sf%d" % i, dma=True)) for i in range(4)])
        SBF = Ring([Slot(sb(G, "sbf%d" % i, [128, 512], BF16), K.buf("sbf%d" % i, dma=True)) for i in range(4)])
        XO = Ring([Slot(sb(G, "xo%d" % i, [128, 512], F32), K.buf("xo%d" % i, dma=True)) for i in range(4)])
        evac_tog = [0]
        K.scoped = True

        def load_w(slot, src, KC, gcols):
            dst = slot.t[:, 0:KC * gcols].rearrange("p (c n) -> p c n", c=KC)
            K.dma(POOL, slot.dsem, dst, src.rearrange("(c p) n -> p c n", p=128), writes=[slot.buf],
                  max_dma_last_dim=4096)

        def mm_group(ps, msz, KC, lhs_fn, rhs_fn, reads, psb):
            def emit():
                last = None
                for c in range(KC):
                    last = nc.tensor.matmul(ps[0:msz, :], lhsT=lhs_fn(c), rhs=rhs_fn(c),
                                            start=(c == 0), stop=(c == KC - 1))
                return last
            K.op(PE, emit, reads=reads, writes=[psb])

        def linear(groups, KC, nlist, rhs_fn, act_bufs, evac, prefetch=None):
            tiles = []
            for gi, g in enumerate(groups):
                for blk in g[2]:
                    for n in nlist:
                        tiles.append((gi, blk, n))
            pres = {}
            PD = 2

            def do_pre(i):
                if prefetch is not None and i < len(tiles) and i not in pres:
                    pres[i] = prefetch(tiles[i][1][2], tiles[i][2])
            cur_g = -1
            slot = None
            extra = None
            for i, (gi, blk, n) in enumerate(tiles):
                if gi != cur_g:
                    cur_g = gi
                    src, gcols, _, post = groups[gi]
                    slot = WS.next()
                    load_w(slot, src, KC, gcols)
                    extra = post(slot, gcols) if post is not None else None
                for j in range(i, i + PD + 1):
                    do_pre(j)
                col0, msz, tag = blk
                gcols = groups[gi][1]
                ps = PS.next()
                if extra is not None and isinstance(tag, tuple) and tag[0] == "rot":
                    et, ebuf = extra
                    mm_group(ps.t, msz, KC, lambda c: et[:, c * 64:(c + 1) * 64], lambda c: rhs_fn(c, n),
                             [ebuf, act_bufs[n]], ps.buf)
                else:
                    mm_group(ps.t, msz, KC, lambda c: slot.t[:, c * gcols + col0:c * gcols + col0 + msz],
                             lambda c: rhs_fn(c, n), [slot.buf, act_bufs[n]], ps.buf)
                evac(tag, n, ps.t, ps.buf, msz, pres.pop(i, None))

        def store(stg, dst_ap, msz, dbuf):
            K.dma(SP, stg.dsem, dst_ap, stg.t[0:msz, :], reads=[stg.buf], pwrites=[dbuf])

        def nsl(n):
            return slice(n * 512, (n + 1) * 512)

        def rms_to_bf16(L, src_ap, src_buf, KC, gcol0, out_t, out_bufs, dim):
            W = 256 if KC > 4 else 512
            xin = [Slot(sb(L, "nx%d" % i, [128, KC * W], F32), K.buf("nx%d" % i, dma=True)) for i in range(2)]
            sq = [Slot(sb(L, "nsq%d" % i, [128, KC * W], F32), Buf("nsq%d" % i)) for i in range(1)]
            rs = [Slot(sb(L, "nrs%d" % i, [128, W], F32), Buf("nrs%d" % i)) for i in range(2)]
            for j in range(T // W):
                n = (j * W) // 512
                t0 = j * W
                xs = xin[j % 2]
                K.dma(SP, xs.dsem, xs.t[:].rearrange("p (c n) -> p c n", c=KC),
                      src_ap[:, t0:t0 + W].rearrange("(c p) n -> p c n", p=128), reads=[src_buf], writes=[xs.buf])
                s = sq[0]
                K.op(ACT, lambda: nc.scalar.activation(out=s.t[:], in_=xs.t[:], func=AF.Square),
                     reads=[xs.buf], writes=[s.buf])
                ps = PS.next()

                def emit():
                    last = None
                    for c in range(KC):
                        last = nc.tensor.matmul(ps.t[:, 0:W], lhsT=ones_f, rhs=s.t[:, c * W:(c + 1) * W],
                                                start=(c == 0), stop=(c == KC - 1))
                    return last
                K.op(PE, emit, reads=[s.buf, gl], writes=[ps.buf])
                r = rs[j % 2]
                K.op(ACT, lambda: nc.scalar.activation(out=r.t[:], in_=ps.t[:, 0:W], func=AF.Sqrt, scale=1.0 / dim,
                                                       bias=float(EPS)), reads=[ps.buf], writes=[r.buf])
                K.op(DVE, lambda: nc.vector.reciprocal(out=r.t[:], in_=r.t[:]), reads=[r.buf], writes=[r.buf])
                for c in range(KC):
                    K.op(DVE, lambda: nc.vector.scalar_tensor_tensor(
                        out=out_t[:, c * T + t0:c * T + t0 + W], in0=xs.t[:, c * W:(c + 1) * W],
                        scalar=vcol(gcol0 + c), in1=r.t[:], op0=ALU.mult, op1=ALU.mult),
                        reads=[xs.buf, r.buf, gl], pwrites=[out_bufs[n]])

        with ExitStack() as L:
            xl = [Slot(sb(L, "p0x%d" % i, [128, 4 * D], F32), K.buf("p0x%d" % i, dma=True)) for i in range(2)]
            for gq in range(NQ):
                s = xl[gq % 2]
                K.dma(SP, s.dsem, s.t[:].rearrange("p (a d) -> p a d", a=4),
                      x_in[gq * 512:(gq + 1) * 512, :].rearrange("(a p) d -> p a d", p=128), writes=[s.buf])
                for c in range(KD):
                    ps = PS.next()

                    def emit():
                        last = None
                        for a in range(4):
                            last = nc.tensor.transpose(ps.t[:, a * 128:(a + 1) * 128],
                                                       s.t[:, a * D + c * 128:a * D + (c + 1) * 128], ident_f)
                        return last
                    K.op(PE, emit, reads=[s.buf, gl], writes=[ps.buf])
                    st = SF.next()
                    evac_tog[0] ^= 1
                    if evac_tog[0]:
                        K.op(ACT, lambda: nc.scalar.copy(out=st.t[:], in_=ps.t[:]), reads=[ps.buf], writes=[st.buf])
                    else:
                        K.op(DVE, lambda: nc.vector.tensor_copy(out=st.t[:], in_=ps.t[:]), reads=[ps.buf],
                             writes=[st.buf])
                    store(st, xT[c * 128:(c + 1) * 128, nsl(gq)], 128, B["xT"])
            pl = [Slot(sb(L, "p0p%d" % i, [128, 4 * DPLE], F32), K.buf("p0p%d" % i, dma=True)) for i in range(2)]
            it = 0
            for l in range(DEPTH):
                for gq in range(NQ):
                    s = pl[it % 2]
                    it += 1
                    K.dma(SP, s.dsem, s.t[:].rearrange("p (a d) -> p a d", a=4),
                          p_in[l, gq * 512:(gq + 1) * 512, :].rearrange("(a p) d -> p a d", p=128), writes=[s.buf])
                    for c in range(2):
                        ps = PS.next()

                        def emit():
                            last = None
                            for a in range(4):
                                last = nc.tensor.transpose(ps.t[:, a * 128:(a + 1) * 128],
                                                           s.t[:, a * DPLE + c * 128:a * DPLE + (c + 1) * 128], ident_f)
                            return last
                        K.op(PE, emit, reads=[s.buf, gl], writes=[ps.buf])
                        st = SBF.next()
                        K.op(ACT, lambda: nc.scalar.copy(out=st.t[:], in_=ps.t[:]), reads=[ps.buf], writes=[st.buf])
                        store(st, pT[l * DPLE + c * 128:l * DPLE + (c + 1) * 128, nsl(gq)], 128, B["pT"])
        K.barrier()
        with ExitStack() as L:
            posi = Slot(sb(L, "posi", [64, T], I32), K.buf("posi", dma=True))
            K.dma(SP, posi.dsem, posi.t[:], pos_in[0, :].partition_broadcast(64), writes=[posi.buf])
            tb = Buf("trigb")
            ang = sb(L, "ang", [64, T], F32)
            nf = sb(L, "nf", [64, T], F32)
            ni = sb(L, "ni", [64, T], I32)
            rr = sb(L, "rr", [64, T], F32)
            tt = sb(L, "tt", [64, T], F32)
            tr = sb(L, "tr", [64, 4 * T], F32)
            K.op(DVE, lambda: nc.vector.tensor_copy(out=ang[:], in_=posi.t[:]), reads=[posi.buf], writes=[tb])
            K.op(DVE, lambda: nc.vector.tensor_scalar(out=ang[:], in0=ang[:], scalar1=vecs[0:64, V_INVF:V_INVF + 1],
                                                      scalar2=None, op0=ALU.mult), reads=[gl], writes=[tb])
            K.op(DVE, lambda: nc.vector.tensor_scalar(out=nf[:], in0=ang[:], scalar1=float(1.0 / (2 * np.pi)),
                                                      scalar2=None, op0=ALU.mult), writes=[tb])
            K.op(DVE, lambda: nc.vector.tensor_copy(out=ni[:], in_=nf[:]), writes=[tb])
            K.op(DVE, lambda: nc.vector.tensor_copy(out=nf[:], in_=ni[:]), writes=[tb])
            K.op(DVE, lambda: nc.vector.scalar_tensor_tensor(out=rr[:], in0=nf[:], scalar=-TWO_PI_HI, in1=ang[:],
                                                             op0=ALU.mult, op1=ALU.add), writes=[tb])
            K.op(DVE, lambda: nc.vector.scalar_tensor_tensor(out=rr[:], in0=nf[:], scalar=-TWO_PI_LO, in1=rr[:],
                                                             op0=ALU.mult, op1=ALU.add), writes=[tb])
            K.op(DVE, lambda: nc.vector.tensor_scalar(out=tt[:], in0=rr[:], scalar1=float(np.pi),
                                                      scalar2=float(-2 * np.pi), op0=ALU.is_gt, op1=ALU.mult),
                 writes=[tb])
            K.op(DVE, lambda: nc.vector.tensor_tensor(out=rr[:], in0=rr[:], in1=tt[:], op=ALU.add), writes=[tb])
            K.op(DVE, lambda: nc.vector.tensor_scalar(out=tt[:], in0=rr[:], scalar1=float(-np.pi),
                                                      scalar2=float(2 * np.pi), op0=ALU.is_lt, op1=ALU.mult),
                 writes=[tb])
            K.op(DVE, lambda: nc.vector.tensor_tensor(out=rr[:], in0=rr[:], in1=tt[:], op=ALU.add), writes=[tb])
            K.op(DVE, lambda: nc.vector.tensor_scalar(out=rr[:], in0=rr[:], scalar1=float(np.pi), scalar2=float(-np.pi),
                                                      op0=ALU.min, op1=ALU.max), writes=[tb])
            K.op(ACT, lambda: nc.scalar.activation(out=tt[:], in_=rr[:], func=AF.Abs), writes=[tb])
            K.op(DVE, lambda: nc.vector.tensor_scalar(out=tt[:], in0=tt[:], scalar1=-1.0, scalar2=float(np.pi / 2),
                                                      op0=ALU.mult, op1=ALU.add), writes=[tb])
            K.op(ACT, lambda: nc.scalar.activation(out=tr[:, T:2 * T], in_=rr[:], func=AF.Sin), reads=[tb], writes=[tb])
            K.op(ACT, lambda: nc.scalar.activation(out=tr[:, 0:T], in_=tt[:], func=AF.Sin), reads=[tb], writes=[tb])
            K.op(DVE, lambda: nc.vector.tensor_scalar(out=tr[:, 2 * T:4 * T], in0=tr[:, 0:2 * T], scalar1=SCALE,
                                                      scalar2=None, op0=ALU.mult), reads=[tb], writes=[tb])
            trd = K.sem("d_trig")
            K.dma(SP, trd, trig.rearrange("(a p) n -> p a n", p=64), tr[:].rearrange("p (a n) -> p a n", a=4),
                  reads=[tb], writes=[B["trig"]])
        K.barrier()

        for l in range(depth):
            vb = l * V_PER_LAYER
            with ExitStack() as L:
                uT = sb(L, "uT", [128, KD * T], BF16)
                ub = [Buf("ub%d" % n) for n in range(NQ)]
                with ExitStack() as L2:
                    rms_to_bf16(L2, xT, B["xT"], KD, vb + V_GMIX, uT, ub, D)
                K.barrier()
                segs = [(C_CQ, 512, "copy"), (C_CKV, 512, "copy"), (C_KPE, 64, "kpe"), (C_XB, 2048, "copy"),
                        (C_YB, 2048, "gelu"), (C_GA, 2048, "sig"), (C_GR, 2048, "sig")]
                groups = []
                for (c0, width, kind) in segs:
                    if kind == "kpe":
                        def post(slot, gcols):
                            rt = sb(L, "krotw", [128, KD * 64], BF16)
                            rb = Buf("krotw")
                            for c in range(KD):
                                K.op(DVE, lambda: nc.vector.tensor_scalar(
                                    out=rt[:, c * 64:c * 64 + 32], in0=slot.t[:, c * 64 + 32:c * 64 + 64],
                                    scalar1=-1.0, scalar2=None, op0=ALU.mult), reads=[slot.buf], pwrites=[rb])
                                K.op(DVE, lambda: nc.vector.tensor_copy(
                                    out=rt[:, c * 64 + 32:c * 64 + 64], in_=slot.t[:, c * 64:c * 64 + 32]),
                                    reads=[slot.buf], pwrites=[rb])
                            return rt, rb
                        groups.append((w_in[l, :, c0:c0 + 64], 64, [(0, 64, ("copy", c0)), (0, 64, ("rot", C_KROT))],
                                       post))
                    else:
                        for g0 in range(0, width, 512):
                            blocks = [(m0, 128, (kind, c0 + g0 + m0)) for m0 in range(0, 512, 128)]
                            groups.append((w_in[l, :, c0 + g0:c0 + g0 + 512], 512, blocks, None))

                def evacA(tag, n, ps, psb, msz, pre):
                    kind, row0 = tag
                    st = SF.next()
                    if kind in ("copy", "rot"):
                        evac_tog[0] ^= 1
                        if evac_tog[0]:
                            K.op(ACT, lambda: nc.scalar.copy(out=st.t[0:msz, :], in_=ps[0:msz, :]), reads=[psb],
                                 writes=[st.buf])
                        else:
                            K.op(DVE, lambda: nc.vector.tensor_copy(out=st.t[0:msz, :], in_=ps[0:msz, :]), reads=[psb],
                                 writes=[st.buf])
                    else:
                        f = AF.Gelu_apprx_tanh if kind == "gelu" else AF.Sigmoid
                        K.op(ACT, lambda: nc.scalar.activation(out=st.t[0:msz, :], in_=ps[0:msz, :], func=f),
                             reads=[psb], writes=[st.buf])
                    store(st, zT[row0:row0 + msz, nsl(n)], msz, B["zT"])
                linear(groups, KD, list(range(NQ)), lambda c, n: uT[:, c * T + n * 512:c * T + (n + 1) * 512], ub, evacA)
            K.barrier()

            with ExitStack() as L:
                cqT = sb(L, "cqT", [128, 4 * T], BF16)
                cqb = [Buf("cqb%d" % n) for n in range(NQ)]
                with ExitStack() as L2:
                    ckvT = sb(L2, "ckvT", [128, 4 * T], BF16)
                    ckb = [Buf("ckb%d" % n) for n in range(NQ)]
                    with ExitStack() as L3:
                        rms_to_bf16(L3, zT[C_CQ:C_CQ + 512, :], B["zT"], 4, vb + V_GQ, cqT, cqb, DQ)
                    K.barrier()
                    with ExitStack() as L3:
                        rms_to_bf16(L3, zT[C_CKV:C_CKV + 512, :], B["zT"], 4, vb + V_GKV, ckvT, ckb, DKV)
                    K.barrier()
                    d1 = K.sem("d_ckvst%d" % l)
                    for pc in range(NP):
                        K.dma(SP, d1, ckv_own[pc][0:512, :].rearrange("(c p) n -> p c n", p=128),
                              ckvT[:].rearrange("p (c n) -> p c n", c=4)[:, :, pc * CW:(pc + 1) * CW], reads=ckb,
                              pwrites=[B["ckv_own"]])
                    kraw = Slot(sb(L2, "kraw", [64, T], F32), K.buf("kraw%d" % l, dma=True))
                    krot = Slot(sb(L2, "krot", [64, T], F32), K.buf("krot%d" % l, dma=True))
                    cs = Slot(sb(L2, "cs", [64, 2 * T], F32), K.buf("cs%d" % l, dma=True))
                    kpe = Slot(sb(L2, "kpe", [64, T], BF16), K.buf("kpe%d" % l, dma=True))
                    K.dma(SP, kraw.dsem, kraw.t[:], zT[C_KPE:C_KPE + 64, :], reads=[B["zT"]], writes=[kraw.buf])
                    K.dma(SP, krot.dsem, krot.t[:], zT[C_KROT:C_KROT + 64, :], reads=[B["zT"]], writes=[krot.buf])
                    K.dma(SP, cs.dsem, cs.t[:].rearrange("p (a n) -> p a n", a=2),
                          trig[0:128, :].rearrange("(a p) n -> p a n", p=64), reads=[B["trig"]], writes=[cs.buf])
                    K.op(DVE, lambda: nc.vector.tensor_tensor(out=kraw.t[:], in0=kraw.t[:], in1=cs.t[:, 0:T],
                                                              op=ALU.mult), reads=[cs.buf], writes=[kraw.buf])
                    K.op(DVE, lambda: nc.vector.tensor_tensor(out=krot.t[:], in0=krot.t[:], in1=cs.t[:, T:2 * T],
                                                              op=ALU.mult), reads=[cs.buf], writes=[krot.buf])
                    K.op(DVE, lambda: nc.vector.tensor_tensor(out=kpe.t[:], in0=kraw.t[:], in1=krot.t[:], op=ALU.add),
                         reads=[kraw.buf, krot.buf], writes=[kpe.buf])
                    for pc in range(NP):
                        K.dma(SP, kpe.dsem, ckv_own[pc][512:576, :], kpe.t[:, pc * CW:(pc + 1) * CW], reads=[kpe.buf],
                              pwrites=[B["ckv_own"]])
                    d2 = K.sem("d_xtail%d" % l)
                    with nc.allow_non_contiguous_dma(reason="tiny conv tail"):
                        K.dma(SP, d2, xtail_own[:, 0:3], zT[C_XB:C_XB + D, T - 3:T], reads=[B["zT"]],
                              pwrites=[B["xtail_own"]])
                    for pc in range(NP):
                        K.cc(PAIRS, ckv_own_ts[pc], ckv_all_ts[pc], reads=[B["ckv_own"]],
                             writes=[B["ckv_all"]] if pc == 0 else (), pwrites=[B["ckv_all"]] if pc else ())
                    K.cc(PAIRS, xtail_own_t, xtail_all_t, reads=[B["xtail_own"]], writes=[B["xtail_all"]])
                K.barrier()
                with ExitStack() as L2:
                    css = Slot(sb(L2, "css", [64, 2 * T], F32), K.buf("css%d" % l, dma=True))
                    K.dma(SP, css.dsem, css.t[:].rearrange("p (a n) -> p a n", a=2),
                          trig[128:256, :].rearrange("(a p) n -> p a n", p=64), reads=[B["trig"]], writes=[css.buf])
                    qtmp = [Slot(sb(L2, "qtmp%d" % i, [64, 512], F32), Buf("qtmp%d" % i)) for i in range(2)]
                    qrw = [Slot(sb(L2, "qrw%d" % i, [128, 2 * 4 * 64], BF16), Buf("qrw%d" % i)) for i in range(2)]
                    for gi, hg in enumerate(range(0, NH, 2)):
                        gcols = 384
                        slot = WS.next()
                        load_w(slot, w_qb[l, :, hg * 192:(hg + 2) * 192], 4, gcols)
                        rw = qrw[gi % 2]
                        first = True
                        for hh_ in range(2):
                            for c in range(4):
                                base = c * gcols + hh_ * 192 + 128
                                ob = (hh_ * 4 + c) * 64
                                K.op(POOL, lambda: nc.gpsimd.tensor_scalar(
                                    out=rw.t[:, ob:ob + 32], in0=slot.t[:, base + 32:base + 64], scalar1=-1.0,
                                    scalar2=None, op0=ALU.mult), reads=[slot.buf],
                                    writes=[rw.buf] if first else (), pwrites=() if first else [rw.buf])
                                first = False
                                K.op(POOL, lambda: nc.gpsimd.tensor_copy(out=rw.t[:, ob + 32:ob + 64],
                                                                         in_=slot.t[:, base:base + 32]),
                                     reads=[slot.buf], pwrites=[rw.buf])
                        for hh_ in range(2):
                            h = hg + hh_
                            for n in range(NQ):
                                rhs = lambda c: cqT[:, c * T + n * 512:c * T + (n + 1) * 512]
                                col0 = hh_ * 192
                                ps = PS.next()
                                mm_group(ps.t, 128, 4, lambda c: slot.t[:, c * gcols + col0:c * gcols + col0 + 128],
                                         rhs, [slot.buf, cqb[n]], ps.buf)
                                st = SBF.next()
                                K.op(ACT, lambda: nc.scalar.activation(out=st.t[:], in_=ps.t[:], func=AF.Copy,
                                                                       scale=SCALE), reads=[ps.buf], writes=[st.buf])
                                store(st, qn[h * 128:(h + 1) * 128, nsl(n)], 128, B["qn"])
                                ps1 = PS.next()
                                mm_group(ps1.t, 64, 4,
                                         lambda c: slot.t[:, c * gcols + col0 + 128:c * gcols + col0 + 192],
                                         rhs, [slot.buf, cqb[n]], ps1.buf)
                                ps2 = PS.next()
                                mm_group(ps2.t, 64, 4, lambda c: rw.t[:, (hh_ * 4 + c) * 64:(hh_ * 4 + c + 1) * 64], rhs,
                                         [rw.buf, cqb[n]], ps2.buf)
                                qt = qtmp[0]
                                q2 = qtmp[1]
                                K.op(DVE, lambda: nc.vector.tensor_tensor(out=qt.t[:], in0=ps1.t[0:64, :],
                                                                          in1=css.t[:, n * 512:(n + 1) * 512],
                                                                          op=ALU.mult),
                                     reads=[ps1.buf, css.buf], writes=[qt.buf])
                                K.op(DVE, lambda: nc.vector.tensor_tensor(out=q2.t[:], in0=ps2.t[0:64, :],
                                                                          in1=css.t[:, T + n * 512:T + (n + 1) * 512],
                                                                          op=ALU.mult),
                                     reads=[ps2.buf, css.buf], writes=[q2.buf])
                                st = SBF.next()
                                K.op(DVE, lambda: nc.vector.tensor_tensor(out=st.t[0:64, :], in0=qt.t[:],
                                                                          in1=q2.t[:], op=ALU.add),
                                     reads=[qt.buf, q2.buf], writes=[st.buf])
                                store(st, qr[h * 64:(h + 1) * 64, nsl(n)], 64, B["qr"])
            K.barrier()

            with ExitStack() as L:
                NK = T2 // 512
                cka = Slot(sb(L, "cka", [128, 4 * T2], BF16), K.buf("cka%d" % l, dma=True))
                cka3 = cka.t[:].rearrange("p (c n) -> p c n", c=4)
                first = True
                for r in range(2):
                    for pc in range(NP):
                        K.dma(SP, cka.dsem, cka3[:, :, r * T + pc * CW:r * T + (pc + 1) * CW],
                              ckv_all[pc][r * 576:r * 576 + 512, :].rearrange("(c p) n -> p c n", p=128),
                              reads=[B["ckv_all"]], writes=[cka.buf] if first else (), pwrites=() if first else [cka.buf])
                        first = False
                wk = Slot(sb(L, "wk", [128, 4 * 2048], BF16), K.buf("wk%d" % l, dma=True))
                wv = Slot(sb(L, "wv", [128, 4 * 2048], BF16), K.buf("wv%d" % l, dma=True))
                wsrc = w_kvb[l].rearrange("(c p) (h two d) -> p c h two d", p=128, two=2, d=128)
                for c in range(4):
                    K.dma(POOL, wk.dsem, wk.t[:, c * 2048:(c + 1) * 2048].rearrange("p (h d) -> p h d", d=128),
                          wsrc[:, c, :, 0, :], writes=[wk.buf] if c == 0 else (), pwrites=[wk.buf] if c else ())
                    K.dma(POOL, wv.dsem, wv.t[:, c * 2048:(c + 1) * 2048].rearrange("p (h d) -> p h d", d=128),
                          wsrc[:, c, :, 1, :], writes=[wv.buf] if c == 0 else (), pwrites=[wv.buf] if c else ())
                for h in range(NH):
                    for kc in range(NK):
                        ps = PS.next()
                        mm_group(ps.t, 128, 4, lambda c: wk.t[:, c * 2048 + h * 128:c * 2048 + (h + 1) * 128],
                                 lambda c: cka.t[:, c * T2 + kc * 512:c * T2 + (kc + 1) * 512], [wk.buf, cka.buf], ps.buf)
                        st = SBF.next()
                        evac_tog[0] ^= 1
                        if evac_tog[0]:
                            K.op(ACT, lambda: nc.scalar.copy(out=st.t[:], in_=ps.t[:]), reads=[ps.buf], writes=[st.buf])
                        else:
                            K.op(DVE, lambda: nc.vector.tensor_copy(out=st.t[:], in_=ps.t[:]), reads=[ps.buf],
                                 writes=[st.buf])
                        store(st, kn[h * 128:(h + 1) * 128, kc * 512:(kc + 1) * 512], 128, B["kn"])
                for kt in range(T2 // 128):
                    for cb in range(4):
                        ps = PS.next()
                        mm_group(ps.t, 128, 4, lambda c: cka.t[:, c * T2 + kt * 128:c * T2 + (kt + 1) * 128],
                                 lambda c: wv.t[:, c * 2048 + cb * 512:c * 2048 + (cb + 1) * 512], [wv.buf, cka.buf],
                                 ps.buf)
                        st = SBF.next()
                        evac_tog[0] ^= 1
                        if evac_tog[0]:
                            K.op(ACT, lambda: nc.scalar.copy(out=st.t[:], in_=ps.t[:]), reads=[ps.buf], writes=[st.buf])
                        else:
                            K.op(DVE, lambda: nc.vector.tensor_copy(out=st.t[:], in_=ps.t[:]), reads=[ps.buf],
                                 writes=[st.buf])
                        store(st, vv[kt * 128:(kt + 1) * 128, cb * 512:(cb + 1) * 512], 128, B["vv"])
            K.barrier()

            with ExitStack() as L:
                wab = Slot(sb(L, "wab", [128, NH * 128], BF16), K.buf("wab%d" % l, dma=True))
                wxb = Slot(sb(L, "wxb", [128, NH * 128], BF16), K.buf("wxb%d" % l, dma=True))
                K.dma(POOL, wab.dsem, wab.t[:].rearrange("p (h j) -> p h j", h=NH), w_a[l].rearrange("h i j -> i h j"),
                      writes=[wab.buf])
                K.dma(POOL, wxb.dsem, wxb.t[:].rearrange("p (h j) -> p h j", h=NH), w_x[l].rearrange("h i j -> i h j"),
                      writes=[wxb.buf])
                lc = sb(L, "lc", [128, 16 * 6], F32)
                lb = Buf("lc")
                lam = vecs[:, vb + V_LAM:vb + V_LAM + 16]
                e_, s_, s2_, pl_, c_, c2_ = [lc[:, i * 16:(i + 1) * 16] for i in range(6)]
                K.op(ACT, lambda: nc.scalar.activation(out=e_, in_=lam, func=AF.Abs), reads=[gl], writes=[lb])
                K.op(ACT, lambda: nc.scalar.activation(out=e_, in_=e_, func=AF.Exp, scale=-1.0), writes=[lb])
                K.op(DVE, lambda: nc.vector.tensor_scalar(out=s_, in0=e_, scalar1=2.0, scalar2=None, op0=ALU.add),
                     writes=[lb])
                K.op(DVE, lambda: nc.vector.reciprocal(out=s_, in_=s_), writes=[lb])
                K.op(DVE, lambda: nc.vector.tensor_tensor(out=s_, in0=s_, in1=e_, op=ALU.mult), writes=[lb])
                K.op(DVE, lambda: nc.vector.tensor_tensor(out=s2_, in0=s_, in1=s_, op=ALU.mult), writes=[lb])
                K.op(DVE, lambda: nc.vector.tensor_scalar(out=pl_, in0=s2_, scalar1=1.0 / 9, scalar2=1.0 / 7,
                                                          op0=ALU.mult, op1=ALU.add), writes=[lb])
                for cf in (1.0 / 5, 1.0 / 3, 1.0):
                    K.op(DVE, lambda: nc.vector.tensor_tensor(out=pl_, in0=pl_, in1=s2_, op=ALU.mult), writes=[lb])
                    K.op(DVE, lambda: nc.vector.tensor_scalar(out=pl_, in0=pl_, scalar1=float(cf), scalar2=None,
                                                              op0=ALU.add), writes=[lb])
                K.op(DVE, lambda: nc.vector.tensor_tensor(out=pl_, in0=pl_, in1=s_, op=ALU.mult), writes=[lb])
                K.op(DVE, lambda: nc.vector.tensor_scalar(out=c_, in0=lam, scalar1=-1.0, scalar2=0.0, op0=ALU.mult,
                                                          op1=ALU.max), reads=[gl], writes=[lb])
                K.op(DVE, lambda: nc.vector.scalar_tensor_tensor(out=c_, in0=pl_, scalar=2.0, in1=c_, op0=ALU.mult,
                                                                 op1=ALU.add), writes=[lb])
                K.op(DVE, lambda: nc.vector.tensor_scalar(out=c_, in0=c_, scalar1=-8.0, scalar2=None, op0=ALU.mult),
                     writes=[lb])
                K.op(DVE, lambda: nc.vector.tensor_scalar(out=c2_, in0=c_, scalar1=2.0, scalar2=None, op0=ALU.mult),
                     writes=[lb])
                hf = Slot(sb(L, "hf", [128, 16], F32), K.buf("hf%d" % l, dma=True))
                xbp = [Slot(sb(L, "xbp%d" % i, [128, T + 4], F32), K.buf("xbp%d_%d" % (i, l), dma=True)) for i in range(2)]
                tl = [Slot(sb(L, "tl%d" % i, [128, 4], F32), K.buf("tl%d_%d" % (i, l), dma=True)) for i in range(2)]
                xcL = [Slot(sb(L, "xc%d" % i, [128, T], F32), Buf("xc%d" % i)) for i in range(2)]
                xcbL = [Slot(sb(L, "xcb%d" % i, [128, T], BF16), Buf("xcb%d" % i)) for i in range(2)]
                rtL = [Slot(sb(L, "rt%d" % i, [128, T], F32), Buf("rt%d" % i)) for i in range(2)]
                itL = [Slot(sb(L, "it%d" % i, [128, T], F32), Buf("it%d" % i)) for i in range(2)]
                at_ = Slot(sb(L, "at", [128, T], F32), K.buf("at%d" % l, dma=True))
                bt_ = Slot(sb(L, "bt", [128, T], F32), K.buf("bt%d" % l, dma=True))
                hs = Slot(sb(L, "hs", [128, T], F32), Buf("hs"))
                for cg in range(NH):
                    xs = xbp[cg % 2]
                    tls = tl[cg % 2]
                    xc, xcb, rt_, it_ = xcL[cg % 2], xcbL[cg % 2], rtL[cg % 2], itL[cg % 2]
                    def c1_load(g_):
                        xs_, tls_ = xbp[g_ % 2], tl[g_ % 2]
                        K.dma(SP, xs_.dsem, xs_.t[:, 4:4 + T], zT[C_XB + g_ * 128:C_XB + (g_ + 1) * 128, :],
                              reads=[B["zT"]], writes=[xs_.buf])
                        K.dma(SP, tls_.dsem, tls_.t[:, 0:4], xtail_all[g_ * 128:(g_ + 1) * 128, :],
                              reads=[B["xtail_all"]], writes=[tls_.buf])
                    if cg == 0:
                        c1_load(0)
                    if cg + 1 < NH:
                        c1_load(cg + 1)
                    K.op(DVE, lambda: nc.vector.tensor_scalar(out=xs.t[:, 1:4], in0=tls.t[:, 0:3], scalar1=vcol(V_FLAG),
                                                              scalar2=None, op0=ALU.mult),
                         reads=[tls.buf, gl], pwrites=[xs.buf])
                    cw = lambda j: vcol(vb + V_CONVW + j * 16 + cg)
                    K.op(DVE, lambda: nc.vector.tensor_scalar(out=xc.t[:], in0=xs.t[:, 4:4 + T], scalar1=cw(3),
                                                              scalar2=vcol(vb + V_CONVB + cg), op0=ALU.mult,
                                                              op1=ALU.add), reads=[xs.buf, gl], writes=[xc.buf])
                    for j in range(3):
                        K.op(DVE, lambda: nc.vector.scalar_tensor_tensor(out=xc.t[:], in0=xs.t[:, 1 + j:1 + j + T],
                                                                         scalar=cw(j), in1=xc.t[:], op0=ALU.mult,
                                                                         op1=ALU.add), reads=[xs.buf], writes=[xc.buf])
                    K.op(ACT, lambda: nc.scalar.copy(out=xcb.t[:], in_=xc.t[:]), reads=[xc.buf], writes=[xcb.buf])
                    for n in range(NQ):
                        ps = PS.next()
                        mm_group(ps.t, 128, 1, lambda c: wab.t[:, cg * 128:(cg + 1) * 128],
                                 lambda c: xcb.t[:, nsl(n)], [wab.buf, xcb.buf], ps.buf)
                        K.op(ACT, lambda: nc.scalar.activation(out=rt_.t[:, nsl(n)], in_=ps.t[:], func=AF.Sigmoid,
                                                               bias=vcol(vb + V_BA + cg)),
                             reads=[ps.buf, gl], writes=[rt_.buf] if n == 0 else (), pwrites=[rt_.buf] if n else ())
                        ps2 = PS.next()
                        mm_group(ps2.t, 128, 1, lambda c: wxb.t[:, cg * 128:(cg + 1) * 128],
                                 lambda c: xcb.t[:, nsl(n)], [wxb.buf, xcb.buf], ps2.buf)
                        K.op(ACT, lambda: nc.scalar.activation(out=it_.t[:, nsl(n)], in_=ps2.t[:], func=AF.Sigmoid,
                                                               bias=vcol(vb + V_BX + cg)),
                             reads=[ps2.buf, gl], writes=[it_.buf] if n == 0 else (), pwrites=[it_.buf] if n else ())
                    K.op(ACT, lambda: nc.scalar.activation(out=at_.t[:], in_=rt_.t[:], func=AF.Exp,
                                                           scale=lc[:, 64 + cg:65 + cg]),
                         reads=[rt_.buf, lb], writes=[at_.buf])
                    K.op(ACT, lambda: nc.scalar.activation(out=rt_.t[:], in_=rt_.t[:], func=AF.Exp,
                                                           scale=lc[:, 80 + cg:81 + cg]),
                         reads=[lb], writes=[rt_.buf])
                    K.op(ACT, lambda: nc.scalar.activation(out=rt_.t[:], in_=rt_.t[:], func=AF.Sqrt, scale=-1.0,
                                                           bias=1.0), writes=[rt_.buf])
                    K.op(DVE, lambda: nc.vector.tensor_tensor(out=it_.t[:], in0=it_.t[:], in1=xc.t[:], op=ALU.mult),
                         reads=[xc.buf], writes=[it_.buf])
                    K.op(DVE, lambda: nc.vector.tensor_tensor(out=bt_.t[:], in0=it_.t[:], in1=rt_.t[:], op=ALU.mult),
                         reads=[it_.buf, rt_.buf], writes=[bt_.buf])
                    K.dma(SP, at_.dsem, aa[cg * 128:(cg + 1) * 128, :], at_.t[:], reads=[at_.buf], pwrites=[B["aa"]])
                    K.dma(SP, bt_.dsem, bb[cg * 128:(cg + 1) * 128, :], bt_.t[:], reads=[bt_.buf], pwrites=[B["bb"]])
                    K.op(DVE, lambda: nc.vector.tensor_tensor_scan(out=hs.t[:], data0=at_.t[:], data1=bt_.t[:],
                                                                   initial=0.0, op0=ALU.mult, op1=ALU.add),
                         reads=[at_.buf, bt_.buf], writes=[hs.buf])
                    K.op(DVE, lambda: nc.vector.tensor_copy(out=hf.t[:, cg:cg + 1], in_=hs.t[:, T - 1:T]),
                         reads=[hs.buf], pwrites=[hf.buf])
                with nc.allow_non_contiguous_dma(reason="tiny state vector"):
                    K.dma(SP, hf.dsem, hfin_own.rearrange("(c p) o -> p (c o)", p=128), hf.t[:], reads=[hf.buf],
                          writes=[B["hfin_own"]])
                K.cc(PAIRS, hfin_own_t, hfin_all_t, reads=[B["hfin_own"]], writes=[B["hfin_all"]])
            K.barrier()

            with ExitStack() as L:
                kpeT = Slot(sb(L, "kpeT", [64, T2], BF16), K.buf("kpeT%d" % l, dma=True))
                masks = sb(L, "masks", [128, 8, 512], BF16)
                mkb = K.buf("masks%d" % l, dma=True)
                K.dma(SP, mkb.dsem, masks[:], masks_in.rearrange("a d p n -> p (a d) n"), writes=[mkb])
                first = True
                for r in range(2):
                    for pc in range(NP):
                        K.dma(SP, kpeT.dsem, kpeT.t[:, r * T + pc * CW:r * T + (pc + 1) * CW],
                              ckv_all[pc][r * 576 + 512:r * 576 + 576, :],
                              reads=[B["ckv_all"]], writes=[kpeT.buf] if first else (), pwrites=() if first else [kpeT.buf])
                        first = False
                knT = [Slot(sb(L, "knT%d" % i, [128, T2], BF16), K.buf("knT%d_%d" % (i, l), dma=True)) for i in range(2)]
                vT = [Slot(sb(L, "vT%d" % i, [128, T2], BF16), K.buf("vT%d_%d" % (i, l), dma=True)) for i in range(2)]
                qnT = [Slot(sb(L, "qnT%d" % i, [128, 512], BF16), K.buf("qnT%d_%d" % (i, l), dma=True)) for i in range(2)]
                qrT = [Slot(sb(L, "qrT%d" % i, [64, 512], BF16), K.buf("qrT%d_%d" % (i, l), dma=True)) for i in range(2)]
                gaT = [Slot(sb(L, "gaT%d" % i, [128, 512], F32), K.buf("gaT%d_%d" % (i, l), dma=True)) for i in range(2)]
                pt = Ring([Slot(sb(L, "pt%d" % i, [128, 512], BF16), Buf("pt%d" % i)) for i in range(4)])
                rc = [Slot(sb(L, "rc%d" % i, [128, 512], F32), Buf("rc%d" % i)) for i in range(2)]
                def mk1(nm, dt=F32):
                    return Slot(sb(L, nm, [128, T], dt), K.buf("%s_%d" % (nm, l), dma=True))
                A_, B_, GY, SG, MA, HS = mk1("c2a"), mk1("c2b"), mk1("c2y"), mk1("c2s"), mk1("c2m"), mk1("c2h")
                MO = mk1("c2o", BF16)
                h0 = Slot(sb(L, "c2h0", [128, 2], F32), K.buf("c2h0_%d" % l, dma=True))

                def c2_load(cg):
                    rows = slice(cg * 128, (cg + 1) * 128)
                    K.dma(SP, A_.dsem, A_.t[:], aa[rows, :], reads=[B["aa"]], writes=[A_.buf])
                    K.dma(SP, B_.dsem, B_.t[:], bb[rows, :], reads=[B["bb"]], writes=[B_.buf])
                    K.dma(SP, GY.dsem, GY.t[:], zT[C_YB + cg * 128:C_YB + (cg + 1) * 128, :], reads=[B["zT"]],
                          writes=[GY.buf])
                    K.dma(SP, SG.dsem, SG.t[:], zT[C_GR + cg * 128:C_GR + (cg + 1) * 128, :], reads=[B["zT"]],
                          writes=[SG.buf])
                    K.dma(SP, MA.dsem, MA.t[:], ma[rows, :], reads=[B["ma"]], writes=[MA.buf])
                    K.dma(SP, h0.dsem, h0.t[:, 0:1], hfin_all[rows, :], reads=[B["hfin_all"]], writes=[h0.buf])

                def c2_compute(cg):
                    rows = slice(cg * 128, (cg + 1) * 128)
                    K.op(DVE, lambda: nc.vector.tensor_scalar(out=h0.t[:, 1:2], in0=h0.t[:, 0:1],
                                                              scalar1=vcol(V_FLAG), scalar2=None, op0=ALU.mult),
                         reads=[gl], writes=[h0.buf])
                    K.op(DVE, lambda: nc.vector.tensor_tensor_scan(out=HS.t[:], data0=A_.t[:], data1=B_.t[:],
                                                                   initial=h0.t[:, 1:2], op0=ALU.mult, op1=ALU.add),
                         reads=[A_.buf, B_.buf, h0.buf], writes=[HS.buf])
                    K.op(DVE, lambda: nc.vector.tensor_tensor(out=GY.t[:], in0=GY.t[:], in1=HS.t[:], op=ALU.mult),
                         reads=[HS.buf], writes=[GY.buf])
                    K.op(DVE, lambda: nc.vector.tensor_tensor(out=GY.t[:], in0=GY.t[:], in1=SG.t[:], op=ALU.mult),
                         reads=[SG.buf], writes=[GY.buf])
                    K.op(DVE, lambda: nc.vector.tensor_tensor(out=MO.t[:], in0=GY.t[:], in1=MA.t[:], op=ALU.add),
                         reads=[GY.buf, MA.buf], writes=[MO.buf])
                    K.dma(SP, MO.dsem, mg[rows, :], MO.t[:], reads=[MO.buf], pwrites=[B["mg"]])
                NKT = T // 128
                it = 0
                PSA = Ring(PS.slots[0:4])
                PSS = Ring(PS.slots[4:8])
                def load_kv(h):
                    ks, vs = knT[h % 2], vT[h % 2]
                    K.dma(SP, ks.dsem, ks.t[:], kn[h * 128:(h + 1) * 128, :], reads=[B["kn"]], writes=[ks.buf])
                    K.dma(POOL, vs.dsem, vs.t[:].rearrange("p (k d) -> p k d", d=128),
                          vv[:, h * 128:(h + 1) * 128].rearrange("(k p) d -> p k d", p=128), reads=[B["vv"]],
                          writes=[vs.buf])

                def load_q(it_):
                    h, qc = it_ // NQ, it_ % NQ
                    qs, qrs, gs = qnT[it_ % 2], qrT[it_ % 2], gaT[it_ % 2]
                    K.dma(SP, qs.dsem, qs.t[:], qn[h * 128:(h + 1) * 128, nsl(qc)], reads=[B["qn"]], writes=[qs.buf])
                    K.dma(SP, qrs.dsem, qrs.t[:], qr[h * 64:(h + 1) * 64, nsl(qc)], reads=[B["qr"]], writes=[qrs.buf])
                    K.dma(SP, gs.dsem, gs.t[:], zT[C_GA + h * 128:C_GA + (h + 1) * 128, nsl(qc)], reads=[B["zT"]],
                          writes=[gs.buf])
                load_kv(0)
                load_q(0)
                for h in range(NH + 1):
                    if h >= 1:
                        c2_load(h - 1)
                    if h == NH:
                        c2_compute(h - 1)
                        break
                    ks, vs = knT[h % 2], vT[h % 2]
                    for qc in range(NQ):
                        qs, qrs, gs = qnT[it % 2], qrT[it % 2], gaT[it % 2]
                        rcs = rc[it % 2]
                        if qc == NQ - 1 and h >= 1:
                            c2_compute(h - 1)
                        if qc == 0 and h + 1 < NH:
                            load_kv(h + 1)
                        if it + 1 < NH * NQ:
                            load_q(it + 1)
                        it += 1
                        tiles = []
                        for kt in range(NKT):
                            d = kt - 4 * qc
                            if d < 0:
                                tiles.append((kt, None, None))
                            elif d < 4:
                                tiles.append((kt, d, None))
                            else:
                                tiles.append((kt, None, V_B0))
                        for kt in range(min(NKT, 4 * qc + 4)):
                            d = kt - 4 * qc
                            tiles.append((NKT + kt, (4 + d) if d >= 0 else None, V_B1))
                        ops_ = PSA.next()
                        sps_ = PSA.next()
                        pend = []

                        def qk(ti):
                            kt, dm, bcol = tiles[ti]
                            ps = PSS.next()

                            def emit():
                                nc.tensor.matmul(ps.t[:], lhsT=ks.t[:, kt * 128:(kt + 1) * 128], rhs=qs.t[:], start=True,
                                                 stop=False)
                                last = nc.tensor.matmul(ps.t[:], lhsT=kpeT.t[:, kt * 128:(kt + 1) * 128], rhs=qrs.t[:],
                                                        start=False, stop=(dm is None))
                                if dm is not None:
                                    last = nc.tensor.matmul(ps.t[:], lhsT=ident_b, rhs=masks[:, dm, :], start=False,
                                                            stop=True)
                                return last
                            K.op(PE, emit, reads=[ks.buf, kpeT.buf, qs.buf, qrs.buf, gl, mkb], writes=[ps.buf])
                            p_ = pt.next()
                            if bcol is None:
                                K.op(ACT, lambda: nc.scalar.activation(out=p_.t[:], in_=ps.t[:], func=AF.Exp),
                                     reads=[ps.buf], writes=[p_.buf])
                            else:
                                K.op(ACT, lambda: nc.scalar.activation(out=p_.t[:], in_=ps.t[:], func=AF.Exp,
                                                                       bias=vcol(bcol)),
                                     reads=[ps.buf, gl], writes=[p_.buf])
                            return (kt, p_)

                        def pv(ti, kt, p_):
                            first, lastt = ti == 0, ti == len(tiles) - 1
                            K.op(PE, lambda: nc.tensor.matmul(ops_.t[:], lhsT=vs.t[:, kt * 128:(kt + 1) * 128], rhs=p_.t[:],
                                                              start=first, stop=lastt),
                                 reads=[vs.buf, p_.buf], writes=[ops_.buf] if first else (),
                                 pwrites=() if first else [ops_.buf])
                            K.op(PE, lambda: nc.tensor.matmul(sps_.t[:], lhsT=ones_b, rhs=p_.t[:], start=first, stop=lastt),
                                 reads=[p_.buf, gl], writes=[sps_.buf] if first else (),
                                 pwrites=() if first else [sps_.buf])
                        LOOK = 3
                        for ti in range(len(tiles)):
                            pend.append((ti,) + qk(ti))
                            if len(pend) > LOOK:
                                a = pend.pop(0)
                                pv(*a)
                        for a in pend:
                            pv(*a)
                        K.op(DVE, lambda: nc.vector.reciprocal(out=rcs.t[:], in_=sps_.t[:]), reads=[sps_.buf],
                             writes=[rcs.buf])
                        K.op(DVE, lambda: nc.vector.tensor_tensor(out=rcs.t[:], in0=ops_.t[:], in1=rcs.t[:], op=ALU.mult),
                             reads=[ops_.buf], writes=[rcs.buf])
                        st = SF.next()
                        K.op(DVE, lambda: nc.vector.tensor_tensor(out=st.t[:], in0=rcs.t[:], in1=gs.t[:], op=ALU.mult),
                             reads=[rcs.buf, gs.buf], writes=[st.buf])
                        store(st, ma[h * 128:(h + 1) * 128, nsl(qc)], 128, B["ma"])
            K.barrier()

            def prefetch_x(tag, n):
                xo = XO.next()
                K.dma(SP, xo.dsem, xo.t[:], xT[tag * 128:(tag + 1) * 128, nsl(n)], reads=[B["xT"]], writes=[xo.buf])
                return xo

            def evac_resid(tag, n, ps, psb, msz, xo):
                st = SF.next()
                K.op(DVE, lambda: nc.vector.tensor_tensor(out=st.t[:], in0=ps[:], in1=xo.t[:], op=ALU.add),
                     reads=[psb, xo.buf], writes=[st.buf])
                store(st, xT[tag * 128:(tag + 1) * 128, nsl(n)], 128, B["xT"])

            with ExitStack() as L:
                mgs = Slot(sb(L, "mgs", [128, KD * T], BF16), K.buf("mgs%d" % l, dma=True))
                K.dma(SP, mgs.dsem, mgs.t[:].rearrange("p (c n) -> p c n", c=KD), mg.rearrange("(c p) n -> p c n", p=128),
                      reads=[B["mg"]], writes=[mgs.buf])
                groups = [(w_out[l, :, g0:g0 + 512], 512, [(m0, 128, (g0 + m0) // 128) for m0 in range(0, 512, 128)], None)
                          for g0 in range(0, D, 512)]
                linear(groups, KD, list(range(NQ)), lambda c, n: mgs.t[:, c * T + n * 512:c * T + (n + 1) * 512],
                       [mgs.buf] * NQ, evac_resid, prefetch_x)
            K.barrier()

            with ExitStack() as L:
                uT = sb(L, "uT", [128, KD * T], BF16)
                ub = [Buf("ub%d" % n) for n in range(NQ)]
                with ExitStack() as L2:
                    rms_to_bf16(L2, xT, B["xT"], KD, vb + V_GMLP, uT, ub, D)
                K.barrier()
                groups = [(w_up[l, :, g0:g0 + 512], 512, [(m0, 128, (g0 + m0) // 128) for m0 in range(0, 512, 128)], None)
                          for g0 in range(0, DFF, 512)]
                rl = Ring([Slot(sb(L, "rl%d" % i, [128, 512], F32), Buf("rl%d" % i)) for i in range(3)])

                def evacF1(tag, n, ps, psb, msz, pre):
                    r = rl.next()
                    K.op(ACT, lambda: nc.scalar.activation(out=r.t[:], in_=ps[:], func=AF.Relu), reads=[psb],
                         writes=[r.buf])
                    st = SBF.next()
                    K.op(DVE, lambda: nc.vector.tensor_tensor(out=st.t[:], in0=r.t[:], in1=r.t[:], op=ALU.mult),
                         reads=[r.buf], writes=[st.buf])
                    store(st, hh[tag * 128:(tag + 1) * 128, nsl(n)], 128, B["hh"])
                linear(groups, KD, list(range(NQ)), lambda c, n: uT[:, c * T + n * 512:c * T + (n + 1) * 512], ub, evacF1)
            K.barrier()

            with ExitStack() as L:
                KF = DFF // 128
                G2 = 2 if NQ >= 2 else 1
                hT = Slot(sb(L, "hT", [128, KF * 512 * G2], BF16), K.buf("hT%d" % l, dma=True))
                KH = KF // 2
                for n0 in range(0, NQ, G2):
                    K.dma(SP, hT.dsem, hT.t[:].rearrange("p (c n) -> p c n", c=KF),
                          hh[:, n0 * 512:(n0 + G2) * 512].rearrange("(c p) n -> p c n", p=128), reads=[B["hh"]],
                          writes=[hT.buf])
                    for g in range(D // 256):
                        sA, sB = WS.next(), WS.next()
                        load_w(sA, w_down[l, 0:KH * 128, g * 256:(g + 1) * 256], KH, 256)
                        load_w(sB, w_down[l, KH * 128:KF * 128, g * 256:(g + 1) * 256], KH, 256)
                        todo = [(mb, n_) for mb in range(2) for n_ in range(n0, n0 + G2)]
                        xos = [prefetch_x(g * 2 + mb, n_) for (mb, n_) in todo]
                        pss = [PS.next() for _ in todo]
                        for half, sl_ in ((0, sA), (1, sB)):
                            for (mb, n_), ps in zip(todo, pss):
                                def emit():
                                    last = None
                                    for cc_ in range(KH):
                                        c = half * KH + cc_
                                        last = nc.tensor.matmul(
                                            ps.t[:], lhsT=sl_.t[:, cc_ * 256 + mb * 128:cc_ * 256 + (mb + 1) * 128],
                                            rhs=hT.t[:, c * 512 * G2 + (n_ - n0) * 512:c * 512 * G2 + (n_ - n0 + 1) * 512],
                                            start=(c == 0), stop=(c == KF - 1))
                                    return last
                                K.op(PE, emit, reads=[sl_.buf, hT.buf], writes=[ps.buf] if half == 0 else (),
                                     pwrites=() if half == 0 else [ps.buf])
                        for (mb, n_), ps, xo in zip(todo, pss, xos):
                            evac_resid(g * 2 + mb, n_, ps.t, ps.buf, 128, xo)
            K.barrier()

            with ExitStack() as L:
                uT = sb(L, "uT", [128, KD * T], BF16)
                ub = [Buf("ub%d" % n) for n in range(NQ)]
                with ExitStack() as L2:
                    rms_to_bf16(L2, xT, B["xT"], KD, vb + V_GPLE, uT, ub, D)
                K.barrier()
                pts = Slot(sb(L, "pts", [128, 2 * T], BF16), K.buf("pts%d" % l, dma=True))
                K.dma(SP, pts.dsem, pts.t[:].rearrange("p (c n) -> p c n", c=2),
                      pT[l * DPLE:(l + 1) * DPLE, :].rearrange("(c p) n -> p c n", p=128), reads=[B["pT"]],
                      writes=[pts.buf])
                wpe = Slot(sb(L, "wpe", [128, 2 * D], BF16), K.buf("wpe%d" % l, dma=True))
                for c in range(2):
                    K.dma(POOL, wpe.dsem, wpe.t[:, c * D:(c + 1) * D], w_pe[l, c * 128:(c + 1) * 128, :],
                          writes=[wpe.buf] if c == 0 else (), pwrites=[wpe.buf] if c else (), max_dma_last_dim=4096)
                groups = [(w_pg[l, :, g0:g0 + 512], 512, [(m0, 128, (g0 + m0) // 128) for m0 in range(0, 512, 128)], None)
                          for g0 in range(0, D, 512)]
                sg = Ring([Slot(sb(L, "sg%d" % i, [128, 512], F32), Buf("sg%d" % i)) for i in range(3)])

                def evacG(tag, n, ps, psb, msz, xo):
                    ps2 = PS.next()
                    mm_group(ps2.t, 128, 2, lambda c: wpe.t[:, c * D + tag * 128:c * D + (tag + 1) * 128],
                             lambda c: pts.t[:, c * T + n * 512:c * T + (n + 1) * 512], [wpe.buf, pts.buf], ps2.buf)
                    s = sg.next()
                    K.op(ACT, lambda: nc.scalar.activation(out=s.t[:], in_=ps[:], func=AF.Sigmoid), reads=[psb],
                         writes=[s.buf])
                    K.op(DVE, lambda: nc.vector.tensor_tensor(out=s.t[:], in0=s.t[:], in1=ps2.t[:], op=ALU.mult),
                         reads=[ps2.buf], writes=[s.buf])
                    st = SF.next()
                    K.op(DVE, lambda: nc.vector.tensor_tensor(out=st.t[:], in0=s.t[:], in1=xo.t[:], op=ALU.add),
                         reads=[s.buf, xo.buf], writes=[st.buf])
                    store(st, xT[tag * 128:(tag + 1) * 128, nsl(n)], 128, B["xT"])
                linear(groups, KD, list(range(NQ)), lambda c, n: uT[:, c * T + n * 512:c * T + (n + 1) * 512], ub, evacG,
                       prefetch_x)
            K.barrier()

        with ExitStack() as L:
            xin = [Slot(sb(L, "fx%d" % i, [128, KD * 512], F32), K.buf("fx%d" % i, dma=True)) for i in range(2)]
            sq = Slot(sb(L, "fsq", [128, KD * 512], F32), Buf("fsq"))
            rs = Slot(sb(L, "frs", [128, 512], F32), Buf("frs"))
            yo = [Slot(sb(L, "fy%d" % i, [128, D], F32), K.buf("fy%d" % i, dma=True)) for i in range(2)]
            it = 0
            for n in range(NQ):
                xs = xin[n % 2]
                K.dma(SP, xs.dsem, xs.t[:].rearrange("p (c n) -> p c n", c=KD),
                      xT[:, nsl(n)].rearrange("(c p) n -> p c n", p=128), reads=[B["xT"]], writes=[xs.buf])
                K.op(ACT, lambda: nc.scalar.activation(out=sq.t[:], in_=xs.t[:], func=AF.Square), reads=[xs.buf],
                     writes=[sq.buf])
                ps = PS.next()
                mm_group(ps.t, 128, KD, lambda c: ones_f, lambda c: sq.t[:, c * 512:(c + 1) * 512], [sq.buf, gl], ps.buf)
                K.op(ACT, lambda: nc.scalar.activation(out=rs.t[:], in_=ps.t[:], func=AF.Sqrt, scale=1.0 / D,
                                                       bias=float(EPS)), reads=[ps.buf], writes=[rs.buf])
                K.op(DVE, lambda: nc.vector.reciprocal(out=rs.t[:], in_=rs.t[:]), reads=[rs.buf], writes=[rs.buf])
                for c in range(KD):
                    K.op(DVE, lambda: nc.vector.scalar_tensor_tensor(
                        out=xs.t[:, c * 512:(c + 1) * 512], in0=xs.t[:, c * 512:(c + 1) * 512],
                        scalar=vcol(V_GFINAL + c), in1=rs.t[:], op0=ALU.mult, op1=ALU.mult),
                        reads=[rs.buf, gl], writes=[xs.buf])
                for a in range(4):
                    y = yo[it % 2]
                    it += 1
                    for cb in range(4):
                        ps = PS.next()

                        def emit():
                            last = None
                            for cc_ in range(4):
                                c = cb * 4 + cc_
                                last = nc.tensor.transpose(ps.t[:, cc_ * 128:(cc_ + 1) * 128],
                                                           xs.t[:, c * 512 + a * 128:c * 512 + (a + 1) * 128], ident_f)
                            return last
                        K.op(PE, emit, reads=[xs.buf, gl], writes=[ps.buf])
                        evac_tog[0] ^= 1
                        if evac_tog[0]:
                            K.op(ACT, lambda: nc.scalar.copy(out=y.t[:, cb * 512:(cb + 1) * 512], in_=ps.t[:]),
                                 reads=[ps.buf], writes=[y.buf] if cb == 0 else (), pwrites=[y.buf] if cb else ())
                        else:
                            K.op(DVE, lambda: nc.vector.tensor_copy(out=y.t[:, cb * 512:(cb + 1) * 512], in_=ps.t[:]),
                                 reads=[ps.buf], writes=[y.buf] if cb == 0 else (), pwrites=[y.buf] if cb else ())
                    r0 = n * 512 + a * 128
                    K.dma(SP, y.dsem, y_out[r0:r0 + 128, :], y.t[:], reads=[y.buf], pwrites=[B["out"]])
            for nme, ap in dbg_outs.items():
                ds = K.sem("d_dbg_" + nme)
                src = dbg_src[nme][0].ap()
                rows = dbg_src[nme][1][0]
                step = 1024
                for r0 in range(0, rows, step):
                    r1 = min(rows, r0 + step)
                    K.dma(SP, ds, ap[r0:r1, :], src[r0:r1, :], reads=[B[nme]] if nme in B else [], pwrites=[B["out"]])
        K.final_wait(SP)
        K.final_wait(POOL)
    return nc


def _diag_masks():
    k = np.arange(128)[:, None]
    q = np.arange(512)[None, :]
    pat = np.zeros((4, 128, 512), np.float32)
    for d in range(4):
        pat[d] = np.where(128 * d + k <= q, 0.0, NEG)
    return pat


def make_inputs(T, x, p, positions, g_mix, w_in, g_q, w_qb, g_kv, w_kvb, conv_w, conv_b, w_a, b_a, w_x, b_x,
                lru_lambda, w_out, g_mlp, w_up, w_down, g_ple, w_ple_gate, w_ple_proj, g_final, n_cores=8):
    f = lambda a: np.ascontiguousarray(np.asarray(a, dtype=np.float32))
    x, p = f(x), f(p)
    positions = np.asarray(positions).astype(np.int32)

    def cols(v):
        v = f(v)
        return v.reshape(-1, 128).T

    pat = _diag_masks()
    cst = np.concatenate([np.eye(128, dtype=np.float32), np.ones((128, 128), np.float32)], axis=1)
    inv_freq = np.power(np.float32(10000.0), -np.arange(32, dtype=np.float32) * np.float32(2.0 / 64)).astype(np.float32)
    shared = dict(w_in=f(w_in), w_qb=f(w_qb), w_kvb=f(w_kvb), w_a=f(w_a), w_x=f(w_x), w_out=f(w_out), w_up=f(w_up),
                  w_down=f(w_down), w_ple_gate=f(w_ple_gate), w_ple_proj=f(w_ple_proj), cst=cst)
    in_maps = []
    for c in range(n_cores):
        b, half = c // 2, c % 2
        vecs = np.zeros((128, NV), np.float32)
        for l in range(DEPTH):
            vb = l * V_PER_LAYER
            vecs[:, vb + V_GMIX:vb + V_GMIX + 16] = cols(g_mix[l])
            vecs[:, vb + V_GMLP:vb + V_GMLP + 16] = cols(g_mlp[l])
            vecs[:, vb + V_GPLE:vb + V_GPLE + 16] = cols(g_ple[l])
            vecs[:, vb + V_GQ:vb + V_GQ + 4] = cols(g_q[l])
            vecs[:, vb + V_GKV:vb + V_GKV + 4] = cols(g_kv[l])
            for j in range(4):
                vecs[:, vb + V_CONVW + j * 16:vb + V_CONVW + (j + 1) * 16] = cols(np.asarray(conv_w)[l, j])
            vecs[:, vb + V_CONVB:vb + V_CONVB + 16] = cols(conv_b[l])
            vecs[:, vb + V_BA:vb + V_BA + 16] = cols(b_a[l])
            vecs[:, vb + V_BX:vb + V_BX + 16] = cols(b_x[l])
            vecs[:, vb + V_LAM:vb + V_LAM + 16] = cols(lru_lambda[l])
        vecs[:, V_GFINAL:V_GFINAL + 16] = cols(g_final)
        vecs[0:64, V_INVF] = np.concatenate([inv_freq, inv_freq])
        vecs[:, V_FLAG] = float(half)
        vecs[:, V_B0] = 0.0 if half else NEG
        vecs[:, V_B1] = 0.0 if half else NEG
        masks = np.zeros((2, 4, 128, 512), np.float32)
        if half == 0:
            masks[0] = pat
        else:
            masks[1] = pat
        m = dict(shared)
        m["x"] = np.ascontiguousarray(x[b, half * T:(half + 1) * T, :])
        m["p"] = np.ascontiguousarray(p[:, b, half * T:(half + 1) * T, :])
        m["pos"] = np.ascontiguousarray(positions[b, half * T:(half + 1) * T].reshape(1, T))
        m["vecs"] = vecs
        m["masks"] = masks.astype(ml_dtypes.bfloat16)
        in_maps.append(m)
    return in_maps


_PROG = {}


def run(T, inputs, dbg=(), depth=DEPTH):
    key = (T, tuple(dbg), depth)
    if key not in _PROG:
        _PROG[key] = build_program(T, dbg, depth)
    nc = _PROG[key]
    in_maps = make_inputs(T, **inputs)
    res = run_bass_kernel_spmd(nc, in_maps, core_ids=list(range(8)))
    return res.results


def kernel(**inputs):
    x = np.asarray(inputs["x"])
    Bn, S, _ = x.shape
    T = S // 2
    R = run(T, inputs)
    out = np.zeros((Bn, S, D), np.float32)
    for c in range(8):
        b, half = c // 2, c % 2
        out[b, half * T:(half + 1) * T, :] = R[c]["out"]
    return out
```

```python
import numpy as np
from contextlib import ExitStack
import concourse.bass as bass
import concourse.mybir as mybir
from concourse.bass_utils import run_bass_kernel_spmd
import ml_dtypes

F32, BF16, I32 = mybir.dt.float32, mybir.dt.bfloat16, mybir.dt.int32
AF = mybir.ActivationFunctionType
ALU = mybir.AluOpType

D = 2048
KD = D // 128
NH = 16
DQ = 512
DKV = 512
DR = 64
DIN = 9280
DFF = 8192
DPLE = 256
DEPTH = 2
EPS = 1e-6
C_CQ, C_CKV, C_KPE, C_XB, C_YB, C_GA, C_GR = 0, 512, 1024, 1088, 3136, 5184, 7232
C_KROT = DIN
NEG = -30000.0
SCALE = float((128 + 64) ** -0.5)
TWO_PI_HI = 6.28125
TWO_PI_LO = float(2.0 * np.pi - 6.28125)

V_GMIX, V_GMLP, V_GPLE, V_GQ, V_GKV, V_CONVW, V_CONVB, V_BA, V_BX, V_LAM = 0, 16, 32, 48, 52, 56, 120, 136, 152, 168
V_PER_LAYER = 184
V_GFINAL = 2 * V_PER_LAYER
V_INVF = V_GFINAL + 16
V_FLAG = V_INVF + 1
V_B0 = V_FLAG + 1
V_B1 = V_B0 + 1
NV = V_B1 + 1


class Sem:
    def __init__(self, h, name):
        self.h = h
        self.name = name
        self.cnt = 0


class Buf:
    def __init__(self, name, dsem=None):
        self.name = name
        self.w = {}
        self.r = {}
        self.dsem = dsem


class Eng:
    def __init__(self, name, h, sem):
        self.name = name
        self.h = h
        self.sem = sem
        self.seen = {}


class Trk:
    def __init__(self, nc, stack):
        self.nc = nc
        self.stack = stack
        self.allsems = []
        self.scoped = False
        self.free = []
        self.inuse = []
        self.pe = Eng("pe", nc.tensor, self.sem("e_pe"))
        self.act = Eng("act", nc.scalar, self.sem("e_act"))
        self.dve = Eng("dve", nc.vector, self.sem("e_dve"))
        self.pool = Eng("pool", nc.gpsimd, self.sem("e_pool"))
        self.sp = Eng("sp", nc.sync, None)
        self.engs = [self.pe, self.act, self.dve, self.pool, self.sp]
        self.ccsem = self.sem("e_cc")
        self.n_wait = 0

    def sem(self, name):
        if self.scoped and self.free:
            s = self.free.pop()
            self.inuse.append(s)
            return s
        s = Sem(self.stack.enter_context(self.nc.semaphore(name)), name)
        self.allsems.append(s)
        if self.scoped:
            self.inuse.append(s)
        return s

    def buf(self, name, dma=False):
        return Buf(name, self.sem("d_" + name) if dma else None)

    def _wait1(self, eng, sem, val):
        if sem is eng.sem and eng is self.pe:
            return
        if eng.seen.get(sem, 0) >= val:
            return
        eng.h.wait_ge(sem.h, val)
        eng.seen[sem] = val
        self.n_wait += 1

    def _waits(self, eng, reads, writes, pwrites):
        for b in reads:
            for s, v in b.w.items():
                self._wait1(eng, s, v)
        for b in writes:
            for s, v in b.w.items():
                self._wait1(eng, s, v)
            for s, v in b.r.items():
                self._wait1(eng, s, v)
        for b in pwrites:
            for s, v in b.r.items():
                self._wait1(eng, s, v)

    def _record(self, sem, val, reads, writes, pwrites):
        for b in reads:
            b.r[sem] = val
        for b in writes:
            b.w[sem] = val
        for b in pwrites:
            b.w[sem] = val

    def op(self, eng, emit, reads=(), writes=(), pwrites=()):
        self._waits(eng, reads, writes, pwrites)
        ins = emit()
        eng.sem.cnt += 1
        ins.then_inc(eng.sem.h, 1)
        self._record(eng.sem, eng.sem.cnt, reads, writes, pwrites)

    def dma(self, q, dsem, out, in_, reads=(), writes=(), pwrites=(), **kw):
        self._waits(q, reads, writes, pwrites)
        ins = q.h.dma_start(out=out, in_=in_, **kw)
        dsem.cnt += 16
        ins.then_inc(dsem.h, 16)
        self._record(dsem, dsem.cnt, reads, writes, pwrites)

    def cc(self, groups, in_t, out_t, reads=(), writes=(), pwrites=()):
        q = self.pool
        self._waits(q, reads, writes, pwrites)
        ins = self.nc.gpsimd.collective_compute("AllGather", ALU.bypass, replica_groups=groups,
                                                ins=[in_t.ap().opt()], outs=[out_t.ap().opt()])
        self.ccsem.cnt += 1
        ins.then_inc(self.ccsem.h)
        self._record(self.ccsem, self.ccsem.cnt, reads, writes, pwrites)

    def barrier(self):
        for e in self.engs:
            for s in self.allsems:
                if s.cnt > 0 and s is not self.ccsem:
                    self._wait1(e, s, s.cnt)
        self.free.extend(self.inuse)
        self.inuse = []

    def final_wait(self, eng):
        for s in self.allsems:
            if s.cnt > 0 and s is not eng.sem:
                if eng.seen.get(s, 0) < s.cnt:
                    eng.h.wait_ge(s.h, s.cnt)
                    eng.seen[s] = s.cnt


class Slot:
    def __init__(self, t, buf):
        self.t = t
        self.buf = buf

    @property
    def dsem(self):
        return self.buf.dsem


class Ring:
    def __init__(self, slots):
        self.slots = slots
        self.i = 0

    def next(self):
        s = self.slots[self.i % len(self.slots)]
        self.i += 1
        return s


def build_program(T, dbg=(), depth=DEPTH):
    NQ = T // 512
    NT = T // 128
    T2 = 2 * T
    nc = bass.Bass("TRN2", target_bir_lowering=False)
    x_in = nc.dram_tensor("x", [T, D], F32, kind="ExternalInput").ap()
    p_in = nc.dram_tensor("p", [DEPTH, T, DPLE], F32, kind="ExternalInput").ap()
    pos_in = nc.dram_tensor("pos", [1, T], I32, kind="ExternalInput").ap()
    vecs_in = nc.dram_tensor("vecs", [128, NV], F32, kind="ExternalInput").ap()
    masks_in = nc.dram_tensor("masks", [2, 4, 128, 512], BF16, kind="ExternalInput").ap()
    cst_in = nc.dram_tensor("cst", [128, 256], F32, kind="ExternalInput").ap()
    w_in = nc.dram_tensor("w_in", [DEPTH, D, DIN], F32, kind="ExternalInput").ap()
    w_qb = nc.dram_tensor("w_qb", [DEPTH, DQ, NH * 192], F32, kind="ExternalInput").ap()
    w_kvb = nc.dram_tensor("w_kvb", [DEPTH, DKV, NH * 256], F32, kind="ExternalInput").ap()
    w_a = nc.dram_tensor("w_a", [DEPTH, NH, 128, 128], F32, kind="ExternalInput").ap()
    w_x = nc.dram_tensor("w_x", [DEPTH, NH, 128, 128], F32, kind="ExternalInput").ap()
    w_out = nc.dram_tensor("w_out", [DEPTH, D, D], F32, kind="ExternalInput").ap()
    w_up = nc.dram_tensor("w_up", [DEPTH, D, DFF], F32, kind="ExternalInput").ap()
    w_down = nc.dram_tensor("w_down", [DEPTH, DFF, D], F32, kind="ExternalInput").ap()
    w_pg = nc.dram_tensor("w_ple_gate", [DEPTH, D, D], F32, kind="ExternalInput").ap()
    w_pe = nc.dram_tensor("w_ple_proj", [DEPTH, DPLE, D], F32, kind="ExternalInput").ap()
    y_out = nc.dram_tensor("out", [T, D], F32, kind="ExternalOutput").ap()
    xT_t = nc.dram_tensor("s_xT", [D, T], F32)
    zT_t = nc.dram_tensor("s_zT", [DIN + 64, T], F32)
    qn_t = nc.dram_tensor("s_qn", [NH * 128, T], BF16)
    qr_t = nc.dram_tensor("s_qr", [NH * 64, T], BF16)
    CW = min(T, 1024)
    NP = T // CW
    ckv_own_ts = [nc.dram_tensor("s_ckv_own%d" % i, [576, CW], BF16) for i in range(NP)]
    ckv_all_ts = [nc.dram_tensor("s_ckv_all%d" % i, [2 * 576, CW], BF16) for i in range(NP)]
    xtail_own_t = nc.dram_tensor("s_xtail_own", [D, 4], F32)
    xtail_all_t = nc.dram_tensor("s_xtail_all", [2 * D, 4], F32)
    bb_t = nc.dram_tensor("s_bb", [D, T], F32)
    hfin_own_t = nc.dram_tensor("s_hfin_own", [D, 1], F32)
    hfin_all_t = nc.dram_tensor("s_hfin_all", [2 * D, 1], F32)
    ma_t = nc.dram_tensor("s_ma", [D, T], F32)
    hh_t = nc.dram_tensor("s_hh", [DFF, T], BF16)
    pT_t = nc.dram_tensor("s_pT", [DEPTH * DPLE, T], BF16)
    trig_t = nc.dram_tensor("s_trig", [4 * 64, T], F32)
    xT, zT, qn, qr = xT_t.ap(), zT_t.ap(), qn_t.ap(), qr_t.ap()
    ckv_own = [t_.ap() for t_ in ckv_own_ts]
    ckv_all = [t_.ap() for t_ in ckv_all_ts]
    xtail_own, xtail_all, bb = xtail_own_t.ap(), xtail_all_t.ap(), bb_t.ap()
    hfin_own, hfin_all, ma, hh, pT, trig = (hfin_own_t.ap(), hfin_all_t.ap(), ma_t.ap(),
                                            hh_t.ap(), pT_t.ap(), trig_t.ap())
    kn = hh[0:4096, :].rearrange("(r two) n -> r (two n)", two=2)
    vv = hh[4096:8192, :].rearrange("(r f) n -> r (f n)", f=2048 // T)
    aa = zT[C_XB:C_XB + D, :]
    mg = qn
    dbg_outs = {}
    dbg_src = {"xT": (xT_t, [D, T], F32), "zT": (zT_t, [DIN + 64, T], F32), "qn": (qn_t, [NH * 128, T], BF16),
               "qr": (qr_t, [NH * 64, T], BF16),
               "bb": (bb_t, [D, T], F32), "ma": (ma_t, [D, T], F32),
               "hh": (hh_t, [DFF, T], BF16), "trig": (trig_t, [256, T], F32),
               "hfin_all": (hfin_all_t, [2 * D, 1], F32), "xtail_all": (xtail_all_t, [2 * D, 4], F32),
               "pT": (pT_t, [DEPTH * DPLE, T], BF16)}
    for nme in dbg:
        dbg_outs[nme] = nc.dram_tensor("dbg_" + nme, dbg_src[nme][1], dbg_src[nme][2], kind="ExternalOutput").ap()

    PAIRS = [[0, 1], [2, 3], [4, 5], [6, 7]]

    with ExitStack() as G:
        K = Trk(nc, G)
        PE, ACT, DVE, POOL, SP = K.pe, K.act, K.dve, K.pool, K.sp

        uid = [0]

        def sb(stack, name, shape, dt):
            uid[0] += 1
            return stack.enter_context(nc.sbuf_tensor("%s_u%d" % (name, uid[0]), shape, dt))

        B = {n: Buf(n) for n in ["xT", "zT", "qn", "qr", "ckv_own", "ckv_all", "kn", "vv", "xtail_own", "xtail_all",
                                 "aa", "bb", "hfin_own", "hfin_all", "ma", "mg", "hh", "pT", "trig", "out"]}

        vecs = sb(G, "vecs", [128, NV], F32)
        cst = sb(G, "cst", [128, 256], F32)
        ident_f = cst[:, 0:128]
        ones_f = cst[:, 128:256]
        cbf = sb(G, "cbf", [128, 256], BF16)
        ident_b = cbf[:, 0:128]
        ones_b = cbf[:, 128:256]
        gl = K.buf("gl", dma=True)
        K.dma(SP, gl.dsem, vecs[:], vecs_in[:, :], writes=[gl])
        K.dma(SP, gl.dsem, cst[:], cst_in[:, :], writes=[gl])
        K.op(DVE, lambda: nc.vector.tensor_copy(out=cbf[:], in_=cst[:]), reads=[gl], writes=[gl])

        def vcol(c):
            return vecs[:, c:c + 1]

        PS = Ring([Slot(G.enter_context(nc.psum_tensor("ps%d" % i, [128, 512], F32)), Buf("ps%d" % i))
                   for i in range(8)])
        WS = Ring([Slot(sb(G, "ws%d" % i, [128, 8192], BF16), K.buf("ws%d" % i, dma=True)) for i in range(3)])
        SF = Ring([Slot(sb(G, "sf%d" % i, [128, 512], F32), K.buf("sf%d" % i, dma=True)) for i in range(4)])
        SBF = Ring([Slot(sb(G, "sbf%d" % i, [128, 512], BF16), K.buf("sbf%d" % i, dma=True)) for i in range(4)])
        XO = Ring([Slot(sb(G, "xo%d" % i, [128, 512], F32), K.buf("xo%d" % i, dma=True)) for i in range(4)])
        evac_tog = [0]
        K.scoped = True

        def load_w(slot, src, KC, gcols):
            dst = slot.t[:, 0:KC * gcols].rearrange("p (c n) -> p c n", c=KC)
            K.dma(POOL, slot.dsem, dst, src.rearrange("(c p) n -> p c n", p=128), writes=[slot.buf],
                  max_dma_last_dim=4096)

        def mm_group(ps, msz, KC, lhs_fn, rhs_fn, reads, psb):
            def emit():
                last = None
                for c in range(KC):
                    last = nc.tensor.matmul(ps[0:msz, :], lhsT=lhs_fn(c), rhs=rhs_fn(c),
                                            start=(c == 0), stop=(c == KC - 1))
                return last
            K.op(PE, emit, reads=reads, writes=[psb])

        def linear(groups, KC, nlist, rhs_fn, act_bufs, evac, prefetch=None):
            tiles = []
            for gi, g in enumerate(groups):
                for blk in g[2]:
                    for n in nlist:
                        tiles.append((gi, blk, n))
            pres = {}
            PD = 2

            def do_pre(i):
                if prefetch is not None and i < len(tiles) and i not in pres:
                    pres[i] = prefetch(tiles[i][1][2], tiles[i][2])
            cur_g = -1
            slot = None
            extra = None
            for i, (gi, blk, n) in enumerate(tiles):
                if gi != cur_g:
                    cur_g = gi
                    src, gcols, _, post = groups[gi]
                    slot = WS.next()
                    load_w(slot, src, KC, gcols)
                    extra = post(slot, gcols) if post is not None else None
                for j in range(i, i + PD + 1):
                    do_pre(j)
                col0, msz, tag = blk
                gcols = groups[gi][1]
                ps = PS.next()
                if extra is not None and isinstance(tag, tuple) and tag[0] == "rot":
                    et, ebuf = extra
                    mm_group(ps.t, msz, KC, lambda c: et[:, c * 64:(c + 1) * 64], lambda c: rhs_fn(c, n),
                             [ebuf, act_bufs[n]], ps.buf)
                else:
                    mm_group(ps.t, msz, KC, lambda c: slot.t[:, c * gcols + col0:c * gcols + col0 + msz],
                             lambda c: rhs_fn(c, n), [slot.buf, act_bufs[n]], ps.buf)
                evac(tag, n, ps.t, ps.buf, msz, pres.pop(i, None))

        def store(stg, dst_ap, msz, dbuf):
            K.dma(SP, stg.dsem, dst_ap, stg.t[0:msz, :], reads=[stg.buf], pwrites=[dbuf])

        def nsl(n):
            return slice(n * 512, (n + 1) * 512)

        def rms_to_bf16(L, src_ap, src_buf, KC, gcol0, out_t, out_bufs, dim):
            W = 256 if KC > 4 else 512
            xin = [Slot(sb(L, "nx%d" % i, [128, KC * W], F32), K.buf("nx%d" % i, dma=True)) for i in range(2)]
            sq = [Slot(sb(L, "nsq%d" % i, [128, KC * W], F32), Buf("nsq%d" % i)) for i in range(1)]
            rs = [Slot(sb(L, "nrs%d" % i, [128, W], F32), Buf("nrs%d" % i)) for i in range(2)]
            for j in range(T // W):
                n = (j * W) // 512
                t0 = j * W
                xs = xin[j % 2]
                K.dma(SP, xs.dsem, xs.t[:].rearrange("p (c n) -> p c n", c=KC),
                      src_ap[:, t0:t0 + W].rearrange("(c p) n -> p c n", p=128), reads=[src_buf], writes=[xs.buf])
                s = sq[0]
                K.op(ACT, lambda: nc.scalar.activation(out=s.t[:], in_=xs.t[:], func=AF.Square),
                     reads=[xs.buf], writes=[s.buf])
                ps = PS.next()

                def emit():
                    last = None
                    for c in range(KC):
                        last = nc.tensor.matmul(ps.t[:, 0:W], lhsT=ones_f, rhs=s.t[:, c * W:(c + 1) * W],
                                                start=(c == 0), stop=(c == KC - 1))
                    return last
                K.op(PE, emit, reads=[s.buf, gl], writes=[ps.buf])
                r = rs[j % 2]
                K.op(ACT, lambda: nc.scalar.activation(out=r.t[:], in_=ps.t[:, 0:W], func=AF.Sqrt, scale=1.0 / dim,
                                                       bias=float(EPS)), reads=[ps.buf], writes=[r.buf])
                K.op(DVE, lambda: nc.vector.reciprocal(out=r.t[:], in_=r.t[:]), reads=[r.buf], writes=[r.buf])
                for c in range(KC):
                    K.op(DVE, lambda: nc.vector.scalar_tensor_tensor(
                        out=out_t[:, c * T + t0:c * T + t0 + W], in0=xs.t[:, c * W:(c + 1) * W],
                        scalar=vcol(gcol0 + c), in1=r.t[:], op0=ALU.mult, op1=ALU.mult),
                        reads=[xs.buf, r.buf, gl], pwrites=[out_bufs[n]])

        with ExitStack() as L:
            xl = [Slot(sb(L, "p0x%d" % i, [128, 4 * D], F32), K.buf("p0x%d" % i, dma=True)) for i in range(2)]
            for gq in range(NQ):
                s = xl[gq % 2]
                K.dma(SP, s.dsem, s.t[:].rearrange("p (a d) -> p a d", a=4),
                      x_in[gq * 512:(gq + 1) * 512, :].rearrange("(a p) d -> p a d", p=128), writes=[s.buf])
                for c in range(KD):
                    ps = PS.next()

                    def emit():
                        last = None
                        for a in range(4):
                            last = nc.tensor.transpose(ps.t[:, a * 128:(a + 1) * 128],
                                                       s.t[:, a * D + c * 128:a * D + (c + 1) * 128], ident_f)
                        return last
                    K.op(PE, emit, reads=[s.buf, gl], writes=[ps.buf])
                    st = SF.next()
                    evac_tog[0] ^= 1
                    if evac_tog[0]:
                        K.op(ACT, lambda: nc.scalar.copy(out=st.t[:], in_=ps.t[:]), reads=[ps.buf], writes=[st.buf])
                    else:
                        K.op(DVE, lambda: nc.vector.tensor_copy(out=st.t[:], in_=ps.t[:]), reads=[ps.buf],
                             writes=[st.buf])
                    store(st, xT[c * 128:(c + 1) * 128, nsl(gq)], 128, B["xT"])
            pl = [Slot(sb(L, "p0p%d" % i, [128, 4 * DPLE], F32), K.buf("p0p%d" % i, dma=True)) for i in range(2)]
            it = 0
            for l in range(DEPTH):
                for gq in range(NQ):
                    s = pl[it % 2]
                    it += 1
                    K.dma(SP, s.dsem, s.t[:].rearrange("p (a d) -> p a d", a=4),
                          p_in[l, gq * 512:(gq + 1) * 512, :].rearrange("(a p) d -> p a d", p=128), writes=[s.buf])
                    for c in range(2):
                        ps = PS.next()

                        def emit():
                            last = None
                            for a in range(4):
                                last = nc.tensor.transpose(ps.t[:, a * 128:(a + 1) * 128],
                                                           s.t[:, a * DPLE + c * 128:a * DPLE + (c + 1) * 128], ident_f)
                            return last
                        K.op(PE, emit, reads=[s.buf, gl], writes=[ps.buf])
                        st = SBF.next()
                        K.op(ACT, lambda: nc.scalar.copy(out=st.t[:], in_=ps.t[:]), reads=[ps.buf], writes=[st.buf])
                        store(st, pT[l * DPLE + c * 128:l * DPLE + (c + 1) * 128, nsl(gq)], 128, B["pT"])
        K.barrier()
        with ExitStack() as L:
            posi = Slot(sb(L, "posi", [64, T], I32), K.buf("posi", dma=True))
            K.dma(SP, posi.dsem, posi.t[:], pos_in[0, :].partition_broadcast(64), writes=[posi.buf])
            tb = Buf("trigb")
            ang = sb(L, "ang", [64, T], F32)
            nf = sb(L, "nf", [64, T], F32)
            ni = sb(L, "ni", [64, T], I32)
            rr = sb(L, "rr", [64, T], F32)
            tt = sb(L, "tt", [64, T], F32)
            tr = sb(L, "tr", [64, 4 * T], F32)
            K.op(DVE, lambda: nc.vector.tensor_copy(out=ang[:], in_=posi.t[:]), reads=[posi.buf], writes=[tb])
            K.op(DVE, lambda: nc.vector.tensor_scalar(out=ang[:], in0=ang[:], scalar1=vecs[0:64, V_INVF:V_INVF + 1],
                                                      scalar2=None, op0=ALU.mult), reads=[gl], writes=[tb])
            K.op(DVE, lambda: nc.vector.tensor_scalar(out=nf[:], in0=ang[:], scalar1=float(1.0 / (2 * np.pi)),
                                                      scalar2=None, op0=ALU.mult), writes=[tb])
            K.op(DVE, lambda: nc.vector.tensor_copy(out=ni[:], in_=nf[:]), writes=[tb])
            K.op(DVE, lambda: nc.vector.tensor_copy(out=nf[:], in_=ni[:]), writes=[tb])
            K.op(DVE, lambda: nc.vector.scalar_tensor_tensor(out=rr[:], in0=nf[:], scalar=-TWO_PI_HI, in1=ang[:],
                                                             op0=ALU.mult, op1=ALU.add), writes=[tb])
            K.op(DVE, lambda: nc.vector.scalar_tensor_tensor(out=rr[:], in0=nf[:], scalar=-TWO_PI_LO, in1=rr[:],
                                                             op0=ALU.mult, op1=ALU.add), writes=[tb])
            K.op(DVE, lambda: nc.vector.tensor_scalar(out=tt[:], in0=rr[:], scalar1=float(np.pi),
                                                      scalar2=float(-2 * np.pi), op0=ALU.is_gt, op1=ALU.mult),
                 writes=[tb])
            K.op(DVE, lambda: nc.vector.tensor_tensor(out=rr[:], in0=rr[:], in1=tt[:], op=ALU.add), writes=[tb])
            K.op(DVE, lambda: nc.vector.tensor_scalar(out=tt[:], in0=rr[:], scalar1=float(-np.pi),
                                                      scalar2=float(2 * np.pi), op0=ALU.is_lt, op1=ALU.mult),
                 writes=[tb])
            K.op(DVE, lambda: nc.vector.tensor_tensor(out=rr[:], in0=rr[:], in1=tt[:], op=ALU.add), writes=[tb])
            K.op(DVE, lambda: nc.vector.tensor_scalar(out=rr[:], in0=rr[:], scalar1=float(np.pi), scalar2=float(-np.pi),
                                                      op0=ALU.min, op1=ALU.max), writes=[tb])
            K.op(ACT, lambda: nc.scalar.activation(out=tt[:], in_=rr[:], func=AF.Abs), writes=[tb])
            K.op(DVE, lambda: nc.vector.tensor_scalar(out=tt[:], in0=tt[:], scalar1=-1.0, scalar2=float(np.pi / 2),
                                                      op0=ALU.mult, op1=ALU.add), writes=[tb])
            K.op(ACT, lambda: nc.scalar.activation(out=tr[:, T:2 * T], in_=rr[:], func=AF.Sin), reads=[tb], writes=[tb])
            K.op(ACT, lambda: nc.scalar.activation(out=tr[:, 0:T], in_=tt[:], func=AF.Sin), reads=[tb], writes=[tb])
            K.op(DVE, lambda: nc.vector.tensor_scalar(out=tr[:, 2 * T:4 * T], in0=tr[:, 0:2 * T], scalar1=SCALE,
                                                      scalar2=None, op0=ALU.mult), reads=[tb], writes=[tb])
            trd = K.sem("d_trig")
            K.dma(SP, trd, trig.rearrange("(a p) n -> p a n", p=64), tr[:].rearrange("p (a n) -> p a n", a=4),
                  reads=[tb], writes=[B["trig"]])
        K.barrier()

        for l in range(depth):
            vb = l * V_PER_LAYER
            with ExitStack() as L:
                uT = sb(L, "uT", [128, KD * T], BF16)
                ub = [Buf("ub%d" % n) for n in range(NQ)]
                with ExitStack() as L2:
                    rms_to_bf16(L2, xT, B["xT"], KD, vb + V_GMIX, uT, ub, D)
                K.barrier()
                segs = [(C_CQ, 512, "copy"), (C_CKV, 512, "copy"), (C_KPE, 64, "kpe"), (C_XB, 2048, "copy"),
                        (C_YB, 2048, "gelu"), (C_GA, 2048, "sig"), (C_GR, 2048, "sig")]
                groups = []
                for (c0, width, kind) in segs:
                    if kind == "kpe":
                        def post(slot, gcols):
                            rt = sb(L, "krotw", [128, KD * 64], BF16)
                            rb = Buf("krotw")
                            for c in range(KD):
                                K.op(DVE, lambda: nc.vector.tensor_scalar(
                                    out=rt[:, c * 64:c * 64 + 32], in0=slot.t[:, c * 64 + 32:c * 64 + 64],
                                    scalar1=-1.0, scalar2=None, op0=ALU.mult), reads=[slot.buf], pwrites=[rb])
                                K.op(DVE, lambda: nc.vector.tensor_copy(
                                    out=rt[:, c * 64 + 32:c * 64 + 64], in_=slot.t[:, c * 64:c * 64 + 32]),
                                    reads=[slot.buf], pwrites=[rb])
                            return rt, rb
                        groups.append((w_in[l, :, c0:c0 + 64], 64, [(0, 64, ("copy", c0)), (0, 64, ("rot", C_KROT))],
                                       post))
                    else:
                        for g0 in range(0, width, 512):
                            blocks = [(m0, 128, (kind, c0 + g0 + m0)) for m0 in range(0, 512, 128)]
                            groups.append((w_in[l, :, c0 + g0:c0 + g0 + 512], 512, blocks, None))

                def evacA(tag, n, ps, psb, msz, pre):
                    kind, row0 = tag
                    st = SF.next()
                    if kind in ("copy", "rot"):
                        evac_tog[0] ^= 1
                        if evac_tog[0]:
                            K.op(ACT, lambda: nc.scalar.copy(out=st.t[0:msz, :], in_=ps[0:msz, :]), reads=[psb],
                                 writes=[st.buf])
                        else:
                            K.op(DVE, lambda: nc.vector.tensor_copy(out=st.t[0:msz, :], in_=ps[0:msz, :]), reads=[psb],
                                 writes=[st.buf])
                    else:
                        f = AF.Gelu_apprx_tanh if kind == "gelu" else AF.Sigmoid
                        K.op(ACT, lambda: nc.scalar.activation(out=st.t[0:msz, :], in_=ps[0:msz, :], func=f),
                             reads=[psb], writes=[st.buf])
                    store(st, zT[row0:row0 + msz, nsl(n)], msz, B["zT"])
                linear(groups, KD, list(range(NQ)), lambda c, n: uT[:, c * T + n * 512:c * T + (n + 1) * 512], ub, evacA)
            K.barrier()

            with ExitStack() as L:
                cqT = sb(L, "cqT", [128, 4 * T], BF16)
                cqb = [Buf("cqb%d" % n) for n in range(NQ)]
                with ExitStack() as L2:
                    ckvT = sb(L2, "ckvT", [128, 4 * T], BF16)
                    ckb = [Buf("ckb%d" % n) for n in range(NQ)]
                    with ExitStack() as L3:
                        rms_to_bf16(L3, zT[C_CQ:C_CQ + 512, :], B["zT"], 4, vb + V_GQ, cqT, cqb, DQ)
                    K.barrier()
                    with ExitStack() as L3:
                        rms_to_bf16(L3, zT[C_CKV:C_CKV + 512, :], B["zT"], 4, vb + V_GKV, ckvT, ckb, DKV)
                    K.barrier()
                    d1 = K.sem("d_ckvst%d" % l)
                    for pc in range(NP):
                        K.dma(SP, d1, ckv_own[pc][0:512, :].rearrange("(c p) n -> p c n", p=128),
                              ckvT[:].rearrange("p (c n) -> p c n", c=4)[:, :, pc * CW:(pc + 1) * CW], reads=ckb,
                              pwrites=[B["ckv_own"]])
                    kraw = Slot(sb(L2, "kraw", [64, T], F32), K.buf("kraw%d" % l, dma=True))
                    krot = Slot(sb(L2, "krot", [64, T], F32), K.buf("krot%d" % l, dma=True))
                    cs = Slot(sb(L2, "cs", [64, 2 * T], F32), K.buf("cs%d" % l, dma=True))
                    kpe = Slot(sb(L2, "kpe", [64, T], BF16), K.buf("kpe%d" % l, dma=True))
                    K.dma(SP, kraw.dsem, kraw.t[:], zT[C_KPE:C_KPE + 64, :], reads=[B["zT"]], writes=[kraw.buf])
                    K.dma(SP, krot.dsem, krot.t[:], zT[C_KROT:C_KROT + 64, :], reads=[B["zT"]], writes=[krot.buf])
                    K.dma(SP, cs.dsem, cs.t[:].rearrange("p (a n) -> p a n", a=2),
                          trig[0:128, :].rearrange("(a p) n -> p a n", p=64), reads=[B["trig"]], writes=[cs.buf])
                    K.op(DVE, lambda: nc.vector.tensor_tensor(out=kraw.t[:], in0=kraw.t[:], in1=cs.t[:, 0:T],
                                                              op=ALU.mult), reads=[cs.buf], writes=[kraw.buf])
                    K.op(DVE, lambda: nc.vector.tensor_tensor(out=krot.t[:], in0=krot.t[:], in1=cs.t[:, T:2 * T],
                                                              op=ALU.mult), reads=[cs.buf], writes=[krot.buf])
                    K.op(DVE, lambda: nc.vector.tensor_tensor(out=kpe.t[:], in0=kraw.t[:], in1=krot.t[:], op=ALU.add),
                         reads=[kraw.buf, krot.buf], writes=[kpe.buf])
                    for pc in range(NP):
                        K.dma(SP, kpe.dsem, ckv_own[pc][512:576, :], kpe.t[:, pc * CW:(pc + 1) * CW], reads=[kpe.buf],
                              pwrites=[B["ckv_own"]])
                    d2 = K.sem("d_xtail%d" % l)
                    with nc.allow_non_contiguous_dma(reason="tiny conv tail"):
                        K.dma(SP, d2, xtail_own[:, 0:3], zT[C_XB:C_XB + D, T - 3:T], reads=[B["zT"]],
                              pwrites=[B["xtail_own"]])
                    for pc in range(NP):
                        K.cc(PAIRS, ckv_own_ts[pc], ckv_all_ts[pc], reads=[B["ckv_own"]],
                             writes=[B["ckv_all"]] if pc == 0 else (), pwrites=[B["ckv_all"]] if pc else ())
                    K.cc(PAIRS, xtail_own_t, xtail_all_t, reads=[B["xtail_own"]], writes=[B["xtail_all"]])
                K.barrier()
                with ExitStack() as L2:
                    css = Slot(sb(L2, "css", [64, 2 * T], F32), K.buf("css%d" % l, dma=True))
                    K.dma(SP, css.dsem, css.t[:].rearrange("p (a n) -> p a n", a=2),
                          trig[128:256, :].rearrange("(a p) n -> p a n", p=64), reads=[B["trig"]], writes=[css.buf])
                    qtmp = [Slot(sb(L2, "qtmp%d" % i, [64, 512], F32), Buf("qtmp%d" % i)) for i in range(2)]
                    qrw = [Slot(sb(L2, "qrw%d" % i, [128, 2 * 4 * 64], BF16), Buf("qrw%d" % i)) for i in range(2)]
                    for gi, hg in enumerate(range(0, NH, 2)):
                        gcols = 384
                        slot = WS.next()
                        load_w(slot, w_qb[l, :, hg * 192:(hg + 2) * 192], 4, gcols)
                        rw = qrw[gi % 2]
                        first = True
                        for hh_ in range(2):
                            for c in range(4):
                                base = c * gcols + hh_ * 192 + 128
                                ob = (hh_ * 4 + c) * 64
                                K.op(POOL, lambda: nc.gpsimd.tensor_scalar(
                                    out=rw.t[:, ob:ob + 32], in0=slot.t[:, base + 32:base + 64], scalar1=-1.0,
                                    scalar2=None, op0=ALU.mult), reads=[slot.buf],
                                    writes=[rw.buf] if first else (), pwrites=() if first else [rw.buf])
                                first = False
                                K.op(POOL, lambda: nc.gpsimd.tensor_copy(out=rw.t[:, ob + 32:ob + 64],
                                                                         in_=slot.t[:, base:base + 32]),
                                     reads=[slot.buf], pwrites=[rw.buf])
                        for hh_ in range(2):
                            h = hg + hh_
                            for n in range(NQ):
                                rhs = lambda c: cqT[:, c * T + n * 512:c * T + (n + 1) * 512]
                                col0 = hh_ * 192
                                ps = PS.next()
                                mm_group(ps.t, 128, 4, lambda c: slot.t[:, c * gcols + col0:c * gcols + col0 + 128],
                                         rhs, [slot.buf, cqb[n]], ps.buf)
                                st = SBF.next()
                                K.op(ACT, lambda: nc.scalar.activation(out=st.t[:], in_=ps.t[:], func=AF.Copy,
                                                                       scale=SCALE), reads=[ps.buf], writes=[st.buf])
                                store(st, qn[h * 128:(h + 1) * 128, nsl(n)], 128, B["qn"])
                                ps1 = PS.next()
                                mm_group(ps1.t, 64, 4,
                                         lambda c: slot.t[:, c * gcols + col0 + 128:c * gcols + col0 + 192],
                                         rhs, [slot.buf, cqb[n]], ps1.buf)
                                ps2 = PS.next()
                                mm_group(ps2.t, 64, 4, lambda c: rw.t[:, (hh_ * 4 + c) * 64:(hh_ * 4 + c + 1) * 64], rhs,
                                         [rw.buf, cqb[n]], ps2.buf)
                                qt = qtmp[0]
                                q2 = qtmp[1]
                                K.op(DVE, lambda: nc.vector.tensor_tensor(out=qt.t[:], in0=ps1.t[0:64, :],
                                                                          in1=css.t[:, n * 512:(n + 1) * 512],
                                                                          op=ALU.mult),
                                     reads=[ps1.buf, css.buf], writes=[qt.buf])
                                K.op(DVE, lambda: nc.vector.tensor_tensor(out=q2.t[:], in0=ps2.t[0:64, :],
                                                                          in1=css.t[:, T + n * 512:T + (n + 1) * 512],
                                                                          op=ALU.mult),
                                     reads=[ps2.buf, css.buf], writes=[q2.buf])
                                st = SBF.next()
                                K.op(DVE, lambda: nc.vector.tensor_tensor(out=st.t[0:64, :], in0=qt.t[:],
                                                                          in1=q2.t[:], op=ALU.add),
                                     reads=[qt.buf, q2.buf], writes=[st.buf])
                                store(st, qr[h * 64:(h + 1) * 64, nsl(n)], 64, B["qr"])
            K.barrier()

            with ExitStack() as L:
                NK = T2 // 512
                cka = Slot(sb(L, "cka", [128, 4 * T2], BF16), K.buf("cka%d" % l, dma=True))
                cka3 = cka.t[:].rearrange("p (c n) -> p c n", c=4)
                first = True
                for r in range(2):
                    for pc in range(NP):
                        K.dma(SP, cka.dsem, cka3[:, :, r * T + pc * CW:r * T + (pc + 1) * CW],
                              ckv_all[pc][r * 576:r * 576 + 512, :].rearrange("(c p) n -> p c n", p=128),
                              reads=[B["ckv_all"]], writes=[cka.buf] if first else (), pwrites=() if first else [cka.buf])
                        first = False
                wk = Slot(sb(L, "wk", [128, 4 * 2048], BF16), K.buf("wk%d" % l, dma=True))
                wv = Slot(sb(L, "wv", [128, 4 * 2048], BF16), K.buf("wv%d" % l, dma=True))
                wsrc = w_kvb[l].rearrange("(c p) (h two d) -> p c h two d", p=128, two=2, d=128)
                for c in range(4):
                    K.dma(POOL, wk.dsem, wk.t[:, c * 2048:(c + 1) * 2048].rearrange("p (h d) -> p h d", d=128),
                          wsrc[:, c, :, 0, :], writes=[wk.buf] if c == 0 else (), pwrites=[wk.buf] if c else ())
                    K.dma(POOL, wv.dsem, wv.t[:, c * 2048:(c + 1) * 2048].rearrange("p (h d) -> p h d", d=128),
                          wsrc[:, c, :, 1, :], writes=[wv.buf] if c == 0 else (), pwrites=[wv.buf] if c else ())
                for h in range(NH):
                    for kc in range(NK):
                        ps = PS.next()
                        mm_group(ps.t, 128, 4, lambda c: wk.t[:, c * 2048 + h * 128:c * 2048 + (h + 1) * 128],
                                 lambda c: cka.t[:, c * T2 + kc * 512:c * T2 + (kc + 1) * 512], [wk.buf, cka.buf], ps.buf)
                        st = SBF.next()
                        evac_tog[0] ^= 1
                        if evac_tog[0]:
                            K.op(ACT, lambda: nc.scalar.copy(out=st.t[:], in_=ps.t[:]), reads=[ps.buf], writes=[st.buf])
                        else:
                            K.op(DVE, lambda: nc.vector.tensor_copy(out=st.t[:], in_=ps.t[:]), reads=[ps.buf],
                                 writes=[st.buf])
                        store(st, kn[h * 128:(h + 1) * 128, kc * 512:(kc + 1) * 512], 128, B["kn"])
                for kt in range(T2 // 128):
                    for cb in range(4):
                        ps = PS.next()
                        mm_group(ps.t, 128, 4, lambda c: cka.t[:, c * T2 + kt * 128:c * T2 + (kt + 1) * 128],
                                 lambda c: wv.t[:, c * 2048 + cb * 512:c * 2048 + (cb + 1) * 512], [wv.buf, cka.buf],
                                 ps.buf)
                        st = SBF.next()
                        evac_tog[0] ^= 1
                        if evac_tog[0]:
                            K.op(ACT, lambda: nc.scalar.copy(out=st.t[:], in_=ps.t[:]), reads=[ps.buf], writes=[st.buf])
                        else:
                            K.op(DVE, lambda: nc.vector.tensor_copy(out=st.t[:], in_=ps.t[:]), reads=[ps.buf],
                                 writes=[st.buf])
                        store(st, vv[kt * 128:(kt + 1) * 128, cb * 512:(cb + 1) * 512], 128, B["vv"])
            K.barrier()

            with ExitStack() as L:
                wab = Slot(sb(L, "wab", [128, NH * 128], BF16), K.buf("wab%d" % l, dma=True))
                wxb = Slot(sb(L, "wxb", [128, NH * 128], BF16), K.buf("wxb%d" % l, dma=True))
                K.dma(POOL, wab.dsem, wab.t[:].rearrange("p (h j) -> p h j", h=NH), w_a[l].rearrange("h i j -> i h j"),
                      writes=[wab.buf])
                K.dma(POOL, wxb.dsem, wxb.t[:].rearrange("p (h j) -> p h j", h=NH), w_x[l].rearrange("h i j -> i h j"),
                      writes=[wxb.buf])
                lc = sb(L, "lc", [128, 16 * 6], F32)
                lb = Buf("lc")
                lam = vecs[:, vb + V_LAM:vb + V_LAM + 16]
                e_, s_, s2_, pl_, c_, c2_ = [lc[:, i * 16:(i + 1) * 16] for i in range(6)]
                K.op(ACT, lambda: nc.scalar.activation(out=e_, in_=lam, func=AF.Abs), reads=[gl], writes=[lb])
                K.op(ACT, lambda: nc.scalar.activation(out=e_, in_=e_, func=AF.Exp, scale=-1.0), writes=[lb])
                K.op(DVE, lambda: nc.vector.tensor_scalar(out=s_, in0=e_, scalar1=2.0, scalar2=None, op0=ALU.add),
                     writes=[lb])
                K.op(DVE, lambda: nc.vector.reciprocal(out=s_, in_=s_), writes=[lb])
                K.op(DVE, lambda: nc.vector.tensor_tensor(out=s_, in0=s_, in1=e_, op=ALU.mult), writes=[lb])
                K.op(DVE, lambda: nc.vector.tensor_tensor(out=s2_, in0=s_, in1=s_, op=ALU.mult), writes=[lb])
                K.op(DVE, lambda: nc.vector.tensor_scalar(out=pl_, in0=s2_, scalar1=1.0 / 9, scalar2=1.0 / 7,
                                                          op0=ALU.mult, op1=ALU.add), writes=[lb])
                for cf in (1.0 / 5, 1.0 / 3, 1.0):
                    K.op(DVE, lambda: nc.vector.tensor_tensor(out=pl_, in0=pl_, in1=s2_, op=ALU.mult), writes=[lb])
                    K.op(DVE, lambda: nc.vector.tensor_scalar(out=pl_, in0=pl_, scalar1=float(cf), scalar2=None,
                                                              op0=ALU.add), writes=[lb])
                K.op(DVE, lambda: nc.vector.tensor_tensor(out=pl_, in0=pl_, in1=s_, op=ALU.mult), writes=[lb])
                K.op(DVE, lambda: nc.vector.tensor_scalar(out=c_, in0=lam, scalar1=-1.0, scalar2=0.0, op0=ALU.mult,
                                                          op1=ALU.max), reads=[gl], writes=[lb])
                K.op(DVE, lambda: nc.vector.scalar_tensor_tensor(out=c_, in0=pl_, scalar=2.0, in1=c_, op0=ALU.mult,
                                                                 op1=ALU.add), writes=[lb])
                K.op(DVE, lambda: nc.vector.tensor_scalar(out=c_, in0=c_, scalar1=-8.0, scalar2=None, op0=ALU.mult),
                     writes=[lb])
                K.op(DVE, lambda: nc.vector.tensor_scalar(out=c2_, in0=c_, scalar1=2.0, scalar2=None, op0=ALU.mult),
                     writes=[lb])
                hf = Slot(sb(L, "hf", [128, 16], F32), K.buf("hf%d" % l, dma=True))
                xbp = [Slot(sb(L, "xbp%d" % i, [128, T + 4], F32), K.buf("xbp%d_%d" % (i, l), dma=True)) for i in range(2)]
                tl = [Slot(sb(L, "tl%d" % i, [128, 4], F32), K.buf("tl%d_%d" % (i, l), dma=True)) for i in range(2)]
                xcL = [Slot(sb(L, "xc%d" % i, [128, T], F32), Buf("xc%d" % i)) for i in range(2)]
                xcbL = [Slot(sb(L, "xcb%d" % i, [128, T], BF16), Buf("xcb%d" % i)) for i in range(2)]
                rtL = [Slot(sb(L, "rt%d" % i, [128, T], F32), Buf("rt%d" % i)) for i in range(2)]
                itL = [Slot(sb(L, "it%d" % i, [128, T], F32), Buf("it%d" % i)) for i in range(2)]
                atL = [Slot(sb(L, "at%d" % i, [128, T], F32), K.buf("at%d_%d" % (i, l), dma=True)) for i in range(2)]
                btL = [Slot(sb(L, "bt%d" % i, [128, T], F32), K.buf("bt%d_%d" % (i, l), dma=True)) for i in range(2)]
                hsL = [Slot(sb(L, "hs%d" % i, [128, T], F32), Buf("hs%d" % i)) for i in range(2)]
                for cg in range(NH):
                    xs = xbp[cg % 2]
                    tls = tl[cg % 2]
                    xc, xcb, rt_, it_ = xcL[cg % 2], xcbL[cg % 2], rtL[cg % 2], itL[cg % 2]
                    at_, bt_, hs = atL[cg % 2], btL[cg % 2], hsL[cg % 2]
                    def c1_load(g_):
                        xs_, tls_ = xbp[g_ % 2], tl[g_ % 2]
                        K.dma(SP, xs_.dsem, xs_.t[:, 4:4 + T], zT[C_XB + g_ * 128:C_XB + (g_ + 1) * 128, :],
                              reads=[B["zT"]], writes=[xs_.buf])
                        K.dma(SP, tls_.dsem, tls_.t[:, 0:4], xtail_all[g_ * 128:(g_ + 1) * 128, :],
                              reads=[B["xtail_all"]], writes=[tls_.buf])
                    if cg == 0:
                        c1_load(0)
                    if cg + 1 < NH:
                        c1_load(cg + 1)
                    K.op(DVE, lambda: nc.vector.tensor_scalar(out=xs.t[:, 1:4], in0=tls.t[:, 0:3], scalar1=vcol(V_FLAG),
                                                              scalar2=None, op0=ALU.mult),
                         reads=[tls.buf, gl], pwrites=[xs.buf])
                    cw = lambda j: vcol(vb + V_CONVW + j * 16 + cg)
                    K.op(DVE, lambda: nc.vector.tensor_scalar(out=xc.t[:], in0=xs.t[:, 4:4 + T], scalar1=cw(3),
                                                              scalar2=vcol(vb + V_CONVB + cg), op0=ALU.mult,
                                                              op1=ALU.add), reads=[xs.buf, gl], writes=[xc.buf])
                    for j in range(3):
                        K.op(DVE, lambda: nc.vector.scalar_tensor_tensor(out=xc.t[:], in0=xs.t[:, 1 + j:1 + j + T],
                                                                         scalar=cw(j), in1=xc.t[:], op0=ALU.mult,
                                                                         op1=ALU.add), reads=[xs.buf], writes=[xc.buf])
                    K.op(ACT, lambda: nc.scalar.copy(out=xcb.t[:], in_=xc.t[:]), reads=[xc.buf], writes=[xcb.buf])
                    for n in range(NQ):
                        ps = PS.next()
                        mm_group(ps.t, 128, 1, lambda c: wab.t[:, cg * 128:(cg + 1) * 128],
                                 lambda c: xcb.t[:, nsl(n)], [wab.buf, xcb.buf], ps.buf)
                        K.op(ACT, lambda: nc.scalar.activation(out=rt_.t[:, nsl(n)], in_=ps.t[:], func=AF.Sigmoid,
                                                               bias=vcol(vb + V_BA + cg)),
                             reads=[ps.buf, gl], writes=[rt_.buf] if n == 0 else (), pwrites=[rt_.buf] if n else ())
                        ps2 = PS.next()
                        mm_group(ps2.t, 128, 1, lambda c: wxb.t[:, cg * 128:(cg + 1) * 128],
                                 lambda c: xcb.t[:, nsl(n)], [wxb.buf, xcb.buf], ps2.buf)
                        K.op(ACT, lambda: nc.scalar.activation(out=it_.t[:, nsl(n)], in_=ps2.t[:], func=AF.Sigmoid,
                                                               bias=vcol(vb + V_BX + cg)),
                             reads=[ps2.buf, gl], writes=[it_.buf] if n == 0 else (), pwrites=[it_.buf] if n else ())
                    K.op(ACT, lambda: nc.scalar.activation(out=at_.t[:], in_=rt_.t[:], func=AF.Exp,
                                                           scale=lc[:, 64 + cg:65 + cg]),
                         reads=[rt_.buf, lb], writes=[at_.buf])
                    K.op(ACT, lambda: nc.scalar.activation(out=rt_.t[:], in_=rt_.t[:], func=AF.Exp,
                                                           scale=lc[:, 80 + cg:81 + cg]),
                         reads=[lb], writes=[rt_.buf])
                    K.op(ACT, lambda: nc.scalar.activation(out=rt_.t[:], in_=rt_.t[:], func=AF.Sqrt, scale=-1.0,
                                                           bias=1.0), writes=[rt_.buf])
                    K.op(DVE, lambda: nc.vector.tensor_tensor(out=it_.t[:], in0=it_.t[:], in1=xc.t[:], op=ALU.mult),
                         reads=[xc.buf], writes=[it_.buf])
                    K.op(DVE, lambda: nc.vector.tensor_tensor(out=bt_.t[:], in0=it_.t[:], in1=rt_.t[:], op=ALU.mult),
                         reads=[it_.buf, rt_.buf], writes=[bt_.buf])
                    K.dma(SP, at_.dsem, aa[cg * 128:(cg + 1) * 128, :], at_.t[:], reads=[at_.buf], pwrites=[B["aa"]])
                    K.dma(SP, bt_.dsem, bb[cg * 128:(cg + 1) * 128, :], bt_.t[:], reads=[bt_.buf], pwrites=[B["bb"]])
                    K.op(DVE, lambda: nc.vector.tensor_tensor_scan(out=hs.t[:], data0=at_.t[:], data1=bt_.t[:],
                                                                   initial=0.0, op0=ALU.mult, op1=ALU.add),
                         reads=[at_.buf, bt_.buf], writes=[hs.buf])
                    K.op(DVE, lambda: nc.vector.tensor_copy(out=hf.t[:, cg:cg + 1], in_=hs.t[:, T - 1:T]),
                         reads=[hs.buf], pwrites=[hf.buf])
                with nc.allow_non_contiguous_dma(reason="tiny state vector"):
                    K.dma(SP, hf.dsem, hfin_own.rearrange("(c p) o -> p (c o)", p=128), hf.t[:], reads=[hf.buf],
                          writes=[B["hfin_own"]])
                K.cc(PAIRS, hfin_own_t, hfin_all_t, reads=[B["hfin_own"]], writes=[B["hfin_all"]])
            K.barrier()

            with ExitStack() as L:
                kpeT = Slot(sb(L, "kpeT", [64, T2], BF16), K.buf("kpeT%d" % l, dma=True))
                masks = sb(L, "masks", [128, 8, 512], BF16)
                mkb = K.buf("masks%d" % l, dma=True)
                K.dma(SP, mkb.dsem, masks[:], masks_in.rearrange("a d p n -> p (a d) n"), writes=[mkb])
                first = True
                for r in range(2):
                    for pc in range(NP):
                        K.dma(SP, kpeT.dsem, kpeT.t[:, r * T + pc * CW:r * T + (pc + 1) * CW],
                              ckv_all[pc][r * 576 + 512:r * 576 + 576, :],
                              reads=[B["ckv_all"]], writes=[kpeT.buf] if first else (), pwrites=() if first else [kpeT.buf])
                        first = False
                knT = [Slot(sb(L, "knT%d" % i, [128, T2], BF16), K.buf("knT%d_%d" % (i, l), dma=True)) for i in range(2)]
                vT = [Slot(sb(L, "vT%d" % i, [128, T2], BF16), K.buf("vT%d_%d" % (i, l), dma=True)) for i in range(2)]
                qnT = [Slot(sb(L, "qnT%d" % i, [128, 512], BF16), K.buf("qnT%d_%d" % (i, l), dma=True)) for i in range(2)]
                qrT = [Slot(sb(L, "qrT%d" % i, [64, 512], BF16), K.buf("qrT%d_%d" % (i, l), dma=True)) for i in range(2)]
                gaT = [Slot(sb(L, "gaT%d" % i, [128, 512], F32), K.buf("gaT%d_%d" % (i, l), dma=True)) for i in range(2)]
                pt = Ring([Slot(sb(L, "pt%d" % i, [128, 512], BF16), Buf("pt%d" % i)) for i in range(4)])
                rc = [Slot(sb(L, "rc%d" % i, [128, 512], F32), Buf("rc%d" % i)) for i in range(2)]
                def mk1(nm, dt=F32):
                    return Slot(sb(L, nm, [128, T], dt), K.buf("%s_%d" % (nm, l), dma=True))
                A_, B_, GY, SG, MA, HS = mk1("c2a"), mk1("c2b"), mk1("c2y"), mk1("c2s"), mk1("c2m"), mk1("c2h")
                MO = mk1("c2o", BF16)
                h0 = Slot(sb(L, "c2h0", [128, 2], F32), K.buf("c2h0_%d" % l, dma=True))

                def c2_load(cg):
                    rows = slice(cg * 128, (cg + 1) * 128)
                    K.dma(SP, A_.dsem, A_.t[:], aa[rows, :], reads=[B["aa"]], writes=[A_.buf])
                    K.dma(SP, B_.dsem, B_.t[:], bb[rows, :], reads=[B["bb"]], writes=[B_.buf])
                    K.dma(SP, GY.dsem, GY.t[:], zT[C_YB + cg * 128:C_YB + (cg + 1) * 128, :], reads=[B["zT"]],
                          writes=[GY.buf])
                    K.dma(SP, SG.dsem, SG.t[:], zT[C_GR + cg * 128:C_GR + (cg + 1) * 128, :], reads=[B["zT"]],
                          writes=[SG.buf])
                    K.dma(SP, MA.dsem, MA.t[:], ma[rows, :], reads=[B["ma"]], writes=[MA.buf])
                    K.dma(SP, h0.dsem, h0.t[:, 0:1], hfin_all[rows, :], reads=[B["hfin_all"]], writes=[h0.buf])

                def c2_compute(cg):
                    rows = slice(cg * 128, (cg + 1) * 128)
                    K.op(DVE, lambda: nc.vector.tensor_scalar(out=h0.t[:, 1:2], in0=h0.t[:, 0:1],
                                                              scalar1=vcol(V_FLAG), scalar2=None, op0=ALU.mult),
                         reads=[gl], writes=[h0.buf])
                    K.op(DVE, lambda: nc.vector.tensor_tensor_scan(out=HS.t[:], data0=A_.t[:], data1=B_.t[:],
                                                                   initial=h0.t[:, 1:2], op0=ALU.mult, op1=ALU.add),
                         reads=[A_.buf, B_.buf, h0.buf], writes=[HS.buf])
                    K.op(DVE, lambda: nc.vector.tensor_tensor(out=GY.t[:], in0=GY.t[:], in1=HS.t[:], op=ALU.mult),
                         reads=[HS.buf], writes=[GY.buf])
                    K.op(DVE, lambda: nc.vector.tensor_tensor(out=GY.t[:], in0=GY.t[:], in1=SG.t[:], op=ALU.mult),
                         reads=[SG.buf], writes=[GY.buf])
                    K.op(DVE, lambda: nc.vector.tensor_tensor(out=MO.t[:], in0=GY.t[:], in1=MA.t[:], op=ALU.add),
                         reads=[GY.buf, MA.buf], writes=[MO.buf])
                    K.dma(SP, MO.dsem, mg[rows, :], MO.t[:], reads=[MO.buf], pwrites=[B["mg"]])
                NKT = T // 128
                it = 0
                PSA = Ring(PS.slots[0:4])
                PSS = Ring(PS.slots[4:8])
                def load_kv(h):
                    ks, vs = knT[h % 2], vT[h % 2]
                    K.dma(SP, ks.dsem, ks.t[:], kn[h * 128:(h + 1) * 128, :], reads=[B["kn"]], writes=[ks.buf])
                    K.dma(POOL, vs.dsem, vs.t[:].rearrange("p (k d) -> p k d", d=128),
                          vv[:, h * 128:(h + 1) * 128].rearrange("(k p) d -> p k d", p=128), reads=[B["vv"]],
                          writes=[vs.buf])

                def load_q(it_):
                    h, qc = it_ // NQ, it_ % NQ
                    qs, qrs, gs = qnT[it_ % 2], qrT[it_ % 2], gaT[it_ % 2]
                    K.dma(SP, qs.dsem, qs.t[:], qn[h * 128:(h + 1) * 128, nsl(qc)], reads=[B["qn"]], writes=[qs.buf])
                    K.dma(SP, qrs.dsem, qrs.t[:], qr[h * 64:(h + 1) * 64, nsl(qc)], reads=[B["qr"]], writes=[qrs.buf])
                    K.dma(SP, gs.dsem, gs.t[:], zT[C_GA + h * 128:C_GA + (h + 1) * 128, nsl(qc)], reads=[B["zT"]],
                          writes=[gs.buf])
                load_kv(0)
                load_q(0)
                for h in range(NH + 1):
                    if h >= 1:
                        c2_load(h - 1)
                    if h == NH:
                        c2_compute(h - 1)
                        break
                    ks, vs = knT[h % 2], vT[h % 2]
                    for qc in range(NQ):
                        qs, qrs, gs = qnT[it % 2], qrT[it % 2], gaT[it % 2]
                        rcs = rc[it % 2]
                        if qc == NQ - 1 and h >= 1:
                            c2_compute(h - 1)
                        if qc == 0 and h + 1 < NH:
                            load_kv(h + 1)
                        if it + 1 < NH * NQ:
                            load_q(it + 1)
                        it += 1
                        tiles = []
                        for kt in range(NKT):
                            d = kt - 4 * qc
                            if d < 0:
                                tiles.append((kt, None, None))
                            elif d < 4:
                                tiles.append((kt, d, None))
                            else:
                                tiles.append((kt, None, V_B0))
                        for kt in range(min(NKT, 4 * qc + 4)):
                            d = kt - 4 * qc
                            tiles.append((NKT + kt, (4 + d) if d >= 0 else None, V_B1))
                        ops_ = PSA.next()
                        sps_ = PSA.next()
                        pend = []

                        def qk(ti):
                            kt, dm, bcol = tiles[ti]
                            ps = PSS.next()

                            def emit():
                                nc.tensor.matmul(ps.t[:], lhsT=ks.t[:, kt * 128:(kt + 1) * 128], rhs=qs.t[:], start=True,
                                                 stop=False)
                                last = nc.tensor.matmul(ps.t[:], lhsT=kpeT.t[:, kt * 128:(kt + 1) * 128], rhs=qrs.t[:],
                                                        start=False, stop=(dm is None))
                                if dm is not None:
                                    last = nc.tensor.matmul(ps.t[:], lhsT=ident_b, rhs=masks[:, dm, :], start=False,
                                                            stop=True)
                                return last
                            K.op(PE, emit, reads=[ks.buf, kpeT.buf, qs.buf, qrs.buf, gl, mkb], writes=[ps.buf])
                            p_ = pt.next()
                            if bcol is None:
                                K.op(ACT, lambda: nc.scalar.activation(out=p_.t[:], in_=ps.t[:], func=AF.Exp),
                                     reads=[ps.buf], writes=[p_.buf])
                            else:
                                K.op(ACT, lambda: nc.scalar.activation(out=p_.t[:], in_=ps.t[:], func=AF.Exp,
                                                                       bias=vcol(bcol)),
                                     reads=[ps.buf, gl], writes=[p_.buf])
                            return (kt, p_)

                        def pv(ti, kt, p_):
                            first, lastt = ti == 0, ti == len(tiles) - 1
                            K.op(PE, lambda: nc.tensor.matmul(ops_.t[:], lhsT=vs.t[:, kt * 128:(kt + 1) * 128], rhs=p_.t[:],
                                                              start=first, stop=lastt),
                                 reads=[vs.buf, p_.buf], writes=[ops_.buf] if first else (),
                                 pwrites=() if first else [ops_.buf])
                            K.op(PE, lambda: nc.tensor.matmul(sps_.t[:], lhsT=ones_b, rhs=p_.t[:], start=first, stop=lastt),
                                 reads=[p_.buf, gl], writes=[sps_.buf] if first else (),
                                 pwrites=() if first else [sps_.buf])
                        LOOK = 3
                        for ti in range(len(tiles)):
                            if len(pend) == LOOK:
                                K._waits(PE, [pend[0][2].buf], (), ())
                            pend.append((ti,) + qk(ti))
                            if len(pend) > LOOK:
                                a = pend.pop(0)
                                pv(*a)
                        for a in pend:
                            pv(*a)
                        K.op(DVE, lambda: nc.vector.reciprocal(out=rcs.t[:], in_=sps_.t[:]), reads=[sps_.buf],
                             writes=[rcs.buf])
                        K.op(DVE, lambda: nc.vector.tensor_tensor(out=rcs.t[:], in0=ops_.t[:], in1=rcs.t[:], op=ALU.mult),
                             reads=[ops_.buf], writes=[rcs.buf])
                        st = SF.next()
                        K.op(DVE, lambda: nc.vector.tensor_tensor(out=st.t[:], in0=rcs.t[:], in1=gs.t[:], op=ALU.mult),
                             reads=[rcs.buf, gs.buf], writes=[st.buf])
                        store(st, ma[h * 128:(h + 1) * 128, nsl(qc)], 128, B["ma"])
            K.barrier()

            def prefetch_x(tag, n):
                xo = XO.next()
                K.dma(SP, xo.dsem, xo.t[:], xT[tag * 128:(tag + 1) * 128, nsl(n)], reads=[B["xT"]], writes=[xo.buf])
                return xo

            def evac_resid(tag, n, ps, psb, msz, xo):
                st = SF.next()
                K.op(DVE, lambda: nc.vector.tensor_tensor(out=st.t[:], in0=ps[:], in1=xo.t[:], op=ALU.add),
                     reads=[psb, xo.buf], writes=[st.buf])
                store(st, xT[tag * 128:(tag + 1) * 128, nsl(n)], 128, B["xT"])

            with ExitStack() as L:
                mgs = Slot(sb(L, "mgs", [128, KD * T], BF16), K.buf("mgs%d" % l, dma=True))
                K.dma(SP, mgs.dsem, mgs.t[:].rearrange("p (c n) -> p c n", c=KD), mg.rearrange("(c p) n -> p c n", p=128),
                      reads=[B["mg"]], writes=[mgs.buf])
                groups = [(w_out[l, :, g0:g0 + 512], 512, [(m0, 128, (g0 + m0) // 128) for m0 in range(0, 512, 128)], None)
                          for g0 in range(0, D, 512)]
                linear(groups, KD, list(range(NQ)), lambda c, n: mgs.t[:, c * T + n * 512:c * T + (n + 1) * 512],
                       [mgs.buf] * NQ, evac_resid, prefetch_x)
            K.barrier()

            with ExitStack() as L:
                uT = sb(L, "uT", [128, KD * T], BF16)
                ub = [Buf("ub%d" % n) for n in range(NQ)]
                with ExitStack() as L2:
                    rms_to_bf16(L2, xT, B["xT"], KD, vb + V_GMLP, uT, ub, D)
                K.barrier()
                groups = [(w_up[l, :, g0:g0 + 512], 512, [(m0, 128, (g0 + m0) // 128) for m0 in range(0, 512, 128)], None)
                          for g0 in range(0, DFF, 512)]
                rl = Ring([Slot(sb(L, "rl%d" % i, [128, 512], F32), Buf("rl%d" % i)) for i in range(3)])

                def evacF1(tag, n, ps, psb, msz, pre):
                    r = rl.next()
                    K.op(ACT, lambda: nc.scalar.activation(out=r.t[:], in_=ps[:], func=AF.Relu), reads=[psb],
                         writes=[r.buf])
                    st = SBF.next()
                    K.op(DVE, lambda: nc.vector.tensor_tensor(out=st.t[:], in0=r.t[:], in1=r.t[:], op=ALU.mult),
                         reads=[r.buf], writes=[st.buf])
                    store(st, hh[tag * 128:(tag + 1) * 128, nsl(n)], 128, B["hh"])
                linear(groups, KD, list(range(NQ)), lambda c, n: uT[:, c * T + n * 512:c * T + (n + 1) * 512], ub, evacF1)
            K.barrier()

            with ExitStack() as L:
                KF = DFF // 128
                G2 = 2 if NQ >= 2 else 1
                hT = Slot(sb(L, "hT", [128, KF * 512 * G2], BF16), K.buf("hT%d" % l, dma=True))
                KH = KF // 2
                for n0 in range(0, NQ, G2):
                    K.dma(SP, hT.dsem, hT.t[:].rearrange("p (c n) -> p c n", c=KF),
                          hh[:, n0 * 512:(n0 + G2) * 512].rearrange("(c p) n -> p c n", p=128), reads=[B["hh"]],
                          writes=[hT.buf])
                    for g in range(D // 256):
                        sA, sB = WS.next(), WS.next()
                        load_w(sA, w_down[l, 0:KH * 128, g * 256:(g + 1) * 256], KH, 256)
                        load_w(sB, w_down[l, KH * 128:KF * 128, g * 256:(g + 1) * 256], KH, 256)
                        todo = [(mb, n_) for mb in range(2) for n_ in range(n0, n0 + G2)]
                        xos = [prefetch_x(g * 2 + mb, n_) for (mb, n_) in todo]
                        pss = [PS.next() for _ in todo]
                        for half, sl_ in ((0, sA), (1, sB)):
                            for (mb, n_), ps in zip(todo, pss):
                                def emit():
                                    last = None
                                    for cc_ in range(KH):
                                        c = half * KH + cc_
                                        last = nc.tensor.matmul(
                                            ps.t[:], lhsT=sl_.t[:, cc_ * 256 + mb * 128:cc_ * 256 + (mb + 1) * 128],
                                            rhs=hT.t[:, c * 512 * G2 + (n_ - n0) * 512:c * 512 * G2 + (n_ - n0 + 1) * 512],
                                            start=(c == 0), stop=(c == KF - 1))
                                    return last
                                K.op(PE, emit, reads=[sl_.buf, hT.buf], writes=[ps.buf] if half == 0 else (),
                                     pwrites=() if half == 0 else [ps.buf])
                        for (mb, n_), ps, xo in zip(todo, pss, xos):
                            evac_resid(g * 2 + mb, n_, ps.t, ps.buf, 128, xo)
            K.barrier()

            with ExitStack() as L:
                uT = sb(L, "uT", [128, KD * T], BF16)
                ub = [Buf("ub%d" % n) for n in range(NQ)]
                with ExitStack() as L2:
                    rms_to_bf16(L2, xT, B["xT"], KD, vb + V_GPLE, uT, ub, D)
                K.barrier()
                pts = Slot(sb(L, "pts", [128, 2 * T], BF16), K.buf("pts%d" % l, dma=True))
                K.dma(SP, pts.dsem, pts.t[:].rearrange("p (c n) -> p c n", c=2),
                      pT[l * DPLE:(l + 1) * DPLE, :].rearrange("(c p) n -> p c n", p=128), reads=[B["pT"]],
                      writes=[pts.buf])
                wpe = Slot(sb(L, "wpe", [128, 2 * D], BF16), K.buf("wpe%d" % l, dma=True))
                for c in range(2):
                    K.dma(POOL, wpe.dsem, wpe.t[:, c * D:(c + 1) * D], w_pe[l, c * 128:(c + 1) * 128, :],
                          writes=[wpe.buf] if c == 0 else (), pwrites=[wpe.buf] if c else (), max_dma_last_dim=4096)
                groups = [(w_pg[l, :, g0:g0 + 512], 512, [(m0, 128, (g0 + m0) // 128) for m0 in range(0, 512, 128)], None)
                          for g0 in range(0, D, 512)]
                sg = Ring([Slot(sb(L, "sg%d" % i, [128, 512], F32), Buf("sg%d" % i)) for i in range(3)])

                def evacG(tag, n, ps, psb, msz, xo):
                    ps2 = PS.next()
                    mm_group(ps2.t, 128, 2, lambda c: wpe.t[:, c * D + tag * 128:c * D + (tag + 1) * 128],
                             lambda c: pts.t[:, c * T + n * 512:c * T + (n + 1) * 512], [wpe.buf, pts.buf], ps2.buf)
                    s = sg.next()
                    K.op(ACT, lambda: nc.scalar.activation(out=s.t[:], in_=ps[:], func=AF.Sigmoid), reads=[psb],
                         writes=[s.buf])
                    K.op(DVE, lambda: nc.vector.tensor_tensor(out=s.t[:], in0=s.t[:], in1=ps2.t[:], op=ALU.mult),
                         reads=[ps2.buf], writes=[s.buf])
                    st = SF.next()
                    K.op(DVE, lambda: nc.vector.tensor_tensor(out=st.t[:], in0=s.t[:], in1=xo.t[:], op=ALU.add),
                         reads=[s.buf, xo.buf], writes=[st.buf])
                    store(st, xT[tag * 128:(tag + 1) * 128, nsl(n)], 128, B["xT"])
                linear(groups, KD, list(range(NQ)), lambda c, n: uT[:, c * T + n * 512:c * T + (n + 1) * 512], ub, evacG,
                       prefetch_x)
            K.barrier()

        with ExitStack() as L:
            xin = [Slot(sb(L, "fx%d" % i, [128, KD * 512], F32), K.buf("fx%d" % i, dma=True)) for i in range(2)]
            sq = Slot(sb(L, "fsq", [128, KD * 512], F32), Buf("fsq"))
            rs = Slot(sb(L, "frs", [128, 512], F32), Buf("frs"))
            yo = [Slot(sb(L, "fy%d" % i, [128, D], F32), K.buf("fy%d" % i, dma=True)) for i in range(2)]
            it = 0
            for n in range(NQ):
                xs = xin[n % 2]
                K.dma(SP, xs.dsem, xs.t[:].rearrange("p (c n) -> p c n", c=KD),
                      xT[:, nsl(n)].rearrange("(c p) n -> p c n", p=128), reads=[B["xT"]], writes=[xs.buf])
                K.op(ACT, lambda: nc.scalar.activation(out=sq.t[:], in_=xs.t[:], func=AF.Square), reads=[xs.buf],
                     writes=[sq.buf])
                ps = PS.next()
                mm_group(ps.t, 128, KD, lambda c: ones_f, lambda c: sq.t[:, c * 512:(c + 1) * 512], [sq.buf, gl], ps.buf)
                K.op(ACT, lambda: nc.scalar.activation(out=rs.t[:], in_=ps.t[:], func=AF.Sqrt, scale=1.0 / D,
                                                       bias=float(EPS)), reads=[ps.buf], writes=[rs.buf])
                K.op(DVE, lambda: nc.vector.reciprocal(out=rs.t[:], in_=rs.t[:]), reads=[rs.buf], writes=[rs.buf])
                for c in range(KD):
                    K.op(DVE, lambda: nc.vector.scalar_tensor_tensor(
                        out=xs.t[:, c * 512:(c + 1) * 512], in0=xs.t[:, c * 512:(c + 1) * 512],
                        scalar=vcol(V_GFINAL + c), in1=rs.t[:], op0=ALU.mult, op1=ALU.mult),
                        reads=[rs.buf, gl], writes=[xs.buf])
                for a in range(4):
                    y = yo[it % 2]
                    it += 1
                    for cb in range(4):
                        ps = PS.next()

                        def emit():
                            last = None
                            for cc_ in range(4):
                                c = cb * 4 + cc_
                                last = nc.tensor.transpose(ps.t[:, cc_ * 128:(cc_ + 1) * 128],
                                                           xs.t[:, c * 512 + a * 128:c * 512 + (a + 1) * 128], ident_f)
                            return last
                        K.op(PE, emit, reads=[xs.buf, gl], writes=[ps.buf])
                        evac_tog[0] ^= 1
                        if evac_tog[0]:
                            K.op(ACT, lambda: nc.scalar.copy(out=y.t[:, cb * 512:(cb + 1) * 512], in_=ps.t[:]),
                                 reads=[ps.buf], writes=[y.buf] if cb == 0 else (), pwrites=[y.buf] if cb else ())
                        else:
                            K.op(DVE, lambda: nc.vector.tensor_copy(out=y.t[:, cb * 512:(cb + 1) * 512], in_=ps.t[:]),
                                 reads=[ps.buf], writes=[y.buf] if cb == 0 else (), pwrites=[y.buf] if cb else ())
                    r0 = n * 512 + a * 128
                    K.dma(SP, y.dsem, y_out[r0:r0 + 128, :], y.t[:], reads=[y.buf], pwrites=[B["out"]])
            for nme, ap in dbg_outs.items():
                ds = K.sem("d_dbg_" + nme)
                src = dbg_src[nme][0].ap()
                rows = dbg_src[nme][1][0]
                step = 1024
                for r0 in range(0, rows, step):
                    r1 = min(rows, r0 + step)
                    K.dma(SP, ds, ap[r0:r1, :], src[r0:r1, :], reads=[B[nme]] if nme in B else [], pwrites=[B["out"]])
        K.final_wait(SP)
        K.final_wait(POOL)
    return nc


def _diag_masks():
    k = np.arange(128)[:, None]
    q = np.arange(512)[None, :]
    pat = np.zeros((4, 128, 512), np.float32)
    for d in range(4):
        pat[d] = np.where(128 * d + k <= q, 0.0, NEG)
    return pat


def make_inputs(T, x, p, positions, g_mix, w_in, g_q, w_qb, g_kv, w_kvb, conv_w, conv_b, w_a, b_a, w_x, b_x,
                lru_lambda, w_out, g_mlp, w_up, w_down, g_ple, w_ple_gate, w_ple_proj, g_final, n_cores=8):
    f = lambda a: np.ascontiguousarray(np.asarray(a, dtype=np.float32))
    x, p = f(x), f(p)
    positions = np.asarray(positions).astype(np.int32)

    def cols(v):
        v = f(v)
        return v.reshape(-1, 128).T

    pat = _diag_masks()
    cst = np.concatenate([np.eye(128, dtype=np.float32), np.ones((128, 128), np.float32)], axis=1)
    inv_freq = np.power(np.float32(10000.0), -np.arange(32, dtype=np.float32) * np.float32(2.0 / 64)).astype(np.float32)
    shared = dict(w_in=f(w_in), w_qb=f(w_qb), w_kvb=f(w_kvb), w_a=f(w_a), w_x=f(w_x), w_out=f(w_out), w_up=f(w_up),
                  w_down=f(w_down), w_ple_gate=f(w_ple_gate), w_ple_proj=f(w_ple_proj), cst=cst)
    in_maps = []
    for c in range(n_cores):
        b, half = c // 2, c % 2
        vecs = np.zeros((128, NV), np.float32)
        for l in range(DEPTH):
            vb = l * V_PER_LAYER
            vecs[:, vb + V_GMIX:vb + V_GMIX + 16] = cols(g_mix[l])
            vecs[:, vb + V_GMLP:vb + V_GMLP + 16] = cols(g_mlp[l])
            vecs[:, vb + V_GPLE:vb + V_GPLE + 16] = cols(g_ple[l])
            vecs[:, vb + V_GQ:vb + V_GQ + 4] = cols(g_q[l])
            vecs[:, vb + V_GKV:vb + V_GKV + 4] = cols(g_kv[l])
            for j in range(4):
                vecs[:, vb + V_CONVW + j * 16:vb + V_CONVW + (j + 1) * 16] = cols(np.asarray(conv_w)[l, j])
            vecs[:, vb + V_CONVB:vb + V_CONVB + 16] = cols(conv_b[l])
            vecs[:, vb + V_BA:vb + V_BA + 16] = cols(b_a[l])
            vecs[:, vb + V_BX:vb + V_BX + 16] = cols(b_x[l])
            vecs[:, vb + V_LAM:vb + V_LAM + 16] = cols(lru_lambda[l])
        vecs[:, V_GFINAL:V_GFINAL + 16] = cols(g_final)
        vecs[0:64, V_INVF] = np.concatenate([inv_freq, inv_freq])
        vecs[:, V_FLAG] = float(half)
        vecs[:, V_B0] = 0.0 if half else NEG
        vecs[:, V_B1] = 0.0 if half else NEG
        masks = np.zeros((2, 4, 128, 512), np.float32)
        if half == 0:
            masks[0] = pat
        else:
            masks[1] = pat
        m = dict(shared)
        m["x"] = np.ascontiguousarray(x[b, half * T:(half + 1) * T, :])
        m["p"] = np.ascontiguousarray(p[:, b, half * T:(half + 1) * T, :])
        m["pos"] = np.ascontiguousarray(positions[b, half * T:(half + 1) * T].reshape(1, T))
        m["vecs"] = vecs
        m["masks"] = masks.astype(ml_dtypes.bfloat16)
        in_maps.append(m)
    return in_maps


_PROG = {}


def run(T, inputs, dbg=(), depth=DEPTH):
    key = (T, tuple(dbg), depth)
    if key not in _PROG:
        _PROG[key] = build_program(T, dbg, depth)
    nc = _PROG[key]
    in_maps = make_inputs(T, **inputs)
    res = run_bass_kernel_spmd(nc, in_maps, core_ids=list(range(8)))
    return res.results


def kernel(**inputs):
    x = np.asarray(inputs["x"])
    Bn, S, _ = x.shape
    T = S // 2
    R = run(T, inputs)
    out = np.zeros((Bn, S, D), np.float32)
    for c in range(8):
        b, half = c // 2, c % 2
        out[b, half * T:(half + 1) * T, :] = R[c]["out"]
    return out
```

```python
import numpy as np
from contextlib import ExitStack
import concourse.bass as bass
import concourse.mybir as mybir
from concourse.bass_utils import run_bass_kernel_spmd
import ml_dtypes

F32, BF16, I32 = mybir.dt.float32, mybir.dt.bfloat16, mybir.dt.int32
AF = mybir.ActivationFunctionType
ALU = mybir.AluOpType

D = 2048
KD = D // 128
NH = 16
DQ = 512
DKV = 512
DR = 64
DIN = 9280
DFF = 8192
DPLE = 256
DEPTH = 2
EPS = 1e-6
C_CQ, C_CKV, C_KPE, C_XB, C_YB, C_GA, C_GR = 0, 512, 1024, 1088, 3136, 5184, 7232
C_KROT = DIN
NEG = -30000.0
SCALE = float((128 + 64) ** -0.5)
TWO_PI_HI = 6.28125
TWO_PI_LO = float(2.0 * np.pi - 6.28125)

V_GMIX, V_GMLP, V_GPLE, V_GQ, V_GKV, V_CONVW, V_CONVB, V_BA, V_BX, V_LAM = 0, 16, 32, 48, 52, 56, 120, 136, 152, 168
V_PER_LAYER = 184
V_GFINAL = 2 * V_PER_LAYER
V_INVF = V_GFINAL + 16
V_FLAG = V_INVF + 1
V_B0 = V_FLAG + 1
V_B1 = V_B0 + 1
NV = V_B1 + 1


class Sem:
    def __init__(self, h, name):
        self.h = h
        self.name = name
        self.cnt = 0


class Buf:
    def __init__(self, name, dsem=None):
        self.name = name
        self.w = {}
        self.r = {}
        self.dsem = dsem


class Eng:
    def __init__(self, name, h, sem):
        self.name = name
        self.h = h
        self.sem = sem
        self.seen = {}


class Trk:
    def __init__(self, nc, stack):
        self.nc = nc
        self.stack = stack
        self.allsems = []
        self.scoped = False
        self.free = []
        self.inuse = []
        self.pe = Eng("pe", nc.tensor, self.sem("e_pe"))
        self.act = Eng("act", nc.scalar, self.sem("e_act"))
        self.dve = Eng("dve", nc.vector, self.sem("e_dve"))
        self.pool = Eng("pool", nc.gpsimd, self.sem("e_pool"))
        self.sp = Eng("sp", nc.sync, None)
        self.engs = [self.pe, self.act, self.dve, self.pool, self.sp]
        self.ccsem = self.sem("e_cc")
        self.n_wait = 0

    def sem(self, name):
        if self.scoped and self.free:
            s = self.free.pop()
            self.inuse.append(s)
            return s
        s = Sem(self.stack.enter_context(self.nc.semaphore(name)), name)
        self.allsems.append(s)
        if self.scoped:
            self.inuse.append(s)
        return s

    def buf(self, name, dma=False):
        return Buf(name, self.sem("d_" + name) if dma else None)

    def _wait1(self, eng, sem, val):
        if sem is eng.sem and eng is self.pe:
            return
        if eng.seen.get(sem, 0) >= val:
            return
        eng.h.wait_ge(sem.h, val)
        eng.seen[sem] = val
        self.n_wait += 1

    def _waits(self, eng, reads, writes, pwrites):
        for b in reads:
            for s, v in b.w.items():
                self._wait1(eng, s, v)
        for b in writes:
            for s, v in b.w.items():
                self._wait1(eng, s, v)
            for s, v in b.r.items():
                self._wait1(eng, s, v)
        for b in pwrites:
            for s, v in b.r.items():
                self._wait1(eng, s, v)

    def _record(self, sem, val, reads, writes, pwrites):
        for b in reads:
            b.r[sem] = val
        for b in writes:
            b.w[sem] = val
        for b in pwrites:
            b.w[sem] = val

    def op(self, eng, emit, reads=(), writes=(), pwrites=()):
        self._waits(eng, reads, writes, pwrites)
        ins = emit()
        eng.sem.cnt += 1
        ins.then_inc(eng.sem.h, 1)
        self._record(eng.sem, eng.sem.cnt, reads, writes, pwrites)

    def dma(self, q, dsem, out, in_, reads=(), writes=(), pwrites=(), **kw):
        self._waits(q, reads, writes, pwrites)
        ins = q.h.dma_start(out=out, in_=in_, **kw)
        dsem.cnt += 16
        ins.then_inc(dsem.h, 16)
        self._record(dsem, dsem.cnt, reads, writes, pwrites)

    def cc(self, groups, in_t, out_t, reads=(), writes=(), pwrites=()):
        q = self.pool
        self._waits(q, reads, writes, pwrites)
        ins = self.nc.gpsimd.collective_compute("AllGather", ALU.bypass, replica_groups=groups,
                                                ins=[in_t.ap().opt()], outs=[out_t.ap().opt()])
        self.ccsem.cnt += 1
        ins.then_inc(self.ccsem.h)
        self._record(self.ccsem, self.ccsem.cnt, reads, writes, pwrites)

    def barrier(self):
        for e in self.engs:
            for s in self.allsems:
                if s.cnt > 0 and s is not self.ccsem:
                    self._wait1(e, s, s.cnt)
        self.free.extend(self.inuse)
        self.inuse = []

    def final_wait(self, eng):
        for s in self.allsems:
            if s.cnt > 0 and s is not eng.sem:
                if eng.seen.get(s, 0) < s.cnt:
                    eng.h.wait_ge(s.h, s.cnt)
                    eng.seen[s] = s.cnt


class Slot:
    def __init__(self, t, buf):
        self.t = t
        self.buf = buf

    @property
    def dsem(self):
        return self.buf.dsem


class Ring:
    def __init__(self, slots):
        self.slots = slots
        self.i = 0

    def next(self):
        s = self.slots[self.i % len(self.slots)]
        self.i += 1
        return s


def build_program(T, dbg=(), depth=DEPTH):
    NQ = T // 512
    NT = T // 128
    T2 = 2 * T
    nc = bass.Bass("TRN2", target_bir_lowering=False)
    x_in = nc.dram_tensor("x", [T, D], F32, kind="ExternalInput").ap()
    p_in = nc.dram_tensor("p", [DEPTH, T, DPLE], F32, kind="ExternalInput").ap()
    pos_in = nc.dram_tensor("pos", [1, T], I32, kind="ExternalInput").ap()
    vecs_in = nc.dram_tensor("vecs", [128, NV], F32, kind="ExternalInput").ap()
    masks_in = nc.dram_tensor("masks", [2, 4, 128, 512], BF16, kind="ExternalInput").ap()
    cst_in = nc.dram_tensor("cst", [128, 256], F32, kind="ExternalInput").ap()
    w_in = nc.dram_tensor("w_in", [DEPTH, D, DIN], F32, kind="ExternalInput").ap()
    w_qb = nc.dram_tensor("w_qb", [DEPTH, DQ, NH * 192], F32, kind="ExternalInput").ap()
    w_kvb = nc.dram_tensor("w_kvb", [DEPTH, DKV, NH * 256], F32, kind="ExternalInput").ap()
    w_a = nc.dram_tensor("w_a", [DEPTH, NH, 128, 128], F32, kind="ExternalInput").ap()
    w_x = nc.dram_tensor("w_x", [DEPTH, NH, 128, 128], F32, kind="ExternalInput").ap()
    w_out = nc.dram_tensor("w_out", [DEPTH, D, D], F32, kind="ExternalInput").ap()
    w_up = nc.dram_tensor("w_up", [DEPTH, D, DFF], F32, kind="ExternalInput").ap()
    w_down = nc.dram_tensor("w_down", [DEPTH, DFF, D], F32, kind="ExternalInput").ap()
    w_pg = nc.dram_tensor("w_ple_gate", [DEPTH, D, D], F32, kind="ExternalInput").ap()
    w_pe = nc.dram_tensor("w_ple_proj", [DEPTH, DPLE, D], F32, kind="ExternalInput").ap()
    y_out = nc.dram_tensor("out", [T, D], F32, kind="ExternalOutput").ap()
    xT_t = nc.dram_tensor("s_xT", [D, T], F32)
    zT_t = nc.dram_tensor("s_zT", [DIN + 64, T], F32)
    qn_t = nc.dram_tensor("s_qn", [NH * 128, T], BF16)
    qr_t = nc.dram_tensor("s_qr", [NH * 64, T], BF16)
    CW = min(T, 1024)
    NP = T // CW
    ckv_own_ts = [nc.dram_tensor("s_ckv_own%d" % i, [576, CW], BF16) for i in range(NP)]
    ckv_all_ts = [nc.dram_tensor("s_ckv_all%d" % i, [2 * 576, CW], BF16) for i in range(NP)]
    xtail_own_t = nc.dram_tensor("s_xtail_own", [D, 4], F32)
    xtail_all_t = nc.dram_tensor("s_xtail_all", [2 * D, 4], F32)
    bb_t = nc.dram_tensor("s_bb", [D, T], F32)
    hfin_own_t = nc.dram_tensor("s_hfin_own", [D, 1], F32)
    hfin_all_t = nc.dram_tensor("s_hfin_all", [2 * D, 1], F32)
    ma_t = nc.dram_tensor("s_ma", [D, T], F32)
    hh_t = nc.dram_tensor("s_hh", [DFF, T], BF16)
    pT_t = nc.dram_tensor("s_pT", [DEPTH * DPLE, T], BF16)
    trig_t = nc.dram_tensor("s_trig", [4 * 64, T], F32)
    xT, zT, qn, qr = xT_t.ap(), zT_t.ap(), qn_t.ap(), qr_t.ap()
    ckv_own = [t_.ap() for t_ in ckv_own_ts]
    ckv_all = [t_.ap() for t_ in ckv_all_ts]
    xtail_own, xtail_all, bb = xtail_own_t.ap(), xtail_all_t.ap(), bb_t.ap()
    hfin_own, hfin_all, ma, hh, pT, trig = (hfin_own_t.ap(), hfin_all_t.ap(), ma_t.ap(),
                                            hh_t.ap(), pT_t.ap(), trig_t.ap())
    kn = hh[0:4096, :].rearrange("(r two) n -> r (two n)", two=2)
    vv = hh[4096:8192, :].rearrange("(r f) n -> r (f n)", f=2048 // T)
    aa = zT[C_XB:C_XB + D, :]
    mg = qn
    dbg_outs = {}
    dbg_src = {"xT": (xT_t, [D, T], F32), "zT": (zT_t, [DIN + 64, T], F32), "qn": (qn_t, [NH * 128, T], BF16),
               "qr": (qr_t, [NH * 64, T], BF16),
               "bb": (bb_t, [D, T], F32), "ma": (ma_t, [D, T], F32),
               "hh": (hh_t, [DFF, T], BF16), "trig": (trig_t, [256, T], F32),
               "hfin_all": (hfin_all_t, [2 * D, 1], F32), "xtail_all": (xtail_all_t, [2 * D, 4], F32),
               "pT": (pT_t, [DEPTH * DPLE, T], BF16)}
    for nme in dbg:
        dbg_outs[nme] = nc.dram_tensor("dbg_" + nme, dbg_src[nme][1], dbg_src[nme][2], kind="ExternalOutput").ap()

    PAIRS = [[0, 1], [2, 3], [4, 5], [6, 7]]

    with ExitStack() as G:
        K = Trk(nc, G)
        PE, ACT, DVE, POOL, SP = K.pe, K.act, K.dve, K.pool, K.sp

        uid = [0]

        def sb(stack, name, shape, dt):
            uid[0] += 1
            return stack.enter_context(nc.sbuf_tensor("%s_u%d" % (name, uid[0]), shape, dt))

        B = {n: Buf(n) for n in ["xT", "zT", "qn", "qr", "ckv_own", "ckv_all", "kn", "vv", "xtail_own", "xtail_all",
                                 "aa", "bb", "hfin_own", "hfin_all", "ma", "mg", "hh", "pT", "trig", "out"]}

        vecs = sb(G, "vecs", [128, NV], F32)
        cst = sb(G, "cst", [128, 256], F32)
        ident_f = cst[:, 0:128]
        ones_f = cst[:, 128:256]
        cbf = sb(G, "cbf", [128, 256], BF16)
        ident_b = cbf[:, 0:128]
        ones_b = cbf[:, 128:256]
        gl = K.buf("gl", dma=True)
        K.dma(SP, gl.dsem, vecs[:], vecs_in[:, :], writes=[gl])
        K.dma(SP, gl.dsem, cst[:], cst_in[:, :], writes=[gl])
        K.op(DVE, lambda: nc.vector.tensor_copy(out=cbf[:], in_=cst[:]), reads=[gl], writes=[gl])

        def vcol(c):
            return vecs[:, c:c + 1]

        PS = Ring([Slot(G.enter_context(nc.psum_tensor("ps%d" % i, [128, 512], F32)), Buf("ps%d" % i))
                   for i in range(8)])
        WS = Ring([Slot(sb(G, "ws%d" % i, [128, 8192], BF16), K.buf("ws%d" % i, dma=True)) for i in range(3)])
        SF = Ring([Slot(sb(G, "sf%d" % i, [128, 512], F32), K.buf("sf%d" % i, dma=True)) for i in range(4)])
        SBF = Ring([Slot(sb(G, "sbf%d" % i, [128, 512], BF16), K.buf("sbf%d" % i, dma=True)) for i in range(4)])
        XO = Ring([Slot(sb(G, "xo%d" % i, [128, 512], F32), K.buf("xo%d" % i, dma=True)) for i in range(4)])
        evac_tog = [0]
        K.scoped = True

        def load_w(slot, src, KC, gcols):
            dst = slot.t[:, 0:KC * gcols].rearrange("p (c n) -> p c n", c=KC)
            K.dma(POOL, slot.dsem, dst, src.rearrange("(c p) n -> p c n", p=128), writes=[slot.buf],
                  max_dma_last_dim=4096)

        def mm_group(ps, msz, KC, lhs_fn, rhs_fn, reads, psb):
            def emit():
                last = None
                for c in range(KC):
                    last = nc.tensor.matmul(ps[0:msz, :], lhsT=lhs_fn(c), rhs=rhs_fn(c),
                                            start=(c == 0), stop=(c == KC - 1))
                return last
            K.op(PE, emit, reads=reads, writes=[psb])

        def linear(groups, KC, nlist, rhs_fn, act_bufs, evac, prefetch=None):
            tiles = []
            for gi, g in enumerate(groups):
                for blk in g[2]:
                    for n in nlist:
                        tiles.append((gi, blk, n))
            pres = {}
            PD = 2

            def do_pre(i):
                if prefetch is not None and i < len(tiles) and i not in pres:
                    pres[i] = prefetch(tiles[i][1][2], tiles[i][2])
            cur_g = -1
            slot = None
            extra = None
            for i, (gi, blk, n) in enumerate(tiles):
                if gi != cur_g:
                    cur_g = gi
                    src, gcols, _, post = groups[gi]
                    slot = WS.next()
                    load_w(slot, src, KC, gcols)
                    extra = post(slot, gcols) if post is not None else None
                for j in range(i, i + PD + 1):
                    do_pre(j)
                col0, msz, tag = blk
                gcols = groups[gi][1]
                ps = PS.next()
                if extra is not None and isinstance(tag, tuple) and tag[0] == "rot":
                    et, ebuf = extra
                    mm_group(ps.t, msz, KC, lambda c: et[:, c * 64:(c + 1) * 64], lambda c: rhs_fn(c, n),
                             [ebuf, act_bufs[n]], ps.buf)
                else:
                    mm_group(ps.t, msz, KC, lambda c: slot.t[:, c * gcols + col0:c * gcols + col0 + msz],
                             lambda c: rhs_fn(c, n), [slot.buf, act_bufs[n]], ps.buf)
                evac(tag, n, ps.t, ps.buf, msz, pres.pop(i, None))

        def store(stg, dst_ap, msz, dbuf):
            K.dma(SP, stg.dsem, dst_ap, stg.t[0:msz, :], reads=[stg.buf], pwrites=[dbuf])

        def nsl(n):
            return slice(n * 512, (n + 1) * 512)

        def rms_to_bf16(L, src_ap, src_buf, KC, gcol0, out_t, out_bufs, dim):
            W = 256 if KC > 4 else 512
            xin = [Slot(sb(L, "nx%d" % i, [128, KC * W], F32), K.buf("nx%d" % i, dma=True)) for i in range(2)]
            sq = [Slot(sb(L, "nsq%d" % i, [128, KC * W], F32), Buf("nsq%d" % i)) for i in range(1)]
            rs = [Slot(sb(L, "nrs%d" % i, [128, W], F32), Buf("nrs%d" % i)) for i in range(2)]
            for j in range(T // W):
                n = (j * W) // 512
                t0 = j * W
                xs = xin[j % 2]
                K.dma(SP, xs.dsem, xs.t[:].rearrange("p (c n) -> p c n", c=KC),
                      src_ap[:, t0:t0 + W].rearrange("(c p) n -> p c n", p=128), reads=[src_buf], writes=[xs.buf])
                s = sq[0]
                K.op(ACT, lambda: nc.scalar.activation(out=s.t[:], in_=xs.t[:], func=AF.Square),
                     reads=[xs.buf], writes=[s.buf])
                ps = PS.next()

                def emit():
                    last = None
                    for c in range(KC):
                        last = nc.tensor.matmul(ps.t[:, 0:W], lhsT=ones_f, rhs=s.t[:, c * W:(c + 1) * W],
                                                start=(c == 0), stop=(c == KC - 1))
                    return last
                K.op(PE, emit, reads=[s.buf, gl], writes=[ps.buf])
                r = rs[j % 2]
                K.op(ACT, lambda: nc.scalar.activation(out=r.t[:], in_=ps.t[:, 0:W], func=AF.Sqrt, scale=1.0 / dim,
                                                       bias=float(EPS)), reads=[ps.buf], writes=[r.buf])
                K.op(DVE, lambda: nc.vector.reciprocal(out=r.t[:], in_=r.t[:]), reads=[r.buf], writes=[r.buf])
                for c in range(KC):
                    K.op(DVE, lambda: nc.vector.scalar_tensor_tensor(
                        out=out_t[:, c * T + t0:c * T + t0 + W], in0=xs.t[:, c * W:(c + 1) * W],
                        scalar=vcol(gcol0 + c), in1=r.t[:], op0=ALU.mult, op1=ALU.mult),
                        reads=[xs.buf, r.buf, gl], pwrites=[out_bufs[n]])

        with ExitStack() as L:
            xl = [Slot(sb(L, "p0x%d" % i, [128, 4 * D], F32), K.buf("p0x%d" % i, dma=True)) for i in range(2)]
            for gq in range(NQ):
                s = xl[gq % 2]
                K.dma(SP, s.dsem, s.t[:].rearrange("p (a d) -> p a d", a=4),
                      x_in[gq * 512:(gq + 1) * 512, :].rearrange("(a p) d -> p a d", p=128), writes=[s.buf])
                for c in range(KD):
                    ps = PS.next()

                    def emit():
                        last = None
                        for a in range(4):
                            last = nc.tensor.transpose(ps.t[:, a * 128:(a + 1) * 128],
                                                       s.t[:, a * D + c * 128:a * D + (c + 1) * 128], ident_f)
                        return last
                    K.op(PE, emit, reads=[s.buf, gl], writes=[ps.buf])
                    st = SF.next()
                    evac_tog[0] ^= 1
                    if evac_tog[0]:
                        K.op(ACT, lambda: nc.scalar.copy(out=st.t[:], in_=ps.t[:]), reads=[ps.buf], writes=[st.buf])
                    else:
                        K.op(DVE, lambda: nc.vector.tensor_copy(out=st.t[:], in_=ps.t[:]), reads=[ps.buf],
                             writes=[st.buf])
                    store(st, xT[c * 128:(c + 1) * 128, nsl(gq)], 128, B["xT"])
            pl = [Slot(sb(L, "p0p%d" % i, [128, 4 * DPLE], F32), K.buf("p0p%d" % i, dma=True)) for i in range(2)]
            it = 0
            for l in range(DEPTH):
                for gq in range(NQ):
                    s = pl[it % 2]
                    it += 1
                    K.dma(SP, s.dsem, s.t[:].rearrange("p (a d) -> p a d", a=4),
                          p_in[l, gq * 512:(gq + 1) * 512, :].rearrange("(a p) d -> p a d", p=128), writes=[s.buf])
                    for c in range(2):
                        ps = PS.next()

                        def emit():
                            last = None
                            for a in range(4):
                                last = nc.tensor.transpose(ps.t[:, a * 128:(a + 1) * 128],
                                                           s.t[:, a * DPLE + c * 128:a * DPLE + (c + 1) * 128], ident_f)
                            return last
                        K.op(PE, emit, reads=[s.buf, gl], writes=[ps.buf])
                        st = SBF.next()
                        K.op(ACT, lambda: nc.scalar.copy(out=st.t[:], in_=ps.t[:]), reads=[ps.buf], writes=[st.buf])
                        store(st, pT[l * DPLE + c * 128:l * DPLE + (c + 1) * 128, nsl(gq)], 128, B["pT"])
        K.barrier()
        with ExitStack() as L:
            posi = Slot(sb(L, "posi", [64, T], I32), K.buf("posi", dma=True))
            K.dma(SP, posi.dsem, posi.t[:], pos_in[0, :].partition_broadcast(64), writes=[posi.buf])
            tb = Buf("trigb")
            ang = sb(L, "ang", [64, T], F32)
            nf = sb(L, "nf", [64, T], F32)
            ni = sb(L, "ni", [64, T], I32)
            rr = sb(L, "rr", [64, T], F32)
            tt = sb(L, "tt", [64, T], F32)
            tr = sb(L, "tr", [64, 4 * T], F32)
            K.op(DVE, lambda: nc.vector.tensor_copy(out=ang[:], in_=posi.t[:]), reads=[posi.buf], writes=[tb])
            K.op(DVE, lambda: nc.vector.tensor_scalar(out=ang[:], in0=ang[:], scalar1=vecs[0:64, V_INVF:V_INVF + 1],
                                                      scalar2=None, op0=ALU.mult), reads=[gl], writes=[tb])
            K.op(DVE, lambda: nc.vector.tensor_scalar(out=nf[:], in0=ang[:], scalar1=float(1.0 / (2 * np.pi)),
                                                      scalar2=None, op0=ALU.mult), writes=[tb])
            K.op(DVE, lambda: nc.vector.tensor_copy(out=ni[:], in_=nf[:]), writes=[tb])
            K.op(DVE, lambda: nc.vector.tensor_copy(out=nf[:], in_=ni[:]), writes=[tb])
            K.op(DVE, lambda: nc.vector.scalar_tensor_tensor(out=rr[:], in0=nf[:], scalar=-TWO_PI_HI, in1=ang[:],
                                                             op0=ALU.mult, op1=ALU.add), writes=[tb])
            K.op(DVE, lambda: nc.vector.scalar_tensor_tensor(out=rr[:], in0=nf[:], scalar=-TWO_PI_LO, in1=rr[:],
                                                             op0=ALU.mult, op1=ALU.add), writes=[tb])
            K.op(DVE, lambda: nc.vector.tensor_scalar(out=tt[:], in0=rr[:], scalar1=float(np.pi),
                                                      scalar2=float(-2 * np.pi), op0=ALU.is_gt, op1=ALU.mult),
                 writes=[tb])
            K.op(DVE, lambda: nc.vector.tensor_tensor(out=rr[:], in0=rr[:], in1=tt[:], op=ALU.add), writes=[tb])
            K.op(DVE, lambda: nc.vector.tensor_scalar(out=tt[:], in0=rr[:], scalar1=float(-np.pi),
                                                      scalar2=float(2 * np.pi), op0=ALU.is_lt, op1=ALU.mult),
                 writes=[tb])
            K.op(DVE, lambda: nc.vector.tensor_tensor(out=rr[:], in0=rr[:], in1=tt[:], op=ALU.add), writes=[tb])
            K.op(DVE, lambda: nc.vector.tensor_scalar(out=rr[:], in0=rr[:], scalar1=float(np.pi), scalar2=float(-np.pi),
                                                      op0=ALU.min, op1=ALU.max), writes=[tb])
            K.op(ACT, lambda: nc.scalar.activation(out=tt[:], in_=rr[:], func=AF.Abs), writes=[tb])
            K.op(DVE, lambda: nc.vector.tensor_scalar(out=tt[:], in0=tt[:], scalar1=-1.0, scalar2=float(np.pi / 2),
                                                      op0=ALU.mult, op1=ALU.add), writes=[tb])
            K.op(ACT, lambda: nc.scalar.activation(out=tr[:, T:2 * T], in_=rr[:], func=AF.Sin), reads=[tb], writes=[tb])
            K.op(ACT, lambda: nc.scalar.activation(out=tr[:, 0:T], in_=tt[:], func=AF.Sin), reads=[tb], writes=[tb])
            K.op(DVE, lambda: nc.vector.tensor_scalar(out=tr[:, 2 * T:4 * T], in0=tr[:, 0:2 * T], scalar1=SCALE,
                                                      scalar2=None, op0=ALU.mult), reads=[tb], writes=[tb])
            trd = K.sem("d_trig")
            K.dma(SP, trd, trig.rearrange("(a p) n -> p a n", p=64), tr[:].rearrange("p (a n) -> p a n", a=4),
                  reads=[tb], writes=[B["trig"]])
        K.barrier()

        for l in range(depth):
            vb = l * V_PER_LAYER
            with ExitStack() as L:
                uT = sb(L, "uT", [128, KD * T], BF16)
                ub = [Buf("ub%d" % n) for n in range(NQ)]
                with ExitStack() as L2:
                    rms_to_bf16(L2, xT, B["xT"], KD, vb + V_GMIX, uT, ub, D)
                K.barrier()
                segs = [(C_CQ, 512, "copy"), (C_CKV, 512, "copy"), (C_KPE, 64, "kpe"), (C_XB, 2048, "copy"),
                        (C_YB, 2048, "gelu"), (C_GA, 2048, "sig"), (C_GR, 2048, "sig")]
                groups = []
                for (c0, width, kind) in segs:
                    if kind == "kpe":
                        def post(slot, gcols):
                            rt = sb(L, "krotw", [128, KD * 64], BF16)
                            rb = Buf("krotw")
                            for c in range(KD):
                                K.op(DVE, lambda: nc.vector.tensor_scalar(
                                    out=rt[:, c * 64:c * 64 + 32], in0=slot.t[:, c * 64 + 32:c * 64 + 64],
                                    scalar1=-1.0, scalar2=None, op0=ALU.mult), reads=[slot.buf], pwrites=[rb])
                                K.op(DVE, lambda: nc.vector.tensor_copy(
                                    out=rt[:, c * 64 + 32:c * 64 + 64], in_=slot.t[:, c * 64:c * 64 + 32]),
                                    reads=[slot.buf], pwrites=[rb])
                            return rt, rb
                        groups.append((w_in[l, :, c0:c0 + 64], 64, [(0, 64, ("copy", c0)), (0, 64, ("rot", C_KROT))],
                                       post))
                    else:
                        for g0 in range(0, width, 512):
                            blocks = [(m0, 128, (kind, c0 + g0 + m0)) for m0 in range(0, 512, 128)]
                            groups.append((w_in[l, :, c0 + g0:c0 + g0 + 512], 512, blocks, None))

                def evacA(tag, n, ps, psb, msz, pre):
                    kind, row0 = tag
                    st = SF.next()
                    if kind in ("copy", "rot"):
                        evac_tog[0] ^= 1
                        if evac_tog[0]:
                            K.op(ACT, lambda: nc.scalar.copy(out=st.t[0:msz, :], in_=ps[0:msz, :]), reads=[psb],
                                 writes=[st.buf])
                        else:
                            K.op(DVE, lambda: nc.vector.tensor_copy(out=st.t[0:msz, :], in_=ps[0:msz, :]), reads=[psb],
                                 writes=[st.buf])
                    else:
                        f = AF.Gelu_apprx_tanh if kind == "gelu" else AF.Sigmoid
                        K.op(ACT, lambda: nc.scalar.activation(out=st.t[0:msz, :], in_=ps[0:msz, :], func=f),
                             reads=[psb], writes=[st.buf])
                    store(st, zT[row0:row0 + msz, nsl(n)], msz, B["zT"])
                linear(groups, KD, list(range(NQ)), lambda c, n: uT[:, c * T + n * 512:c * T + (n + 1) * 512], ub, evacA)
            K.barrier()

            with ExitStack() as L:
                cqT = sb(L, "cqT", [128, 4 * T], BF16)
                cqb = [Buf("cqb%d" % n) for n in range(NQ)]
                with ExitStack() as L2:
                    ckvT = sb(L2, "ckvT", [128, 4 * T], BF16)
                    ckb = [Buf("ckb%d" % n) for n in range(NQ)]
                    with ExitStack() as L3:
                        rms_to_bf16(L3, zT[C_CQ:C_CQ + 512, :], B["zT"], 4, vb + V_GQ, cqT, cqb, DQ)
                    K.barrier()
                    with ExitStack() as L3:
                        rms_to_bf16(L3, zT[C_CKV:C_CKV + 512, :], B["zT"], 4, vb + V_GKV, ckvT, ckb, DKV)
                    K.barrier()
                    d1 = K.sem("d_ckvst%d" % l)
                    for pc in range(NP):
                        K.dma(SP, d1, ckv_own[pc][0:512, :].rearrange("(c p) n -> p c n", p=128),
                              ckvT[:].rearrange("p (c n) -> p c n", c=4)[:, :, pc * CW:(pc + 1) * CW], reads=ckb,
                              pwrites=[B["ckv_own"]])
                    kraw = Slot(sb(L2, "kraw", [64, T], F32), K.buf("kraw%d" % l, dma=True))
                    krot = Slot(sb(L2, "krot", [64, T], F32), K.buf("krot%d" % l, dma=True))
                    cs = Slot(sb(L2, "cs", [64, 2 * T], F32), K.buf("cs%d" % l, dma=True))
                    kpe = Slot(sb(L2, "kpe", [64, T], BF16), K.buf("kpe%d" % l, dma=True))
                    K.dma(SP, kraw.dsem, kraw.t[:], zT[C_KPE:C_KPE + 64, :], reads=[B["zT"]], writes=[kraw.buf])
                    K.dma(SP, krot.dsem, krot.t[:], zT[C_KROT:C_KROT + 64, :], reads=[B["zT"]], writes=[krot.buf])
                    K.dma(SP, cs.dsem, cs.t[:].rearrange("p (a n) -> p a n", a=2),
                          trig[0:128, :].rearrange("(a p) n -> p a n", p=64), reads=[B["trig"]], writes=[cs.buf])
                    K.op(DVE, lambda: nc.vector.tensor_tensor(out=kraw.t[:], in0=kraw.t[:], in1=cs.t[:, 0:T],
                                                              op=ALU.mult), reads=[cs.buf], writes=[kraw.buf])
                    K.op(DVE, lambda: nc.vector.tensor_tensor(out=krot.t[:], in0=krot.t[:], in1=cs.t[:, T:2 * T],
                                                              op=ALU.mult), reads=[cs.buf], writes=[krot.buf])
                    K.op(DVE, lambda: nc.vector.tensor_tensor(out=kpe.t[:], in0=kraw.t[:], in1=krot.t[:], op=ALU.add),
                         reads=[kraw.buf, krot.buf], writes=[kpe.buf])
                    for pc in range(NP):
                        K.dma(SP, kpe.dsem, ckv_own[pc][512:576, :], kpe.t[:, pc * CW:(pc + 1) * CW], reads=[kpe.buf],
                              pwrites=[B["ckv_own"]])
                    d2 = K.sem("d_xtail%d" % l)
                    with nc.allow_non_contiguous_dma(reason="tiny conv tail"):
                        K.dma(SP, d2, xtail_own[:, 0:3], zT[C_XB:C_XB + D, T - 3:T], reads=[B["zT"]],
                              pwrites=[B["xtail_own"]])
                    for pc in range(NP):
                        K.cc(PAIRS, ckv_own_ts[pc], ckv_all_ts[pc], reads=[B["ckv_own"]],
                             writes=[B["ckv_all"]] if pc == 0 else (), pwrites=[B["ckv_all"]] if pc else ())
                    K.cc(PAIRS, xtail_own_t, xtail_all_t, reads=[B["xtail_own"]], writes=[B["xtail_all"]])
                K.barrier()
                with ExitStack() as L2:
                    css = Slot(sb(L2, "css", [64, 2 * T], F32), K.buf("css%d" % l, dma=True))
                    K.dma(SP, css.dsem, css.t[:].rearrange("p (a n) -> p a n", a=2),
                          trig[128:256, :].rearrange("(a p) n -> p a n", p=64), reads=[B["trig"]], writes=[css.buf])
                    qtmp = [Slot(sb(L2, "qtmp%d" % i, [64, 512], F32), Buf("qtmp%d" % i)) for i in range(2)]
                    qrw = [Slot(sb(L2, "qrw%d" % i, [128, 2 * 4 * 64], BF16), Buf("qrw%d" % i)) for i in range(2)]
                    for gi, hg in enumerate(range(0, NH, 2)):
                        gcols = 384
                        slot = WS.next()
                        load_w(slot, w_qb[l, :, hg * 192:(hg + 2) * 192], 4, gcols)
                        rw = qrw[gi % 2]
                        first = True
                        for hh_ in range(2):
                            for c in range(4):
                                base = c * gcols + hh_ * 192 + 128
                                ob = (hh_ * 4 + c) * 64
                                K.op(POOL, lambda: nc.gpsimd.tensor_scalar(
                                    out=rw.t[:, ob:ob + 32], in0=slot.t[:, base + 32:base + 64], scalar1=-1.0,
                                    scalar2=None, op0=ALU.mult), reads=[slot.buf],
                                    writes=[rw.buf] if first else (), pwrites=() if first else [rw.buf])
                                first = False
                                K.op(POOL, lambda: nc.gpsimd.tensor_copy(out=rw.t[:, ob + 32:ob + 64],
                                                                         in_=slot.t[:, base:base + 32]),
                                     reads=[slot.buf], pwrites=[rw.buf])
                        for hh_ in range(2):
                            h = hg + hh_
                            for n in range(NQ):
                                rhs = lambda c: cqT[:, c * T + n * 512:c * T + (n + 1) * 512]
                                col0 = hh_ * 192
                                ps = PS.next()
                                mm_group(ps.t, 128, 4, lambda c: slot.t[:, c * gcols + col0:c * gcols + col0 + 128],
                                         rhs, [slot.buf, cqb[n]], ps.buf)
                                st = SBF.next()
                                K.op(ACT, lambda: nc.scalar.activation(out=st.t[:], in_=ps.t[:], func=AF.Copy,
                                                                       scale=SCALE), reads=[ps.buf], writes=[st.buf])
                                store(st, qn[h * 128:(h + 1) * 128, nsl(n)], 128, B["qn"])
                                ps1 = PS.next()
                                mm_group(ps1.t, 64, 4,
                                         lambda c: slot.t[:, c * gcols + col0 + 128:c * gcols + col0 + 192],
                                         rhs, [slot.buf, cqb[n]], ps1.buf)
                                ps2 = PS.next()
                                mm_group(ps2.t, 64, 4, lambda c: rw.t[:, (hh_ * 4 + c) * 64:(hh_ * 4 + c + 1) * 64], rhs,
                                         [rw.buf, cqb[n]], ps2.buf)
                                qt = qtmp[0]
                                q2 = qtmp[1]
                                K.op(DVE, lambda: nc.vector.tensor_tensor(out=qt.t[:], in0=ps1.t[0:64, :],
                                                                          in1=css.t[:, n * 512:(n + 1) * 512],
                                                                          op=ALU.mult),
                                     reads=[ps1.buf, css.buf], writes=[qt.buf])
                                K.op(DVE, lambda: nc.vector.tensor_tensor(out=q2.t[:], in0=ps2.t[0:64, :],
                                                                          in1=css.t[:, T + n * 512:T + (n + 1) * 512],
                                                                          op=ALU.mult),
                                     reads=[ps2.buf, css.buf], writes=[q2.buf])
                                st = SBF.next()
                                K.op(DVE, lambda: nc.vector.tensor_tensor(out=st.t[0:64, :], in0=qt.t[:],
                                                                          in1=q2.t[:], op=ALU.add),
                                     reads=[qt.buf, q2.buf], writes=[st.buf])
                                store(st, qr[h * 64:(h + 1) * 64, nsl(n)], 64, B["qr"])
            K.barrier()

            with ExitStack() as L:
                NK = T2 // 512
                cka = Slot(sb(L, "cka", [128, 4 * T2], BF16), K.buf("cka%d" % l, dma=True))
                cka3 = cka.t[:].rearrange("p (c n) -> p c n", c=4)
                first = True
                for r in range(2):
                    for pc in range(NP):
                        K.dma(SP, cka.dsem, cka3[:, :, r * T + pc * CW:r * T + (pc + 1) * CW],
                              ckv_all[pc][r * 576:r * 576 + 512, :].rearrange("(c p) n -> p c n", p=128),
                              reads=[B["ckv_all"]], writes=[cka.buf] if first else (), pwrites=() if first else [cka.buf])
                        first = False
                wk = Slot(sb(L, "wk", [128, 4 * 2048], BF16), K.buf("wk%d" % l, dma=True))
                wv = Slot(sb(L, "wv", [128, 4 * 2048], BF16), K.buf("wv%d" % l, dma=True))
                wsrc = w_kvb[l].rearrange("(c p) (h two d) -> p c h two d", p=128, two=2, d=128)
                for c in range(4):
                    K.dma(POOL, wk.dsem, wk.t[:, c * 2048:(c + 1) * 2048].rearrange("p (h d) -> p h d", d=128),
                          wsrc[:, c, :, 0, :], writes=[wk.buf] if c == 0 else (), pwrites=[wk.buf] if c else ())
                    K.dma(POOL, wv.dsem, wv.t[:, c * 2048:(c + 1) * 2048].rearrange("p (h d) -> p h d", d=128),
                          wsrc[:, c, :, 1, :], writes=[wv.buf] if c == 0 else (), pwrites=[wv.buf] if c else ())
                for h in range(NH):
                    for kc in range(NK):
                        ps = PS.next()
                        mm_group(ps.t, 128, 4, lambda c: wk.t[:, c * 2048 + h * 128:c * 2048 + (h + 1) * 128],
                                 lambda c: cka.t[:, c * T2 + kc * 512:c * T2 + (kc + 1) * 512], [wk.buf, cka.buf], ps.buf)
                        st = SBF.next()
                        evac_tog[0] ^= 1
                        if evac_tog[0]:
                            K.op(ACT, lambda: nc.scalar.copy(out=st.t[:], in_=ps.t[:]), reads=[ps.buf], writes=[st.buf])
                        else:
                            K.op(DVE, lambda: nc.vector.tensor_copy(out=st.t[:], in_=ps.t[:]), reads=[ps.buf],
                                 writes=[st.buf])
                        store(st, kn[h * 128:(h + 1) * 128, kc * 512:(kc + 1) * 512], 128, B["kn"])
                for kt in range(T2 // 128):
                    for cb in range(4):
                        ps = PS.next()
                        mm_group(ps.t, 128, 4, lambda c: cka.t[:, c * T2 + kt * 128:c * T2 + (kt + 1) * 128],
                                 lambda c: wv.t[:, c * 2048 + cb * 512:c * 2048 + (cb + 1) * 512], [wv.buf, cka.buf],
                                 ps.buf)
                        st = SBF.next()
                        evac_tog[0] ^= 1
                        if evac_tog[0]:
                            K.op(ACT, lambda: nc.scalar.copy(out=st.t[:], in_=ps.t[:]), reads=[ps.buf], writes=[st.buf])
                        else:
                            K.op(DVE, lambda: nc.vector.tensor_copy(out=st.t[:], in_=ps.t[:]), reads=[ps.buf],
                                 writes=[st.buf])
                        store(st, vv[kt * 128:(kt + 1) * 128, cb * 512:(cb + 1) * 512], 128, B["vv"])
            K.barrier()

            with ExitStack() as L:
                wab = Slot(sb(L, "wab", [128, NH * 128], BF16), K.buf("wab%d" % l, dma=True))
                wxb = Slot(sb(L, "wxb", [128, NH * 128], BF16), K.buf("wxb%d" % l, dma=True))
                K.dma(POOL, wab.dsem, wab.t[:].rearrange("p (h j) -> p h j", h=NH), w_a[l].rearrange("h i j -> i h j"),
                      writes=[wab.buf])
                K.dma(POOL, wxb.dsem, wxb.t[:].rearrange("p (h j) -> p h j", h=NH), w_x[l].rearrange("h i j -> i h j"),
                      writes=[wxb.buf])
                lc = sb(L, "lc", [128, 16 * 6], F32)
                lb = Buf("lc")
                lam = vecs[:, vb + V_LAM:vb + V_LAM + 16]
                e_, s_, s2_, pl_, c_, c2_ = [lc[:, i * 16:(i + 1) * 16] for i in range(6)]
                K.op(ACT, lambda: nc.scalar.activation(out=e_, in_=lam, func=AF.Abs), reads=[gl], writes=[lb])
                K.op(ACT, lambda: nc.scalar.activation(out=e_, in_=e_, func=AF.Exp, scale=-1.0), writes=[lb])
                K.op(DVE, lambda: nc.vector.tensor_scalar(out=s_, in0=e_, scalar1=2.0, scalar2=None, op0=ALU.add),
                     writes=[lb])
                K.op(DVE, lambda: nc.vector.reciprocal(out=s_, in_=s_), writes=[lb])
                K.op(DVE, lambda: nc.vector.tensor_tensor(out=s_, in0=s_, in1=e_, op=ALU.mult), writes=[lb])
                K.op(DVE, lambda: nc.vector.tensor_tensor(out=s2_, in0=s_, in1=s_, op=ALU.mult), writes=[lb])
                K.op(DVE, lambda: nc.vector.tensor_scalar(out=pl_, in0=s2_, scalar1=1.0 / 9, scalar2=1.0 / 7,
                                                          op0=ALU.mult, op1=ALU.add), writes=[lb])
                for cf in (1.0 / 5, 1.0 / 3, 1.0):
                    K.op(DVE, lambda: nc.vector.tensor_tensor(out=pl_, in0=pl_, in1=s2_, op=ALU.mult), writes=[lb])
                    K.op(DVE, lambda: nc.vector.tensor_scalar(out=pl_, in0=pl_, scalar1=float(cf), scalar2=None,
                                                              op0=ALU.add), writes=[lb])
                K.op(DVE, lambda: nc.vector.tensor_tensor(out=pl_, in0=pl_, in1=s_, op=ALU.mult), writes=[lb])
                K.op(DVE, lambda: nc.vector.tensor_scalar(out=c_, in0=lam, scalar1=-1.0, scalar2=0.0, op0=ALU.mult,
                                                          op1=ALU.max), reads=[gl], writes=[lb])
                K.op(DVE, lambda: nc.vector.scalar_tensor_tensor(out=c_, in0=pl_, scalar=2.0, in1=c_, op0=ALU.mult,
                                                                 op1=ALU.add), writes=[lb])
                K.op(DVE, lambda: nc.vector.tensor_scalar(out=c_, in0=c_, scalar1=-8.0, scalar2=None, op0=ALU.mult),
                     writes=[lb])
                K.op(DVE, lambda: nc.vector.tensor_scalar(out=c2_, in0=c_, scalar1=2.0, scalar2=None, op0=ALU.mult),
                     writes=[lb])
                hf = Slot(sb(L, "hf", [128, 16], F32), K.buf("hf%d" % l, dma=True))
                xbp = [Slot(sb(L, "xbp%d" % i, [128, T + 4], F32), K.buf("xbp%d_%d" % (i, l), dma=True)) for i in range(2)]
                tl = [Slot(sb(L, "tl%d" % i, [128, 4], F32), K.buf("tl%d_%d" % (i, l), dma=True)) for i in range(2)]
                xcL = [Slot(sb(L, "xc%d" % i, [128, T], F32), Buf("xc%d" % i)) for i in range(2)]
                xcbL = [Slot(sb(L, "xcb%d" % i, [128, T], BF16), Buf("xcb%d" % i)) for i in range(2)]
                rtL = [Slot(sb(L, "rt%d" % i, [128, T], F32), Buf("rt%d" % i)) for i in range(2)]
                itL = [Slot(sb(L, "it%d" % i, [128, T], F32), Buf("it%d" % i)) for i in range(2)]
                atL = [Slot(sb(L, "at%d" % i, [128, T], F32), K.buf("at%d_%d" % (i, l), dma=True)) for i in range(2)]
                btL = [Slot(sb(L, "bt%d" % i, [128, T], F32), K.buf("bt%d_%d" % (i, l), dma=True)) for i in range(2)]
                hsL = [Slot(sb(L, "hs%d" % i, [128, T], F32), Buf("hs%d" % i)) for i in range(2)]
                for cg in range(NH):
                    xs = xbp[cg % 2]
                    tls = tl[cg % 2]
                    xc, xcb, rt_, it_ = xcL[cg % 2], xcbL[cg % 2], rtL[cg % 2], itL[cg % 2]
                    at_, bt_, hs = atL[cg % 2], btL[cg % 2], hsL[cg % 2]
                    def c1_load(g_):
                        xs_, tls_ = xbp[g_ % 2], tl[g_ % 2]
                        K.dma(SP, xs_.dsem, xs_.t[:, 4:4 + T], zT[C_XB + g_ * 128:C_XB + (g_ + 1) * 128, :],
                              reads=[B["zT"]], writes=[xs_.buf])
                        K.dma(SP, tls_.dsem, tls_.t[:, 0:4], xtail_all[g_ * 128:(g_ + 1) * 128, :],
                              reads=[B["xtail_all"]], writes=[tls_.buf])
                    if cg == 0:
                        c1_load(0)
                    if cg + 1 < NH:
                        c1_load(cg + 1)
                    K.op(DVE, lambda: nc.vector.tensor_scalar(out=xs.t[:, 1:4], in0=tls.t[:, 0:3], scalar1=vcol(V_FLAG),
                                                              scalar2=None, op0=ALU.mult),
                         reads=[tls.buf, gl], pwrites=[xs.buf])
                    cw = lambda j: vcol(vb + V_CONVW + j * 16 + cg)
                    K.op(DVE, lambda: nc.vector.tensor_scalar(out=xc.t[:], in0=xs.t[:, 4:4 + T], scalar1=cw(3),
                                                              scalar2=vcol(vb + V_CONVB + cg), op0=ALU.mult,
                                                              op1=ALU.add), reads=[xs.buf, gl], writes=[xc.buf])
                    for j in range(3):
                        K.op(DVE, lambda: nc.vector.scalar_tensor_tensor(out=xc.t[:], in0=xs.t[:, 1 + j:1 + j + T],
                                                                         scalar=cw(j), in1=xc.t[:], op0=ALU.mult,
                                                                         op1=ALU.add), reads=[xs.buf], writes=[xc.buf])
                    K.op(ACT, lambda: nc.scalar.copy(out=xcb.t[:], in_=xc.t[:]), reads=[xc.buf], writes=[xcb.buf])
                    for n in range(NQ):
                        ps = PS.next()
                        mm_group(ps.t, 128, 1, lambda c: wab.t[:, cg * 128:(cg + 1) * 128],
                                 lambda c: xcb.t[:, nsl(n)], [wab.buf, xcb.buf], ps.buf)
                        K.op(ACT, lambda: nc.scalar.activation(out=rt_.t[:, nsl(n)], in_=ps.t[:], func=AF.Sigmoid,
                                                               bias=vcol(vb + V_BA + cg)),
                             reads=[ps.buf, gl], writes=[rt_.buf] if n == 0 else (), pwrites=[rt_.buf] if n else ())
                        ps2 = PS.next()
                        mm_group(ps2.t, 128, 1, lambda c: wxb.t[:, cg * 128:(cg + 1) * 128],
                                 lambda c: xcb.t[:, nsl(n)], [wxb.buf, xcb.buf], ps2.buf)
                        K.op(ACT, lambda: nc.scalar.activation(out=it_.t[:, nsl(n)], in_=ps2.t[:], func=AF.Sigmoid,
                                                               bias=vcol(vb + V_BX + cg)),
                             reads=[ps2.buf, gl], writes=[it_.buf] if n == 0 else (), pwrites=[it_.buf] if n else ())
                    K.op(ACT, lambda: nc.scalar.activation(out=at_.t[:], in_=rt_.t[:], func=AF.Exp,
                                                           scale=lc[:, 64 + cg:65 + cg]),
                         reads=[rt_.buf, lb], writes=[at_.buf])
                    K.op(ACT, lambda: nc.scalar.activation(out=rt_.t[:], in_=rt_.t[:], func=AF.Exp,
                                                           scale=lc[:, 80 + cg:81 + cg]),
                         reads=[lb], writes=[rt_.buf])
                    K.op(ACT, lambda: nc.scalar.activation(out=rt_.t[:], in_=rt_.t[:], func=AF.Sqrt, scale=-1.0,
                                                           bias=1.0), writes=[rt_.buf])
                    K.op(DVE, lambda: nc.vector.tensor_tensor(out=it_.t[:], in0=it_.t[:], in1=xc.t[:], op=ALU.mult),
                         reads=[xc.buf], writes=[it_.buf])
                    K.op(DVE, lambda: nc.vector.tensor_tensor(out=bt_.t[:], in0=it_.t[:], in1=rt_.t[:], op=ALU.mult),
                         reads=[it_.buf, rt_.buf], writes=[bt_.buf])
                    K.dma(SP, at_.dsem, aa[cg * 128:(cg + 1) * 128, :], at_.t[:], reads=[at_.buf], pwrites=[B["aa"]])
                    K.dma(SP, bt_.dsem, bb[cg * 128:(cg + 1) * 128, :], bt_.t[:], reads=[bt_.buf], pwrites=[B["bb"]])
                    K.op(DVE, lambda: nc.vector.tensor_tensor_scan(out=hs.t[:], data0=at_.t[:], data1=bt_.t[:],
                                                                   initial=0.0, op0=ALU.mult, op1=ALU.add),
                         reads=[at_.buf, bt_.buf], writes=[hs.buf])
                    K.op(DVE, lambda: nc.vector.tensor_copy(out=hf.t[:, cg:cg + 1], in_=hs.t[:, T - 1:T]),
                         reads=[hs.buf], pwrites=[hf.buf])
                with nc.allow_non_contiguous_dma(reason="tiny state vector"):
                    K.dma(SP, hf.dsem, hfin_own.rearrange("(c p) o -> p (c o)", p=128), hf.t[:], reads=[hf.buf],
                          writes=[B["hfin_own"]])
                K.cc(PAIRS, hfin_own_t, hfin_all_t, reads=[B["hfin_own"]], writes=[B["hfin_all"]])
            K.barrier()

            with ExitStack() as L:
                kpeT = Slot(sb(L, "kpeT", [128, T2], BF16), K.buf("kpeT%d" % l, dma=True))
                K.op(DVE, lambda: nc.vector.memset(kpeT.t[64:128, :], 0.0), writes=[kpeT.buf])
                masks = sb(L, "masks", [128, 8, 512], BF16)
                mkb = K.buf("masks%d" % l, dma=True)
                K.dma(SP, mkb.dsem, masks[:], masks_in.rearrange("a d p n -> p (a d) n"), writes=[mkb])
                first = True
                for r in range(2):
                    for pc in range(NP):
                        K.dma(SP, kpeT.dsem, kpeT.t[0:64, r * T + pc * CW:r * T + (pc + 1) * CW],
                              ckv_all[pc][r * 576 + 512:r * 576 + 576, :],
                              reads=[B["ckv_all"]], pwrites=[kpeT.buf])
                        first = False
                knT = [Slot(sb(L, "knT%d" % i, [128, T2], BF16), K.buf("knT%d_%d" % (i, l), dma=True)) for i in range(2)]
                vT = [Slot(sb(L, "vT%d" % i, [128, T2], BF16), K.buf("vT%d_%d" % (i, l), dma=True)) for i in range(2)]
                qnT = [Slot(sb(L, "qnT%d" % i, [128, 512], BF16), K.buf("qnT%d_%d" % (i, l), dma=True)) for i in range(2)]
                qrT = [Slot(sb(L, "qrT%d" % i, [128, 512], BF16), K.buf("qrT%d_%d" % (i, l), dma=True)) for i in range(2)]
                for q_ in qrT:
                    K.op(DVE, lambda: nc.vector.memset(q_.t[64:128, :], 0.0), writes=[q_.buf])
                gaT = [Slot(sb(L, "gaT%d" % i, [128, 512], F32), K.buf("gaT%d_%d" % (i, l), dma=True)) for i in range(2)]
                pt = Ring([Slot(sb(L, "pt%d" % i, [128, 512], BF16), Buf("pt%d" % i)) for i in range(4)])
                rc = [Slot(sb(L, "rc%d" % i, [128, 512], F32), Buf("rc%d" % i)) for i in range(2)]
                def mk1(nm, dt=F32):
                    return Slot(sb(L, nm, [128, T], dt), K.buf("%s_%d" % (nm, l), dma=True))
                A_, B_, GY, SG, MA, HS = mk1("c2a"), mk1("c2b"), mk1("c2y"), mk1("c2s"), mk1("c2m"), mk1("c2h")
                MO = mk1("c2o", BF16)
                h0 = Slot(sb(L, "c2h0", [128, 2], F32), K.buf("c2h0_%d" % l, dma=True))

                def c2_load(cg):
                    rows = slice(cg * 128, (cg + 1) * 128)
                    K.dma(SP, A_.dsem, A_.t[:], aa[rows, :], reads=[B["aa"]], writes=[A_.buf])
                    K.dma(SP, B_.dsem, B_.t[:], bb[rows, :], reads=[B["bb"]], writes=[B_.buf])
                    K.dma(SP, GY.dsem, GY.t[:], zT[C_YB + cg * 128:C_YB + (cg + 1) * 128, :], reads=[B["zT"]],
                          writes=[GY.buf])
                    K.dma(SP, SG.dsem, SG.t[:], zT[C_GR + cg * 128:C_GR + (cg + 1) * 128, :], reads=[B["zT"]],
                          writes=[SG.buf])
                    K.dma(SP, MA.dsem, MA.t[:], ma[rows, :], reads=[B["ma"]], writes=[MA.buf])
                    K.dma(SP, h0.dsem, h0.t[:, 0:1], hfin_all[rows, :], reads=[B["hfin_all"]], writes=[h0.buf])

                def c2_compute(cg):
                    rows = slice(cg * 128, (cg + 1) * 128)
                    K.op(DVE, lambda: nc.vector.tensor_scalar(out=h0.t[:, 1:2], in0=h0.t[:, 0:1],
                                                              scalar1=vcol(V_FLAG), scalar2=None, op0=ALU.mult),
                         reads=[gl], writes=[h0.buf])
                    K.op(DVE, lambda: nc.vector.tensor_tensor_scan(out=HS.t[:], data0=A_.t[:], data1=B_.t[:],
                                                                   initial=h0.t[:, 1:2], op0=ALU.mult, op1=ALU.add),
                         reads=[A_.buf, B_.buf, h0.buf], writes=[HS.buf])
                    K.op(DVE, lambda: nc.vector.tensor_tensor(out=GY.t[:], in0=GY.t[:], in1=HS.t[:], op=ALU.mult),
                         reads=[HS.buf], writes=[GY.buf])
                    K.op(DVE, lambda: nc.vector.tensor_tensor(out=GY.t[:], in0=GY.t[:], in1=SG.t[:], op=ALU.mult),
                         reads=[SG.buf], writes=[GY.buf])
                    K.op(DVE, lambda: nc.vector.tensor_tensor(out=MO.t[:], in0=GY.t[:], in1=MA.t[:], op=ALU.add),
                         reads=[GY.buf, MA.buf], writes=[MO.buf])
                    K.dma(SP, MO.dsem, mg[rows, :], MO.t[:], reads=[MO.buf], pwrites=[B["mg"]])
                NKT = T // 128
                it = 0
                PSA = Ring(PS.slots[0:4])
                PSS = Ring(PS.slots[4:8])
                def load_kv(h):
                    ks, vs = knT[h % 2], vT[h % 2]
                    K.dma(SP, ks.dsem, ks.t[:], kn[h * 128:(h + 1) * 128, :], reads=[B["kn"]], writes=[ks.buf])
                    K.dma(POOL, vs.dsem, vs.t[:].rearrange("p (k d) -> p k d", d=128),
                          vv[:, h * 128:(h + 1) * 128].rearrange("(k p) d -> p k d", p=128), reads=[B["vv"]],
                          writes=[vs.buf])

                def load_q(it_):
                    h, qc = it_ // NQ, it_ % NQ
                    qs, qrs, gs = qnT[it_ % 2], qrT[it_ % 2], gaT[it_ % 2]
                    K.dma(SP, qs.dsem, qs.t[:], qn[h * 128:(h + 1) * 128, nsl(qc)], reads=[B["qn"]], writes=[qs.buf])
                    K.dma(SP, qrs.dsem, qrs.t[0:64, :], qr[h * 64:(h + 1) * 64, nsl(qc)], reads=[B["qr"]], writes=[qrs.buf])
                    K.dma(SP, gs.dsem, gs.t[:], zT[C_GA + h * 128:C_GA + (h + 1) * 128, nsl(qc)], reads=[B["zT"]],
                          writes=[gs.buf])
                load_kv(0)
                load_q(0)
                for h in range(NH + 1):
                    if h >= 1:
                        c2_load(h - 1)
                    if h == NH:
                        c2_compute(h - 1)
                        break
                    ks, vs = knT[h % 2], vT[h % 2]
                    for qc in range(NQ):
                        qs, qrs, gs = qnT[it % 2], qrT[it % 2], gaT[it % 2]
                        rcs = rc[it % 2]
                        if qc == NQ - 1 and h >= 1:
                            c2_compute(h - 1)
                        if qc == 0 and h + 1 < NH:
                            load_kv(h + 1)
                        if it + 1 < NH * NQ:
                            load_q(it + 1)
                        it += 1
                        tiles = []
                        for kt in range(NKT):
                            d = kt - 4 * qc
                            if d < 0:
                                tiles.append((kt, None, None))
                            elif d < 4:
                                tiles.append((kt, d, None))
                            else:
                                tiles.append((kt, None, V_B0))
                        for kt in range(min(NKT, 4 * qc + 4)):
                            d = kt - 4 * qc
                            tiles.append((NKT + kt, (4 + d) if d >= 0 else None, V_B1))
                        ops_ = PSA.next()
                        sps_ = PSA.next()
                        pend = []

                        def qk(ti):
                            kt, dm, bcol = tiles[ti]
                            ps = PSS.next()

                            def emit():
                                nc.tensor.matmul(ps.t[:], lhsT=ks.t[:, kt * 128:(kt + 1) * 128], rhs=qs.t[:], start=True,
                                                 stop=False)
                                last = nc.tensor.matmul(ps.t[:], lhsT=kpeT.t[:, kt * 128:(kt + 1) * 128], rhs=qrs.t[:],
                                                        start=False, stop=(dm is None))
                                if dm is not None:
                                    last = nc.tensor.matmul(ps.t[:], lhsT=ident_b, rhs=masks[:, dm, :], start=False,
                                                            stop=True)
                                return last
                            K.op(PE, emit, reads=[ks.buf, kpeT.buf, qs.buf, qrs.buf, gl, mkb], writes=[ps.buf])
                            p_ = pt.next()
                            if bcol is None:
                                K.op(ACT, lambda: nc.scalar.activation(out=p_.t[:], in_=ps.t[:], func=AF.Exp),
                                     reads=[ps.buf], writes=[p_.buf])
                            else:
                                K.op(ACT, lambda: nc.scalar.activation(out=p_.t[:], in_=ps.t[:], func=AF.Exp,
                                                                       bias=vcol(bcol)),
                                     reads=[ps.buf, gl], writes=[p_.buf])
                            return (kt, p_)

                        def pv(ti, kt, p_):
                            first, lastt = ti == 0, ti == len(tiles) - 1
                            K.op(PE, lambda: nc.tensor.matmul(ops_.t[:], lhsT=vs.t[:, kt * 128:(kt + 1) * 128], rhs=p_.t[:],
                                                              start=first, stop=lastt),
                                 reads=[vs.buf, p_.buf], writes=[ops_.buf] if first else (),
                                 pwrites=() if first else [ops_.buf])
                            K.op(PE, lambda: nc.tensor.matmul(sps_.t[:], lhsT=ones_b, rhs=p_.t[:], start=first, stop=lastt),
                                 reads=[p_.buf, gl], writes=[sps_.buf] if first else (),
                                 pwrites=() if first else [sps_.buf])
                        LOOK = 3
                        for ti in range(len(tiles)):
                            if len(pend) == LOOK:
                                K._waits(PE, [pend[0][2].buf], (), ())
                            pend.append((ti,) + qk(ti))
                            if len(pend) > LOOK:
                                a = pend.pop(0)
                                pv(*a)
                        for a in pend:
                            pv(*a)
                        K.op(DVE, lambda: nc.vector.reciprocal(out=rcs.t[:], in_=sps_.t[:]), reads=[sps_.buf],
                             writes=[rcs.buf])
                        K.op(DVE, lambda: nc.vector.tensor_tensor(out=rcs.t[:], in0=ops_.t[:], in1=rcs.t[:], op=ALU.mult),
                             reads=[ops_.buf], writes=[rcs.buf])
                        st = SF.next()
                        K.op(DVE, lambda: nc.vector.tensor_tensor(out=st.t[:], in0=rcs.t[:], in1=gs.t[:], op=ALU.mult),
                             reads=[rcs.buf, gs.buf], writes=[st.buf])
                        store(st, ma[h * 128:(h + 1) * 128, nsl(qc)], 128, B["ma"])
            K.barrier()

            def prefetch_x(tag, n):
                xo = XO.next()
                K.dma(SP, xo.dsem, xo.t[:], xT[tag * 128:(tag + 1) * 128, nsl(n)], reads=[B["xT"]], writes=[xo.buf])
                return xo

            def evac_resid(tag, n, ps, psb, msz, xo):
                st = SF.next()
                K.op(DVE, lambda: nc.vector.tensor_tensor(out=st.t[:], in0=ps[:], in1=xo.t[:], op=ALU.add),
                     reads=[psb, xo.buf], writes=[st.buf])
                store(st, xT[tag * 128:(tag + 1) * 128, nsl(n)], 128, B["xT"])

            with ExitStack() as L:
                mgs = Slot(sb(L, "mgs", [128, KD * T], BF16), K.buf("mgs%d" % l, dma=True))
                K.dma(SP, mgs.dsem, mgs.t[:].rearrange("p (c n) -> p c n", c=KD), mg.rearrange("(c p) n -> p c n", p=128),
                      reads=[B["mg"]], writes=[mgs.buf])
                groups = [(w_out[l, :, g0:g0 + 512], 512, [(m0, 128, (g0 + m0) // 128) for m0 in range(0, 512, 128)], None)
                          for g0 in range(0, D, 512)]
                linear(groups, KD, list(range(NQ)), lambda c, n: mgs.t[:, c * T + n * 512:c * T + (n + 1) * 512],
                       [mgs.buf] * NQ, evac_resid, prefetch_x)
            K.barrier()

            with ExitStack() as L:
                uT = sb(L, "uT", [128, KD * T], BF16)
                ub = [Buf("ub%d" % n) for n in range(NQ)]
                with ExitStack() as L2:
                    rms_to_bf16(L2, xT, B["xT"], KD, vb + V_GMLP, uT, ub, D)
                K.barrier()
                groups = [(w_up[l, :, g0:g0 + 512], 512, [(m0, 128, (g0 + m0) // 128) for m0 in range(0, 512, 128)], None)
                          for g0 in range(0, DFF, 512)]
                rl = Ring([Slot(sb(L, "rl%d" % i, [128, 512], F32), Buf("rl%d" % i)) for i in range(3)])

                def evacF1(tag, n, ps, psb, msz, pre):
                    r = rl.next()
                    K.op(ACT, lambda: nc.scalar.activation(out=r.t[:], in_=ps[:], func=AF.Relu), reads=[psb],
                         writes=[r.buf])
                    st = SBF.next()
                    K.op(DVE, lambda: nc.vector.tensor_tensor(out=st.t[:], in0=r.t[:], in1=r.t[:], op=ALU.mult),
                         reads=[r.buf], writes=[st.buf])
                    store(st, hh[tag * 128:(tag + 1) * 128, nsl(n)], 128, B["hh"])
                linear(groups, KD, list(range(NQ)), lambda c, n: uT[:, c * T + n * 512:c * T + (n + 1) * 512], ub, evacF1)
            K.barrier()

            with ExitStack() as L:
                KF = DFF // 128
                G2 = 2 if NQ >= 2 else 1
                hT = Slot(sb(L, "hT", [128, KF * 512 * G2], BF16), K.buf("hT%d" % l, dma=True))
                KH = KF // 2
                for n0 in range(0, NQ, G2):
                    K.dma(SP, hT.dsem, hT.t[:].rearrange("p (c n) -> p c n", c=KF),
                          hh[:, n0 * 512:(n0 + G2) * 512].rearrange("(c p) n -> p c n", p=128), reads=[B["hh"]],
                          writes=[hT.buf])
                    for g in range(D // 256):
                        sA, sB = WS.next(), WS.next()
                        load_w(sA, w_down[l, 0:KH * 128, g * 256:(g + 1) * 256], KH, 256)
                        load_w(sB, w_down[l, KH * 128:KF * 128, g * 256:(g + 1) * 256], KH, 256)
                        todo = [(mb, n_) for mb in range(2) for n_ in range(n0, n0 + G2)]
                        xos = [prefetch_x(g * 2 + mb, n_) for (mb, n_) in todo]
                        pss = [PS.next() for _ in todo]
                        for half, sl_ in ((0, sA), (1, sB)):
                            for (mb, n_), ps in zip(todo, pss):
                                def emit():
                                    last = None
                                    for cc_ in range(KH):
                                        c = half * KH + cc_
                                        last = nc.tensor.matmul(
                                            ps.t[:], lhsT=sl_.t[:, cc_ * 256 + mb * 128:cc_ * 256 + (mb + 1) * 128],
                                            rhs=hT.t[:, c * 512 * G2 + (n_ - n0) * 512:c * 512 * G2 + (n_ - n0 + 1) * 512],
                                            start=(c == 0), stop=(c == KF - 1))
                                    return last
                                K.op(PE, emit, reads=[sl_.buf, hT.buf], writes=[ps.buf] if half == 0 else (),
                                     pwrites=() if half == 0 else [ps.buf])
                        for (mb, n_), ps, xo in zip(todo, pss, xos):
                            evac_resid(g * 2 + mb, n_, ps.t, ps.buf, 128, xo)
            K.barrier()

            with ExitStack() as L:
                uT = sb(L, "uT", [128, KD * T], BF16)
                ub = [Buf("ub%d" % n) for n in range(NQ)]
                with ExitStack() as L2:
                    rms_to_bf16(L2, xT, B["xT"], KD, vb + V_GPLE, uT, ub, D)
                K.barrier()
                pts = Slot(sb(L, "pts", [128, 2 * T], BF16), K.buf("pts%d" % l, dma=True))
                K.dma(SP, pts.dsem, pts.t[:].rearrange("p (c n) -> p c n", c=2),
                      pT[l * DPLE:(l + 1) * DPLE, :].rearrange("(c p) n -> p c n", p=128), reads=[B["pT"]],
                      writes=[pts.buf])
                wpe = Slot(sb(L, "wpe", [128, 2 * D], BF16), K.buf("wpe%d" % l, dma=True))
                for c in range(2):
                    K.dma(POOL, wpe.dsem, wpe.t[:, c * D:(c + 1) * D], w_pe[l, c * 128:(c + 1) * 128, :],
                          writes=[wpe.buf] if c == 0 else (), pwrites=[wpe.buf] if c else (), max_dma_last_dim=4096)
                groups = [(w_pg[l, :, g0:g0 + 512], 512, [(m0, 128, (g0 + m0) // 128) for m0 in range(0, 512, 128)], None)
                          for g0 in range(0, D, 512)]
                sg = Ring([Slot(sb(L, "sg%d" % i, [128, 512], F32), Buf("sg%d" % i)) for i in range(3)])

                def evacG(tag, n, ps, psb, msz, xo):
                    ps2 = PS.next()
                    mm_group(ps2.t, 128, 2, lambda c: wpe.t[:, c * D + tag * 128:c * D + (tag + 1) * 128],
                             lambda c: pts.t[:, c * T + n * 512:c * T + (n + 1) * 512], [wpe.buf, pts.buf], ps2.buf)
                    s = sg.next()
                    K.op(ACT, lambda: nc.scalar.activation(out=s.t[:], in_=ps[:], func=AF.Sigmoid), reads=[psb],
                         writes=[s.buf])
                    K.op(DVE, lambda: nc.vector.tensor_tensor(out=s.t[:], in0=s.t[:], in1=ps2.t[:], op=ALU.mult),
                         reads=[ps2.buf], writes=[s.buf])
                    st = SF.next()
                    K.op(DVE, lambda: nc.vector.tensor_tensor(out=st.t[:], in0=s.t[:], in1=xo.t[:], op=ALU.add),
                         reads=[s.buf, xo.buf], writes=[st.buf])
                    store(st, xT[tag * 128:(tag + 1) * 128, nsl(n)], 128, B["xT"])
                linear(groups, KD, list(range(NQ)), lambda c, n: uT[:, c * T + n * 512:c * T + (n + 1) * 512], ub, evacG,
                       prefetch_x)
            K.barrier()

        with ExitStack() as L:
            xin = [Slot(sb(L, "fx%d" % i, [128, KD * 512], F32), K.buf("fx%d" % i, dma=True)) for i in range(2)]
            sq = Slot(sb(L, "fsq", [128, KD * 512], F32), Buf("fsq"))
            rs = Slot(sb(L, "frs", [128, 512], F32), Buf("frs"))
            yo = [Slot(sb(L, "fy%d" % i, [128, D], F32), K.buf("fy%d" % i, dma=True)) for i in range(2)]
            it = 0
            for n in range(NQ):
                xs = xin[n % 2]
                K.dma(SP, xs.dsem, xs.t[:].rearrange("p (c n) -> p c n", c=KD),
                      xT[:, nsl(n)].rearrange("(c p) n -> p c n", p=128), reads=[B["xT"]], writes=[xs.buf])
                K.op(ACT, lambda: nc.scalar.activation(out=sq.t[:], in_=xs.t[:], func=AF.Square), reads=[xs.buf],
                     writes=[sq.buf])
                ps = PS.next()
                mm_group(ps.t, 128, KD, lambda c: ones_f, lambda c: sq.t[:, c * 512:(c + 1) * 512], [sq.buf, gl], ps.buf)
                K.op(ACT, lambda: nc.scalar.activation(out=rs.t[:], in_=ps.t[:], func=AF.Sqrt, scale=1.0 / D,
                                                       bias=float(EPS)), reads=[ps.buf], writes=[rs.buf])
                K.op(DVE, lambda: nc.vector.reciprocal(out=rs.t[:], in_=rs.t[:]), reads=[rs.buf], writes=[rs.buf])
                for c in range(KD):
                    K.op(DVE, lambda: nc.vector.scalar_tensor_tensor(
                        out=xs.t[:, c * 512:(c + 1) * 512], in0=xs.t[:, c * 512:(c + 1) * 512],
                        scalar=vcol(V_GFINAL + c), in1=rs.t[:], op0=ALU.mult, op1=ALU.mult),
                        reads=[rs.buf, gl], writes=[xs.buf])
                for a in range(4):
                    y = yo[it % 2]
                    it += 1
                    for cb in range(4):
                        ps = PS.next()

                        def emit():
                            last = None
                            for cc_ in range(4):
                                c = cb * 4 + cc_
                                last = nc.tensor.transpose(ps.t[:, cc_ * 128:(cc_ + 1) * 128],
                                                           xs.t[:, c * 512 + a * 128:c * 512 + (a + 1) * 128], ident_f)
                            return last
                        K.op(PE, emit, reads=[xs.buf, gl], writes=[ps.buf])
                        evac_tog[0] ^= 1
                        if evac_tog[0]:
                            K.op(ACT, lambda: nc.scalar.copy(out=y.t[:, cb * 512:(cb + 1) * 512], in_=ps.t[:]),
                                 reads=[ps.buf], writes=[y.buf] if cb == 0 else (), pwrites=[y.buf] if cb else ())
                        else:
                            K.op(DVE, lambda: nc.vector.tensor_copy(out=y.t[:, cb * 512:(cb + 1) * 512], in_=ps.t[:]),
                                 reads=[ps.buf], writes=[y.buf] if cb == 0 else (), pwrites=[y.buf] if cb else ())
                    r0 = n * 512 + a * 128
                    K.dma(SP, y.dsem, y_out[r0:r0 + 128, :], y.t[:], reads=[y.buf], pwrites=[B["out"]])
            for nme, ap in dbg_outs.items():
                ds = K.sem("d_dbg_" + nme)
                src = dbg_src[nme][0].ap()
                rows = dbg_src[nme][1][0]
                step = 1024
                for r0 in range(0, rows, step):
                    r1 = min(rows, r0 + step)
                    K.dma(SP, ds, ap[r0:r1, :], src[r0:r1, :], reads=[B[nme]] if nme in B else [], pwrites=[B["out"]])
        K.final_wait(SP)
        K.final_wait(POOL)
    return nc


def _diag_masks():
    k = np.arange(128)[:, None]
    q = np.arange(512)[None, :]
    pat = np.zeros((4, 128, 512), np.float32)
    for d in range(4):
        pat[d] = np.where(128 * d + k <= q, 0.0, NEG)
    return pat


def make_inputs(T, x, p, positions, g_mix, w_in, g_q, w_qb, g_kv, w_kvb, conv_w, conv_b, w_a, b_a, w_x, b_x,
                lru_lambda, w_out, g_mlp, w_up, w_down, g_ple, w_ple_gate, w_ple_proj, g_final, n_cores=8):
    f = lambda a: np.ascontiguousarray(np.asarray(a, dtype=np.float32))
    x, p = f(x), f(p)
    positions = np.asarray(positions).astype(np.int32)

    def cols(v):
        v = f(v)
        return v.reshape(-1, 128).T

    pat = _diag_masks()
    cst = np.concatenate([np.eye(128, dtype=np.float32), np.ones((128, 128), np.float32)], axis=1)
    inv_freq = np.power(np.float32(10000.0), -np.arange(32, dtype=np.float32) * np.float32(2.0 / 64)).astype(np.float32)
    shared = dict(w_in=f(w_in), w_qb=f(w_qb), w_kvb=f(w_kvb), w_a=f(w_a), w_x=f(w_x), w_out=f(w_out), w_up=f(w_up),
                  w_down=f(w_down), w_ple_gate=f(w_ple_gate), w_ple_proj=f(w_ple_proj), cst=cst)
    in_maps = []
    for c in range(n_cores):
        b, half = c // 2, c % 2
        vecs = np.zeros((128, NV), np.float32)
        for l in range(DEPTH):
            vb = l * V_PER_LAYER
            vecs[:, vb + V_GMIX:vb + V_GMIX + 16] = cols(g_mix[l])
            vecs[:, vb + V_GMLP:vb + V_GMLP + 16] = cols(g_mlp[l])
            vecs[:, vb + V_GPLE:vb + V_GPLE + 16] = cols(g_ple[l])
            vecs[:, vb + V_GQ:vb + V_GQ + 4] = cols(g_q[l])
            vecs[:, vb + V_GKV:vb + V_GKV + 4] = cols(g_kv[l])
            for j in range(4):
                vecs[:, vb + V_CONVW + j * 16:vb + V_CONVW + (j + 1) * 16] = cols(np.asarray(conv_w)[l, j])
            vecs[:, vb + V_CONVB:vb + V_CONVB + 16] = cols(conv_b[l])
            vecs[:, vb + V_BA:vb + V_BA + 16] = cols(b_a[l])
            vecs[:, vb + V_BX:vb + V_BX + 16] = cols(b_x[l])
            vecs[:, vb + V_LAM:vb + V_LAM + 16] = cols(lru_lambda[l])
        vecs[:, V_GFINAL:V_GFINAL + 16] = cols(g_final)
        vecs[0:64, V_INVF] = np.concatenate([inv_freq, inv_freq])
        vecs[:, V_FLAG] = float(half)
        vecs[:, V_B0] = 0.0 if half else NEG
        vecs[:, V_B1] = 0.0 if half else NEG
        masks = np.zeros((2, 4, 128, 512), np.float32)
        if half == 0:
            masks[0] = pat
        else:
            masks[1] = pat
        m = dict(shared)
        m["x"] = np.ascontiguousarray(x[b, half * T:(half + 1) * T, :])
        m["p"] = np.ascontiguousarray(p[:, b, half * T:(half + 1) * T, :])
        m["pos"] = np.ascontiguousarray(positions[b, half * T:(half + 1) * T].reshape(1, T))
        m["vecs"] = vecs
        m["masks"] = masks.astype(ml_dtypes.bfloat16)
        in_maps.append(m)
    return in_maps


_PROG = {}


def run(T, inputs, dbg=(), depth=DEPTH):
    key = (T, tuple(dbg), depth)
    if key not in _PROG:
        _PROG[key] = build_program(T, dbg, depth)
    nc = _PROG[key]
    in_maps = make_inputs(T, **inputs)
    res = run_bass_kernel_spmd(nc, in_maps, core_ids=list(range(8)))
    return res.results


def kernel(**inputs):
    x = np.asarray(inputs["x"])
    Bn, S, _ = x.shape
    T = S // 2
    R = run(T, inputs)
    out = np.zeros((Bn, S, D), np.float32)
    for c in range(8):
        b, half = c // 2, c % 2
        out[b, half * T:(half + 1) * T, :] = R[c]["out"]
    return out
```
